# Optimizing a Trainium2 kernel written in Bass

```python
import math
import jax, jax.numpy as jnp
from jax import lax
import numpy as np

D_MODEL = 4096
BATCH = 8
SEQ = 2048
DEPTH = 1

FOURIER_WIDTH = D_MODEL // 2
FOURIER_GROUPS = 8
FOURIER_GROUP_DIM = FOURIER_WIDTH // FOURIER_GROUPS
DIFF_HEAD_DIM = 128
DIFF_HEADS = (D_MODEL // 2) // (2 * DIFF_HEAD_DIM)
ATTN_WIDTH = DIFF_HEADS * 2 * DIFF_HEAD_DIM
IN_COLS = FOURIER_WIDTH + 3 * ATTN_WIDTH + 2 * D_MODEL
SPLITS = (FOURIER_WIDTH,
          FOURIER_WIDTH + ATTN_WIDTH,
          FOURIER_WIDTH + 2 * ATTN_WIDTH,
          FOURIER_WIDTH + 3 * ATTN_WIDTH,
          FOURIER_WIDTH + 3 * ATTN_WIDTH + D_MODEL)
FFN_HIDDEN = -(-8 * D_MODEL // (3 * 256)) * 256
ROPE_THETA = 10000.0
RMS_EPS = 1e-6
Q_BLOCK = 128
LAMBDA_STD = 0.1

kernel_name = "hybrid_fourier_diffattn_gated_block"


def rmsnorm(x, g):
    xf = x.astype(jnp.float32)
    y = xf * lax.rsqrt(jnp.mean(xf * xf, axis=-1, keepdims=True) + RMS_EPS)
    return (y * g.astype(jnp.float32)).astype(x.dtype)


def rope_tables(positions):
    inv_freq = ROPE_THETA ** (-jnp.arange(0, DIFF_HEAD_DIM, 2, dtype=jnp.float32) / DIFF_HEAD_DIM)
    ang = positions.astype(jnp.float32)[..., None] * inv_freq
    return jnp.cos(ang)[:, :, None, None, :], jnp.sin(ang)[:, :, None, None, :]


def apply_rope(x, cos, sin):
    xf = x.astype(jnp.float32)
    x1, x2 = jnp.split(xf, 2, axis=-1)
    out = jnp.concatenate([x1 * cos - x2 * sin, x2 * cos + x1 * sin], axis=-1)
    return out.astype(x.dtype)


def fourier_mix(f):
    b, s, _ = f.shape
    fg = f.reshape(b, s, FOURIER_GROUPS, FOURIER_GROUP_DIM).astype(jnp.float32)
    y = jnp.fft.fft2(fg, axes=(1, 3), norm="ortho").real
    return y.reshape(b, s, FOURIER_WIDTH).astype(f.dtype)


def diff_attention(q, k, v, lam):
    b, s = q.shape[0], q.shape[1]
    nblk = s // Q_BLOCK
    scale = 1.0 / math.sqrt(DIFF_HEAD_DIM)
    qb = q.reshape(b, nblk, Q_BLOCK, DIFF_HEADS, 2, DIFF_HEAD_DIM).transpose(1, 0, 3, 4, 2, 5)
    kt = k.transpose(0, 2, 3, 1, 4)
    vt = v.transpose(0, 2, 1, 3)

    def attend_block(qblk):
        sc = jnp.einsum('bhcqd,bhckd->bhcqk', qblk.astype(jnp.float32), kt.astype(jnp.float32)) * scale
        p = jax.nn.softmax(sc, axis=-1)
        a = (p[:, :, 0] - lam * p[:, :, 1]).astype(vt.dtype)
        return jnp.einsum('bhqk,bhkv->bhqv', a, vt)

    o = lax.map(attend_block, qb)
    return o.transpose(1, 0, 3, 2, 4).reshape(b, s, DIFF_HEADS, 2 * DIFF_HEAD_DIM)


def setup_inputs(seed: int = 0) -> dict:
    key = jax.random.key(seed)
    ks = jax.random.split(key, 20)
    f32 = jnp.float32

    def nrm(k, shape, fan_in):
        return jax.random.normal(k, shape, f32) * (fan_in ** -0.5)

    def gain(k, shape):
        return 1.0 + 0.02 * jax.random.normal(k, shape, f32)

    x = jax.random.normal(ks[0], (BATCH, SEQ, D_MODEL), f32)
    offsets = jax.random.randint(ks[1], (BATCH, 1), 0, 64, dtype=jnp.int32)
    positions = jnp.arange(SEQ, dtype=jnp.int32)[None, :] + offsets
    return {
        "x": x,
        "positions": positions,
        "norm_mix_g": gain(ks[2], (DEPTH, D_MODEL)),
        "w_in": nrm(ks[3], (DEPTH, D_MODEL, IN_COLS), D_MODEL),
        "b_gate": 0.01 * jax.random.normal(ks[4], (DEPTH, 2, D_MODEL), f32),
        "lambda_q1": LAMBDA_STD * jax.random.normal(ks[5], (DEPTH, DIFF_HEAD_DIM), f32),
        "lambda_k1": LAMBDA_STD * jax.random.normal(ks[6], (DEPTH, DIFF_HEAD_DIM), f32),
        "lambda_q2": LAMBDA_STD * jax.random.normal(ks[7], (DEPTH, DIFF_HEAD_DIM), f32),
        "lambda_k2": LAMBDA_STD * jax.random.normal(ks[8], (DEPTH, DIFF_HEAD_DIM), f32),
        "subln_g": gain(ks[9], (DEPTH, 2 * DIFF_HEAD_DIM)),
        "w_fourier_out": nrm(ks[10], (DEPTH, FOURIER_WIDTH, D_MODEL), FOURIER_WIDTH),
        "w_attn_out": nrm(ks[11], (DEPTH, ATTN_WIDTH, D_MODEL), ATTN_WIDTH),
        "w_out": nrm(ks[12], (DEPTH, D_MODEL, D_MODEL), D_MODEL),
        "norm_ffn_g": gain(ks[13], (DEPTH, D_MODEL)),
        "w_ffn_gate": nrm(ks[14], (DEPTH, D_MODEL, FFN_HIDDEN), D_MODEL),
        "w_ffn_up": nrm(ks[15], (DEPTH, D_MODEL, FFN_HIDDEN), D_MODEL),
        "w_ffn_down": nrm(ks[16], (DEPTH, FFN_HIDDEN, D_MODEL), FFN_HIDDEN),
        "norm_final_g": gain(ks[17], (D_MODEL,)),
    }


def reference(x, positions, norm_mix_g, w_in, b_gate, lambda_q1, lambda_k1, lambda_q2, lambda_k2,
              subln_g, w_fourier_out, w_attn_out, w_out, norm_ffn_g, w_ffn_gate, w_ffn_up,
              w_ffn_down, norm_final_g):
    b, s, _ = x.shape
    cos, sin = rope_tables(positions)
    h = x
    for l in range(DEPTH):
        u = rmsnorm(h, norm_mix_g[l])
        z = jnp.einsum('bsd,dc->bsc', u, w_in[l])
        f, q, k, v, gf, ga = jnp.split(z, SPLITS, axis=-1)
        gate_f = jax.nn.sigmoid(gf + b_gate[l, 0])
        gate_a = jax.nn.sigmoid(ga + b_gate[l, 1])

        y_f = jnp.einsum('bsf,fd->bsd', fourier_mix(f), w_fourier_out[l])

        q = apply_rope(q.reshape(b, s, DIFF_HEADS, 2, DIFF_HEAD_DIM), cos, sin)
        k = apply_rope(k.reshape(b, s, DIFF_HEADS, 2, DIFF_HEAD_DIM), cos, sin)
        v = v.reshape(b, s, DIFF_HEADS, 2 * DIFF_HEAD_DIM)
        lam_init = 0.8 - 0.6 * math.exp(-0.3 * l)
        lam = (jnp.exp(jnp.sum(lambda_q1[l].astype(jnp.float32) * lambda_k1[l].astype(jnp.float32)))
               - jnp.exp(jnp.sum(lambda_q2[l].astype(jnp.float32) * lambda_k2[l].astype(jnp.float32)))
               + lam_init)
        o = diff_attention(q, k, v, lam)
        o = rmsnorm(o, subln_g[l]) * (1.0 - lam_init)
        y_a = jnp.einsum('bsa,ad->bsd', o.reshape(b, s, ATTN_WIDTH), w_attn_out[l])

        merged = gate_f * y_f + gate_a * y_a
        h = h + jnp.einsum('bsd,de->bse', merged, w_out[l])

        u2 = rmsnorm(h, norm_ffn_g[l])
        hid = jax.nn.silu(jnp.einsum('bsd,df->bsf', u2, w_ffn_gate[l])) * jnp.einsum('bsd,df->bsf', u2, w_ffn_up[l])
        h = h + jnp.einsum('bsf,fd->bsd', hid, w_ffn_down[l])
    return rmsnorm(h, norm_final_g)
```

```python
import ml_dtypes
import time
import numpy as np
import concourse.bass as bass
import concourse.mybir as mybir
from concourse.bass_utils import run_bass_kernel_spmd

F32 = mybir.dt.float32
BF16 = mybir.dt.bfloat16
I32 = mybir.dt.int32
AF = mybir.ActivationFunctionType
ALU = mybir.AluOpType
AX = mybir.AxisListType

COMPUTE = ("pe", "act", "dve", "pool")


class Buf:
    __slots__ = ("name", "lw", "rd", "sem")

    def __init__(self, name):
        self.name = name
        self.lw = None
        self.rd = {}
        self.sem = None


class Ins:
    __slots__ = ("stream", "src", "fn", "deps", "sig", "sigval", "dma")

    def __init__(self, stream, src, fn, dma):
        self.stream = stream
        self.src = src
        self.fn = fn
        self.deps = []
        self.sig = False
        self.sigval = 0
        self.dma = dma


class Sched:
    def __init__(self, nc):
        self.nc = nc
        self.streams = {"pe": [], "act": [], "dve": [], "pool": [], "sp": []}
        self.all = []
        self.sems = {e: nc.alloc_semaphore("sem_" + e) for e in COMPUTE}
        self.dma_sem_pool = []
        self.ndma = 0
        self.last = {}
        self.bufs_with_sem = []

    def buf(self, name):
        return Buf(name)

    def _dma_sem(self, b):
        if b.sem is None:
            if self.dma_sem_pool:
                b.sem = self.dma_sem_pool.pop()
            else:
                self.ndma += 1
                b.sem = self.nc.alloc_semaphore("dsem%d" % self.ndma)
            self.bufs_with_sem.append(b)
        return b.sem

    def _track(self, ins, reads, writes):
        deps = []
        for b in reads:
            if b.lw is not None:
                deps.append((b.lw, "raw"))
        for b in writes:
            if b.lw is not None:
                deps.append((b.lw, "waw"))
            for r in b.rd.values():
                deps.append((r, "war"))
        for d, kind in deps:
            if d is ins:
                continue
            if (not ins.dma) and (not d.dma) and d.src == ins.src:
                if ins.src == "pe" or kind != "raw":
                    continue
            d.sig = True
            ins.deps.append(d)
        for b in reads:
            b.rd[ins.src] = ins
        for b in writes:
            b.lw = ins
            b.rd = {}

    def op(self, eng, fn, reads=(), writes=()):
        ins = Ins(eng, eng, fn, False)
        self._track(ins, reads, writes)
        self.streams[eng].append(ins)
        self.all.append(ins)
        self.last[eng] = ins
        return ins

    def dma(self, fn, chan, reads=(), writes=(), q="sp"):
        sem = self._dma_sem(chan)
        ins = Ins(q, sem, fn, True)
        ins.sig = True
        self._track(ins, reads, writes)
        self.streams[q].append(ins)
        self.all.append(ins)
        self.last[sem] = ins
        return ins

    def barrier(self):
        lasts = list(self.last.values())
        for d in lasts:
            d.sig = True
        for s in self.streams:
            ins = Ins(s, None, None, False)
            ins.deps = list(lasts)
            self.streams[s].append(ins)
            self.all.append(ins)
        for b in self.bufs_with_sem:
            self.dma_sem_pool.append(b.sem)
            b.sem = None
        self.bufs_with_sem = []

    def finalize(self):
        nc = self.nc
        cnt = {}
        for ins in self.all:
            if ins.src is None:
                continue
            if ins.sig:
                cnt[ins.src] = cnt.get(ins.src, 0) + (16 if ins.dma else 1)
                ins.sigval = cnt[ins.src]
        sems = self.sems
        streams = self.streams

        def replay(sname):
            def run(e):
                waited = {}
                for ins in streams[sname]:
                    need = {}
                    for d in ins.deps:
                        k = d.src
                        if d.sigval > need.get(k, 0):
                            need[k] = d.sigval
                    for k, v in need.items():
                        if v > waited.get(k, 0):
                            waited[k] = v
                            e.wait_ge(sems[k] if isinstance(k, str) else k, v)
                    if ins.fn is None:
                        continue
                    bi = ins.fn(e)
                    if ins.sig:
                        if ins.dma:
                            bi.then_inc(ins.src, 16)
                        else:
                            bi.then_inc(sems[ins.src], 1)
            return run

        with nc.Block() as block:
            block.tensor(replay("pe"))
            block.scalar(replay("act"))
            block.vector(replay("dve"))
            block.gpsimd(replay("pool"))
            block.sync(replay("sp"))


class SbufAlloc:
    def __init__(self, nc, base=None, top=None):
        self.nc = nc
        self.base = ((nc.sbuf_base + 63) // 64) * 64 if base is None else base
        self.top = nc.sbuf_top if top is None else top
        self.cur = self.base
        self.n = 0

    def reset(self):
        self.cur = self.base

    def alloc(self, name, shape, dtype):
        esz = 4 if dtype in (F32, I32) else 2
        per = esz
        for s in shape[1:]:
            per *= s
        off = self.cur
        self.cur = ((off + per + 63) // 64) * 64
        assert self.cur <= self.top, ("SBUF overflow", name, self.cur, self.top)
        self.n += 1
        return self.nc.alloc_sbuf_tensor_at("%s_%d" % (name, self.n), list(shape), dtype, offset=off)


D = 4096
T = 2048
NT = T // 128
FW = 2048
AW = 2048
HID = 11008
HC = HID // 128
NH = 8
EPS = 1e-6
LAM_INIT = 0.2


def _consts():
    s = np.arange(2048, dtype=np.float64)
    sp = np.arange(1024, dtype=np.float64)
    ang = 2.0 * np.pi * np.outer(s, sp) / 2048.0
    dftw = np.concatenate([np.cos(ang), np.sin(ang)], 0) / np.sqrt(2048.0)
    alt = np.tile(((-1.0) ** np.arange(128)).reshape(128, 1), (1, 16)) / np.sqrt(2048.0)
    c = np.arange(256, dtype=np.float64)
    angc = 2.0 * np.pi * np.outer(c, c) / 256.0
    ccsc = np.concatenate([np.cos(angc), np.sin(angc)], 1) / 16.0
    ident = np.eye(128)
    j = np.arange(0, 128, 2, dtype=np.float32) / np.float32(128)
    inv = (np.float32(10000.0) ** (-j)).astype(np.float32)
    invf = np.concatenate([inv, inv]).reshape(128, 1).astype(np.float32)
    return (dftw.astype(ml_dtypes.bfloat16), ccsc.astype(ml_dtypes.bfloat16),
            ident.astype(ml_dtypes.bfloat16), invf, alt.astype(ml_dtypes.bfloat16))


class K:
    pass


def build(upto=99, dbg=False):
    nc = bass.Bass("TRN2", target_bir_lowering=False)
    S = Sched(nc)

    def din(name, shape, dt):
        return nc.dram_tensor(name, list(shape), dt, kind="ExternalInput").ap()

    def dscr(name, shape, dt):
        return nc.dram_tensor(name, list(shape), dt, kind=("ExternalOutput" if dbg else "Internal")).ap()

    x_d = din("x", [T, D], F32)
    pos_d = din("positions", [T], I32)
    g1_d = din("norm_mix_g", [D], F32)
    win_d = din("w_in", [D, 16384], F32)
    bg_d = din("bgate", [128, 64], F32)
    lq1_d = din("lambda_q1", [128], F32)
    lk1_d = din("lambda_k1", [128], F32)
    lq2_d = din("lambda_q2", [128], F32)
    lk2_d = din("lambda_k2", [128], F32)
    sg_d = din("subln_g", [256], F32)
    wf_d = din("w_fourier_out", [FW, D], F32)
    wa_d = din("w_attn_out", [AW, D], F32)
    wo_d = din("w_out", [D, D], F32)
    g2_d = din("norm_ffn_g", [D], F32)
    wg_d = din("w_ffn_gate", [D, HID], F32)
    wu_d = din("w_ffn_up", [D, HID], F32)
    wd_d = din("w_ffn_down", [HID, D], F32)
    g3_d = din("norm_final_g", [D], F32)
    dftw_d = din("dftw", [4096, 1024], BF16)
    alt_d = din("alt", [128, 16], BF16)
    ccsc_d = din("ccsc", [256, 512], BF16)
    ident_d = din("ident", [128, 128], BF16)
    invf_d = din("invf", [128, 1], F32)
    out_d = nc.dram_tensor("out", [T, D], F32, kind="ExternalOutput").ap()

    fT_d = dscr("fT", [FW, T], BF16)
    qT_d = dscr("qT", [AW, T], BF16)
    kT_d = dscr("kT", [AW, T], BF16)
    vT_d = dscr("vT", [AW, T], BF16)
    gfT_d = dscr("gfT", [D, T], BF16)
    gaT_d = dscr("gaT", [D, T], BF16)
    gw_d = dscr("gw", [4096, 2048], BF16)
    yT_d = dscr("yT", [FW, T], BF16)
    oT_d = dscr("oT", [AW, T], BF16)
    tmpT_d = dscr("tmpT", [D, T], F32)
    mT_d = dscr("mT", [D, T], BF16)
    h1_d = dscr("h1", [T, D], F32)
    hidT_d = dscr("hidT", [HID, T], BF16)
    h2_d = dscr("h2", [T, D], F32)

    A = SbufAlloc(nc)
    ps = [nc.alloc_psum_tensor("psA", [128, 2048], F32), nc.alloc_psum_tensor("psB", [128, 2048], F32)]
    psb = [p.bitcast(BF16) for p in ps]
    b_ps = [S.buf("psA"), S.buf("psB")]
    b_bank = [S.buf("bank%d" % i) for i in range(8)]

    def bank(i):
        return ps[i // 4][:, (i % 4) * 512:(i % 4 + 1) * 512]

    def bank_bf(i):
        return psb[i // 4][:, (i % 4) * 1024:(i % 4 + 1) * 1024]

    ident = A.alloc("ident", [128, 128], BF16)
    b_ident = S.buf("ident")
    S.dma(lambda e: e.dma_start(out=ident[:], in_=ident_d), chan=b_ident, writes=[b_ident])
    small = A.alloc("small", [128, 64], F32)
    b_small = S.buf("small")
    bgate = A.alloc("bgate", [128, 64], F32)
    b_bgate = S.buf("bgate")
    S.dma(lambda e: e.dma_start(out=bgate[:], in_=bg_d), chan=b_bgate, writes=[b_bgate])
    g08 = A.alloc("g08", [128, 256], F32)
    b_g08 = S.buf("g08")
    S.dma(lambda e: e.dma_start(out=g08[:], in_=sg_d.partition_broadcast(128)), chan=b_g08, writes=[b_g08])
    S.op("dve", lambda e: e.tensor_scalar(out=g08[:], in0=g08[:], scalar1=1.0 - LAM_INIT, scalar2=None, op0=ALU.mult),
         reads=[b_g08], writes=[b_g08])
    lam4 = A.alloc("lam4", [128, 4, 128], F32)
    b_lam4 = S.buf("lam4")
    for i, dd in enumerate([lq1_d, lk1_d, lq2_d, lk2_d]):
        S.dma(lambda e, i=i, dd=dd: e.dma_start(out=lam4[:, i, :], in_=dd.partition_broadcast(128)),
              chan=b_lam4, writes=[b_lam4])
    lamp = A.alloc("lamp", [128, 2, 128], F32)
    b_lamp = S.buf("lamp")
    S.op("dve", lambda e: e.tensor_tensor(out=lamp[:, 0, :], in0=lam4[:, 0, :], in1=lam4[:, 1, :], op=ALU.mult),
         reads=[b_lam4], writes=[b_lamp])
    S.op("dve", lambda e: e.tensor_tensor(out=lamp[:, 1, :], in0=lam4[:, 2, :], in1=lam4[:, 3, :], op=ALU.mult),
         reads=[b_lam4, b_lamp], writes=[b_lamp])
    S.op("dve", lambda e: e.tensor_reduce(out=small[:, 0:2], in_=lamp[:], axis=AX.X, op=ALU.add),
         reads=[b_lamp], writes=[b_small])
    S.op("act", lambda e: e.activation(out=small[:, 2:4], in_=small[:, 0:2], func=AF.Exp),
         reads=[b_small], writes=[b_small])
    S.op("dve", lambda e: e.tensor_tensor(out=small[:, 4:5], in0=small[:, 3:4], in1=small[:, 2:3], op=ALU.subtract),
         reads=[b_small], writes=[b_small])
    S.op("dve", lambda e: e.tensor_scalar(out=small[:, 5:6], in0=small[:, 4:5], scalar1=-LAM_INIT, scalar2=None, op0=ALU.add),
         reads=[b_small], writes=[b_small])
    NEGLAM = small[:, 5:6]
    A.base = A.cur

    K.nc, K.S, K.A = nc, S, A

    def norm_to_actT(src_d, g_d, actT, b_act):
        gbc = A.alloc("gbc", [128, D], F32)
        b_gbc = S.buf("gbc")
        S.dma(lambda e: e.dma_start(out=gbc[:], in_=g_d.partition_broadcast(128)), chan=b_gbc, writes=[b_gbc])
        xt = [A.alloc("xt", [128, D], F32) for _ in range(2)]
        b_xt = [S.buf("xt%d" % i) for i in range(2)]
        junk = A.alloc("junk", [128, D], BF16)
        b_junk = S.buf("junk")
        ub = [A.alloc("ub", [128, D], BF16) for _ in range(2)]
        b_ub = [S.buf("ub%d" % i) for i in range(2)]
        st = A.alloc("st", [128, NT, 8], F32)
        b_stl = [S.buf("st%d" % i) for i in range(NT)]
        S.op("pool", lambda e: e.memset(st[:], 0.0), writes=b_stl)
        for tt in range(NT):
            sl = tt % 2
            b_st = b_stl[tt]
            S.dma(lambda e, sl=sl, tt=tt: e.dma_start(out=xt[sl][:], in_=src_d[tt * 128:(tt + 1) * 128, :]),
                  chan=b_xt[sl], writes=[b_xt[sl]])
            S.op("act", lambda e, sl=sl, tt=tt: e.activation(out=junk[:], in_=xt[sl][:], func=AF.Square,
                                                             accum_out=st[:, tt, 0:1]),
                 reads=[b_xt[sl], b_st], writes=[b_junk, b_st])
            S.op("pool", lambda e, tt=tt: e.tensor_scalar(out=st[:, tt, 1:2], in0=st[:, tt, 0:1], scalar1=1.0 / D,
                                                          scalar2=EPS, op0=ALU.mult, op1=ALU.add),
                 reads=[b_st], writes=[b_st])
            S.op("act", lambda e, tt=tt: e.activation(out=st[:, tt, 1:2], in_=st[:, tt, 1:2], func=AF.Ln),
                 reads=[b_st], writes=[b_st])
            S.op("act", lambda e, tt=tt: e.activation(out=st[:, tt, 1:2], in_=st[:, tt, 1:2], func=AF.Exp, scale=-0.5),
                 reads=[b_st], writes=[b_st])
            S.op("dve", lambda e, sl=sl, tt=tt: e.scalar_tensor_tensor(out=ub[sl][:], in0=xt[sl][:], scalar=st[:, tt, 1:2],
                                                                       in1=gbc[:], op0=ALU.mult, op1=ALU.mult),
                 reads=[b_xt[sl], b_st, b_gbc], writes=[b_ub[sl]])
            for c8 in range(4):
                bi = (tt * 4 + c8) % 8
                for j in range(8):
                    kc = c8 * 8 + j
                    S.op("pe", lambda e, bi=bi, j=j, kc=kc, sl=sl: e.transpose(
                        bank_bf(bi)[:, j * 128:(j + 1) * 128], ub[sl][:, kc * 128:(kc + 1) * 128], ident[:]),
                        reads=[b_ub[sl], b_ident], writes=[b_bank[bi]])
                eng = "dve"
                if eng == "act":
                    S.op("act", lambda e, bi=bi, c8=c8, tt=tt: e.activation(
                        out=actT[:, c8 * 8:(c8 + 1) * 8, tt * 128:(tt + 1) * 128],
                        in_=bank_bf(bi).rearrange("p (k t) -> p k t", k=8), func=AF.Copy),
                        reads=[b_bank[bi]], writes=[b_act])
                else:
                    S.op("dve", lambda e, bi=bi, c8=c8, tt=tt: e.tensor_copy(
                        out=actT[:, c8 * 8:(c8 + 1) * 8, tt * 128:(tt + 1) * 128],
                        in_=bank_bf(bi).rearrange("p (k t) -> p k t", k=8)),
                        reads=[b_bank[bi]], writes=[b_act])

    def load_actT(actT, b_act, src_d, kc0, KC):
        v = src_d.rearrange("(k p) t -> p k t", p=128)
        for tb in range(4):
            bb = b_act[tb] if isinstance(b_act, list) else b_act
            S.dma(lambda e, tb=tb: e.dma_start(out=actT[:, 0:KC, tb * 512:(tb + 1) * 512],
                                               in_=v[:, kc0:kc0 + KC, tb * 512:(tb + 1) * 512]),
                  chan=bb, writes=[bb])

    def gemm_fm(actT, b_act, KC, tiles, epi, kc0=0, PK=16, NW=3, NSTG=2, DIST=2, act_loader=None):
        stg = [A.alloc("stg", [128, PK, 128], F32) for _ in range(NSTG)]
        b_stg = [S.buf("stg%d" % i) for i in range(NSTG)]
        wb = [A.alloc("wb", [128, KC, 128], BF16) for _ in range(NW)]
        b_wb = [S.buf("wb%d" % i) for i in range(NW)]
        state = {"pc": 0}

        def load(i):
            w_ap, c0, isbf, _ = tiles[i]
            ws = i % NW
            wv = w_ap.rearrange("(k p) n -> p k n", p=128)
            if isbf:
                S.dma(lambda e: e.dma_start(out=wb[ws][:], in_=wv[:, kc0:kc0 + KC, c0:c0 + 128]),
                      chan=b_wb[ws], writes=[b_wb[ws]])
                return
            for p0 in range(0, KC, PK):
                n = min(PK, KC - p0)
                sl = state["pc"] % NSTG
                state["pc"] += 1
                S.dma(lambda e, sl=sl, p0=p0, n=n: e.dma_start(
                    out=stg[sl][:, 0:n, :], in_=wv[:, kc0 + p0:kc0 + p0 + n, c0:c0 + 128]),
                    chan=b_stg[sl], writes=[b_stg[sl]])
                S.op("pool", lambda e, sl=sl, p0=p0, n=n: e.tensor_copy(out=wb[ws][:, p0:p0 + n, :], in_=stg[sl][:, 0:n, :]),
                     reads=[b_stg[sl]], writes=[b_wb[ws]])

        n = len(tiles)
        for i in range(min(DIST, n)):
            load(i)
        if act_loader is not None:
            act_loader()
        for i in range(n):
            if i + DIST < n:
                load(i + DIST)
            pp = i % 2
            ws = i % NW
            for tb in range(4):
                for kc in range(KC):
                    S.op("pe", lambda e, pp=pp, tb=tb, ws=ws, kc=kc: e.matmul(
                        ps[pp][:, tb * 512:(tb + 1) * 512], lhsT=wb[ws][:, kc, :],
                        rhs=actT[:, kc, tb * 512:(tb + 1) * 512], start=(kc == 0), stop=(kc == KC - 1)),
                        reads=[b_wb[ws], (b_act[tb] if isinstance(b_act, list) else b_act)], writes=[b_ps[pp]])
            epi(i, tiles[i][3], ps[pp], b_ps[pp])

    def gemm_tm(src_d, KCs, w_ap, resid0_d, out_d_, NW=2, NSTG=3, PK=4):
        KCm = max(KCs)
        NSL = 8
        aT = A.alloc("aT", [128, KCm, T], BF16)
        b_aT = [S.buf("aT%d" % i) for i in range(NSL)]
        stg = [A.alloc("stg", [128, PK, 512], F32) for _ in range(NSTG)]
        b_stg = [S.buf("stg%d" % i) for i in range(NSTG)]
        wb = [A.alloc("wb", [128, KCm, 512], BF16) for _ in range(NW)]
        b_wb = [S.buf("wb%d" % i) for i in range(NW)]
        NR = 8
        rt = [A.alloc("rt", [128, 512], F32) for _ in range(NR)]
        b_rt = [S.buf("rt%d" % i) for i in range(NR)]
        dt = {}
        for tt in range(NT):
            for cb in range(8):
                dt[(tt, cb)] = S.buf("dt")
        wv = w_ap.rearrange("(k p) n -> p k n", p=128)
        sv = src_d.rearrange("(k p) t -> p k t", p=128)
        state = {"pc": 0}
        kc0s = [sum(KCs[:i]) for i in range(len(KCs))]

        def load_slice(pi_, sl_):
            KC, kc0 = KCs[pi_], kc0s[pi_]
            S.dma(lambda e: e.dma_start(out=aT[:, 0:KC, sl_ * 256:(sl_ + 1) * 256],
                                        in_=sv[:, kc0:kc0 + KC, sl_ * 256:(sl_ + 1) * 256]),
                  chan=b_aT[sl_], writes=[b_aT[sl_]])

        def pieces_of(pi_):
            KC = KCs[pi_]
            return [(p0, min(PK, KC - p0)) for p0 in range(0, KC, PK)]

        def load_piece(gb, k):
            pi_, cb = gb // 8, gb % 8
            ws = gb % NW
            p0, n = pieces_of(pi_)[k]
            kc0 = kc0s[pi_]
            sl = state["pc"] % NSTG
            state["pc"] += 1
            S.dma(lambda e: e.dma_start(
                out=stg[sl][:, 0:n, :], in_=wv[:, kc0 + p0:kc0 + p0 + n, cb * 512:(cb + 1) * 512]),
                chan=b_stg[sl], writes=[b_stg[sl]])
            S.op("pool", lambda e: e.tensor_copy(out=wb[ws][:, p0:p0 + n, :], in_=stg[sl][:, 0:n, :]),
                 reads=[b_stg[sl]], writes=[b_wb[ws]])

        NP = len(KCs)
        for k in range(len(pieces_of(0))):
            load_piece(0, k)
        for sl_ in range(NSL):
            load_slice(0, sl_)
        LA = 3
        NIT = NP * 8 * NT

        def load_resid(j):
            gb_, tt_ = j // NT, j % NT
            pj, cbj = gb_ // 8, gb_ % 8
            rsj = j % NR
            src = resid0_d if pj == 0 else out_d_
            rd = [dt[(tt_, cbj)]] if pj > 0 else []
            S.dma(lambda e: e.dma_start(
                out=rt[rsj][:], in_=src[tt_ * 128:(tt_ + 1) * 128, cbj * 512:(cbj + 1) * 512]),
                chan=b_rt[rsj], reads=rd, writes=[b_rt[rsj]])

        for j in range(LA):
            load_resid(j)
        it = 0
        for gb in range(NP * 8):
            pi_, cb = gb // 8, gb % 8
            KC = KCs[pi_]
            ws = gb % NW
            for tt in range(NT):
                if gb + 1 < NP * 8 and tt % 2 == 0 and tt // 2 < len(pieces_of((gb + 1) // 8)):
                    load_piece(gb + 1, tt // 2)
                if it + LA < NIT:
                    load_resid(it + LA)
                bi = it % 8
                rs = it % NR
                it += 1
                for kc in range(KC):
                    S.op("pe", lambda e, bi=bi, ws=ws, kc=kc, tt=tt, KC=KC: e.matmul(
                        bank(bi), lhsT=aT[:, kc, tt * 128:(tt + 1) * 128], rhs=wb[ws][:, kc, :],
                        start=(kc == 0), stop=(kc == KC - 1)),
                        reads=[b_wb[ws], b_aT[tt // 2]], writes=[b_bank[bi]])
                S.op("dve", lambda e, bi=bi, rs=rs: e.tensor_tensor(out=rt[rs][:], in0=bank(bi), in1=rt[rs][:], op=ALU.add),
                     reads=[b_bank[bi], b_rt[rs]], writes=[b_rt[rs]])
                S.dma(lambda e, rs=rs, tt=tt, cb=cb: e.dma_start(
                    out=out_d_[tt * 128:(tt + 1) * 128, cb * 512:(cb + 1) * 512], in_=rt[rs][:]),
                    chan=b_rt[rs], reads=[b_rt[rs]], writes=[dt[(tt, cb)]])
                if cb == 7 and pi_ + 1 < NP and tt % 2 == 1:
                    load_slice(pi_ + 1, tt // 2)

    K.norm_to_actT, K.load_actT, K.gemm_fm, K.gemm_tm = norm_to_actT, load_actT, gemm_fm, gemm_tm
    K.bank, K.bank_bf, K.b_bank, K.ps, K.psb, K.b_ps = bank, bank_bf, b_bank, ps, psb, b_ps

    mark = A.cur
    actT = A.alloc("actT", [128, 32, T], BF16)
    b_act = S.buf("actT")
    mark1 = A.cur
    norm_to_actT(x_d, g1_d, actT, b_act)
    S.barrier()
    if upto <= 0:
        return finish(nc, S)

    A.cur = mark1
    cosT = A.alloc("cosT", [128, T], F32)
    sinT = A.alloc("sinT", [128, T], F32)
    b_rope = S.buf("rope")
    _pm = A.cur
    posi = A.alloc("posi", [128, T], I32)
    A.cur = _pm
    t1 = A.alloc("t1", [128, 1024], F32)
    t2 = A.alloc("t2", [128, 1024], F32)
    b_posi = S.buf("posi")
    invf = A.alloc("invf", [128, 1], F32)
    b_invf = S.buf("invf")
    S.dma(lambda e: e.dma_start(out=invf[:], in_=invf_d), chan=b_invf, writes=[b_invf])
    S.dma(lambda e: e.dma_start(out=posi[:], in_=pos_d.partition_broadcast(128)), chan=b_posi, writes=[b_posi])
    S.op("dve", lambda e: e.tensor_copy(out=cosT[:], in_=posi[:]), reads=[b_posi], writes=[b_rope])
    S.op("dve", lambda e: e.tensor_scalar(out=cosT[:], in0=cosT[:], scalar1=invf[:, 0:1], scalar2=None, op0=ALU.mult),
         reads=[b_rope, b_invf], writes=[b_rope])
    TWO_PI = 2.0 * np.pi
    PI_LO = 3.1415925
    _om = A.cur
    tmpk = A.alloc("tmpk", [128, T], F32)
    A.cur = _om
    ot = [A.alloc("ot", [128, T], BF16) for _ in range(2)]
    b_ot = [S.buf("ot%d" % i) for i in range(2)]

    def reduce_sin(dst):
        S.op("dve", lambda e: e.tensor_scalar(out=posi[:], in0=cosT[:], scalar1=float(1.0 / TWO_PI), scalar2=None, op0=ALU.mult),
             reads=[b_rope], writes=[b_posi])
        S.op("dve", lambda e: e.tensor_copy(out=tmpk[:], in_=posi[:]), reads=[b_posi], writes=b_ot)
        S.op("dve", lambda e: e.scalar_tensor_tensor(out=dst[:], in0=tmpk[:], scalar=-float(TWO_PI), in1=cosT[:],
                                                     op0=ALU.mult, op1=ALU.add), reads=b_ot + [b_rope], writes=[b_rope])
        S.op("dve", lambda e: e.tensor_scalar(out=dst[:], in0=dst[:], scalar1=PI_LO, scalar2=-PI_LO, op0=ALU.min, op1=ALU.max),
             reads=[b_rope], writes=[b_rope])
        S.op("act", lambda e: e.activation(out=dst[:], in_=dst[:], func=AF.Sin), reads=[b_rope], writes=[b_rope])

    reduce_sin(sinT)
    S.op("dve", lambda e: e.tensor_scalar(out=cosT[:], in0=cosT[:], scalar1=float(0.5 * np.pi), scalar2=None, op0=ALU.add),
         reads=[b_rope], writes=[b_rope])
    reduce_sin(cosT)

    qsw = A.alloc("qsw", [128, 1024], F32)
    b_qsw, b_t1, b_t2 = S.buf("qsw"), S.buf("t1"), S.buf("t2")

    def epi1(i, info, pst, b_pst):
        kind, dst, row0, bcol = info
        os_ = i % 2
        if kind == "copy":
            if i % 2 == 0:
                S.op("act", lambda e: e.activation(out=ot[os_][:], in_=pst[:], func=AF.Copy),
                     reads=[b_pst], writes=[b_ot[os_]])
            else:
                S.op("dve", lambda e: e.tensor_copy(out=ot[os_][:], in_=pst[:]), reads=[b_pst], writes=[b_ot[os_]])
        elif kind == "sig":
            S.op("act", lambda e: e.activation(out=ot[os_][:], in_=pst[:], func=AF.Sigmoid, bias=bgate[:, bcol:bcol + 1]),
                 reads=[b_pst, b_bgate], writes=[b_ot[os_]])
        else:
            for hf in range(2):
                sl = slice(hf * 1024, (hf + 1) * 1024)
                S.op("act", lambda e, sl=sl: e.activation(out=qsw[0:64, :], in_=pst[64:128, sl], func=AF.Copy, scale=-1.0),
                     reads=[b_pst], writes=[b_qsw])
                S.op("act", lambda e, sl=sl: e.activation(out=qsw[64:128, :], in_=pst[0:64, sl], func=AF.Copy),
                     reads=[b_pst], writes=[b_qsw])
                S.op("act", lambda e, sl=sl: e.activation(out=t1[:], in_=pst[:, sl], func=AF.Copy),
                     reads=[b_pst], writes=[b_t1])
                S.op("dve", lambda e, sl=sl: e.tensor_tensor(out=t1[:], in0=t1[:], in1=cosT[:, sl], op=ALU.mult),
                     reads=[b_t1, b_rope], writes=[b_t1])
                S.op("pool", lambda e, sl=sl: e.tensor_tensor(out=t2[:], in0=qsw[:], in1=sinT[:, sl], op=ALU.mult),
                     reads=[b_qsw, b_rope], writes=[b_t2])
                S.op("dve", lambda e, sl=sl: e.tensor_tensor(out=ot[os_][:, sl], in0=t1[:], in1=t2[:], op=ALU.add),
                     reads=[b_t1, b_t2], writes=[b_ot[os_]])
        S.dma(lambda e: e.dma_start(out=dst[row0:row0 + 128, :], in_=ot[os_][:]), chan=b_ot[os_], reads=[b_ot[os_]])

    tiles = []
    for ct in range(128):
        c0 = ct * 128
        if ct < 16:
            info = ("copy", fT_d, ct * 128, 0)
        elif ct < 32:
            info = ("rope", qT_d, (ct - 16) * 128, 0)
        elif ct < 48:
            info = ("rope", kT_d, (ct - 32) * 128, 0)
        elif ct < 64:
            info = ("copy", vT_d, (ct - 48) * 128, 0)
        elif ct < 96:
            info = ("sig", gfT_d, (ct - 64) * 128, ct - 64)
        else:
            info = ("sig", gaT_d, (ct - 96) * 128, 32 + ct - 96)
        tiles.append((win_d, c0, False, info))
    if dbg and upto == 1.5:
        tiles = tiles[0:2] + tiles[16:18] + tiles[32:34] + tiles[48:50] + tiles[64:66] + tiles[96:98]
    gemm_fm(actT, b_act, 32, tiles, epi1, PK=8, NSTG=3)
    S.barrier()
    if upto <= 2:
        return finish(nc, S)
    A.cur = mark
    fTs = A.alloc("fTs", [128, 16, T], BF16)
    b_fTs = [S.buf("fTs%d" % i) for i in range(4)]
    load_actT(fTs, b_fTs, fT_d, 0, 16)
    ccsc = A.alloc("ccsc", [128, 2, 512], BF16)
    b_ccsc = S.buf("ccsc")
    S.dma(lambda e: e.dma_start(out=ccsc[:], in_=ccsc_d.rearrange("(j p) n -> p j n", p=128)), chan=b_ccsc, writes=[b_ccsc])
    gsb = [A.alloc("gsb", [128, 2, T], BF16) for _ in range(2)]
    b_gsb = [S.buf("gsb%d" % i) for i in range(2)]
    it = 0
    for tt in range(NT):
        gs = tt % 2
        for g in range(8):
            bi = it % 8
            it += 1
            for j in range(2):
                S.op("pe", lambda e, bi=bi, g=g, j=j, tt=tt: e.matmul(
                    bank(bi), lhsT=fTs[:, 2 * g + j, tt * 128:(tt + 1) * 128], rhs=ccsc[:, j, :],
                    start=(j == 0), stop=(j == 1)), reads=[b_fTs[tt // 4], b_ccsc], writes=[b_bank[bi]])
            if it % 2 == 0:
                S.op("act", lambda e, bi=bi, g=g, gs=gs: e.activation(
                    out=gsb[gs][:, :, g * 256:(g + 1) * 256], in_=bank(bi).rearrange("p (c n) -> p c n", c=2), func=AF.Copy),
                    reads=[b_bank[bi]], writes=[b_gsb[gs]])
            else:
                S.op("dve", lambda e, bi=bi, g=g, gs=gs: e.tensor_copy(
                    out=gsb[gs][:, :, g * 256:(g + 1) * 256], in_=bank(bi).rearrange("p (c n) -> p c n", c=2)),
                    reads=[b_bank[bi]], writes=[b_gsb[gs]])
        for c in range(2):
            S.dma(lambda e, gs=gs, tt=tt, c=c: e.dma_start(
                out=gw_d[c * 2048 + tt * 128:c * 2048 + (tt + 1) * 128, :], in_=gsb[gs][:, c, :]),
                chan=b_gsb[gs], reads=[b_gsb[gs]])
    S.barrier()

    A.cur = mark
    dfc = A.alloc("dfc", [128, 16, 1024], BF16)
    dfs = A.alloc("dfs", [128, 16, 1024], BF16)
    b_dfc = [S.buf("dfc%d" % i) for i in range(2)]
    b_dfs = [S.buf("dfs%d" % i) for i in range(2)]
    altt = A.alloc("altt", [128, 16], BF16)
    b_alt = S.buf("alt")
    S.dma(lambda e: e.dma_start(out=altt[:], in_=alt_d), chan=b_alt, writes=[b_alt])
    dv = dftw_d.rearrange("(k p) t -> p k t", p=128)
    for (dst, bd, k0) in ((dfc, b_dfc, 0), (dfs, b_dfs, 16)):
        for hf in range(2):
            S.dma(lambda e, dst=dst, hf=hf, k0=k0: e.dma_start(out=dst[:, :, hf * 512:(hf + 1) * 512],
                                                                in_=dv[:, k0:k0 + 16, hf * 512:(hf + 1) * 512]),
                  chan=bd[hf], writes=[bd[hf]])
    ot3 = [A.alloc("ot3", [128, T], BF16) for _ in range(2)]
    b_ot3 = [S.buf("ot3%d" % i) for i in range(2)]
    Bs = [A.alloc("Bs", [128, 1024], F32) for _ in range(2)]
    b_Bs = [S.buf("Bs%d" % i) for i in range(2)]
    NW3 = 3
    wb3 = [A.alloc("wb3", [128, 32, 128], BF16) for _ in range(NW3)]
    b_wb3 = [S.buf("wb3%d" % i) for i in range(NW3)]
    gwv = gw_d.rearrange("(k p) n -> p k n", p=128)

    def load3(i):
        ws = i % NW3
        S.dma(lambda e: e.dma_start(out=wb3[ws][:], in_=gwv[:, :, i * 128:(i + 1) * 128]),
              chan=b_wb3[ws], writes=[b_wb3[ws]])

    load3(0)
    load3(1)
    for i in range(16):
        if i + 2 < 16:
            load3(i + 2)
        pp = i % 2
        ws = i % NW3
        pst = ps[pp]
        for (mat, bm, k0, c0) in ((dfc, b_dfc, 0, 0), (dfs, b_dfs, 16, 1024)):
            for hf in range(2):
                for kc in range(16):
                    S.op("pe", lambda e, pst=pst, ws=ws, kc=kc, hf=hf, mat=mat, k0=k0, c0=c0: e.matmul(
                        pst[:, c0 + hf * 512:c0 + (hf + 1) * 512], lhsT=wb3[ws][:, k0 + kc, :],
                        rhs=mat[:, kc, hf * 512:(hf + 1) * 512], start=(kc == 0), stop=(kc == 15)),
                        reads=[b_wb3[ws], bm[hf]], writes=[b_ps[pp]])
        for kc in range(16):
            S.op("pe", lambda e, pst=pst, ws=ws, kc=kc: e.matmul(
                pst[:, 1024:1025], lhsT=wb3[ws][:, kc, :], rhs=altt[:, kc:kc + 1], start=False, stop=(kc == 15),
                skip_group_check=True), reads=[b_wb3[ws], b_alt], writes=[b_ps[pp]])
        os_ = i % 2
        S.op("act", lambda e, pst=pst, os_=os_: e.activation(out=Bs[os_][:], in_=pst[:, 1024:2048], func=AF.Copy),
             reads=[b_ps[pp]], writes=[b_Bs[os_]])
        S.op("dve", lambda e, pst=pst, os_=os_: e.tensor_tensor(out=ot3[os_][:, 0:1024], in0=pst[:, 0:1024], in1=Bs[os_][:],
                                                                 op=ALU.subtract),
             reads=[b_ps[pp], b_Bs[os_]], writes=[b_ot3[os_]])
        pstride = ot3[os_][:].ap[0][0]
        rev = bass.AP(ot3[os_], 2047, [[pstride, 128], [-1, 1023]])
        S.op("dve", lambda e, pst=pst, os_=os_, rev=rev: e.tensor_tensor(out=rev, in0=pst[:, 1:1024], in1=Bs[os_][:, 1:1024],
                                                                          op=ALU.add),
             reads=[b_ps[pp], b_Bs[os_]], writes=[b_ot3[os_]])
        S.op("dve", lambda e, pst=pst, os_=os_: e.tensor_copy(out=ot3[os_][:, 0:1], in_=pst[:, 0:1]),
             reads=[b_ps[pp]], writes=[b_ot3[os_]])
        S.op("dve", lambda e, os_=os_: e.tensor_copy(out=ot3[os_][:, 1024:1025], in_=Bs[os_][:, 0:1]),
             reads=[b_Bs[os_]], writes=[b_ot3[os_]])
        S.dma(lambda e, os_=os_, i=i: e.dma_start(out=yT_d[i * 128:(i + 1) * 128, :], in_=ot3[os_][:]),
              chan=b_ot3[os_], reads=[b_ot3[os_]])
    S.barrier()
    if upto <= 3:
        return finish(nc, S)

    A.cur = mark
    qs = [A.alloc("qs", [128, 2, T], BF16) for _ in range(2)]
    ks = [A.alloc("ks", [128, 2, T], BF16) for _ in range(2)]
    vs = [A.alloc("vs", [128, 2, T], BF16) for _ in range(2)]
    V1 = [A.alloc("V1", [128, 16, 257], BF16) for _ in range(2)]
    oTs = [A.alloc("oTs", [128, 2, T], BF16) for _ in range(2)]
    b_qs = [S.buf("qs%d" % i) for i in range(2)]
    b_ks = [S.buf("ks%d" % i) for i in range(2)]
    b_vs = [S.buf("vs%d" % i) for i in range(2)]
    b_V1 = [S.buf("V1%d" % i) for i in range(2)]
    b_oTs = [S.buf("oTs%d" % i) for i in range(2)]
    NPT = 3
    pt = [A.alloc("pt", [128, 512], BF16) for _ in range(NPT)]
    b_pt = [S.buf("pt%d" % i) for i in range(NPT)]
    o1 = [A.alloc("o1", [128, 256], F32) for _ in range(2)]
    of = [A.alloc("of", [128, 256], F32) for _ in range(2)]
    sq = [A.alloc("sq", [128, 256], F32) for _ in range(2)]
    onb = [A.alloc("onb", [128, 256], BF16) for _ in range(2)]
    stat = [A.alloc("stat", [128, 8], F32) for _ in range(2)]
    b_o1 = [S.buf("o1%d" % i) for i in range(2)]
    b_of = [S.buf("of%d" % i) for i in range(2)]
    b_sq = [S.buf("sq%d" % i) for i in range(2)]
    b_onb = [S.buf("onb%d" % i) for i in range(2)]
    b_stat = [S.buf("stat%d" % i) for i in range(2)]
    for par in range(2):
        S.op("pool", lambda e, par=par: e.memset(V1[par][:, :, 256:257], 1.0), writes=[b_V1[par]])
    SC = 1.0 / float(np.sqrt(128.0))

    def load_head(h):
        p_ = h % 2
        r0 = h * 256
        for (dst, b_dst, src) in ((qs, b_qs, qT_d), (ks, b_ks, kT_d), (vs, b_vs, vT_d)):
            S.dma(lambda e, dst=dst, src=src: e.dma_start(
                out=dst[p_][:], in_=src[r0:r0 + 256, :].rearrange("(c p) t -> p c t", p=128)),
                chan=b_dst[p_], writes=[b_dst[p_]])

    def build_v(h, k4):
        p_ = h % 2
        for kk in range(4):
            kt = k4 * 4 + kk
            for vc in range(2):
                S.op("pe", lambda e, kk=kk, vc=vc, kt=kt: e.transpose(
                    bank_bf(7)[:, (kk * 2 + vc) * 128:(kk * 2 + vc + 1) * 128],
                    vs[p_][:, vc, kt * 128:(kt + 1) * 128], ident[:]),
                    reads=[b_vs[p_], b_ident], writes=[b_bank[7]])
        S.op("dve", lambda e: e.tensor_copy(
            out=V1[p_][:, k4 * 4:(k4 + 1) * 4, 0:256], in_=bank_bf(7).rearrange("p (k n) -> p k n", k=4)),
            reads=[b_bank[7]], writes=[b_V1[p_]])

    def out_transposes(p_, qsub, q0):
        for vc in range(2):
            S.op("pe", lambda e, vc=vc: e.transpose(
                bank_bf(7)[:, vc * 128:(vc + 1) * 128], onb[qsub][:, vc * 128:(vc + 1) * 128], ident[:]),
                reads=[b_onb[qsub], b_ident], writes=[b_bank[7]])
        S.op("dve", lambda e: e.tensor_copy(
            out=oTs[p_][:, :, q0:q0 + 128], in_=bank_bf(7)[:, 0:256].rearrange("p (c t) -> p c t", c=2)),
            reads=[b_bank[7]], writes=[b_oTs[p_]])

    def store_head(h):
        p_ = h % 2
        r0 = h * 256
        S.dma(lambda e: e.dma_start(
            out=oT_d[r0:r0 + 256, :].rearrange("(c p) t -> p c t", p=128), in_=oTs[p_][:]),
            chan=b_oTs[p_], reads=[b_oTs[p_]])

    steps = [(h, qb, c, k2) for h in range(NH) for qb in range(8) for c in range(2) for k2 in range(8)]
    NS = len(steps)

    def score(n):
        h, qb, c, k2 = steps[n]
        p_ = h % 2
        sb = 4 + n % 3
        pi = n % NPT
        for j in range(2):
            kt = 2 * k2 + j
            S.op("pe", lambda e, j=j, kt=kt: e.matmul(bank(sb)[:, j * 256:(j + 1) * 256], lhsT=ks[p_][:, c, kt * 128:(kt + 1) * 128],
                                                      rhs=qs[p_][:, c, qb * 256:(qb + 1) * 256], start=True, stop=True),
                 reads=[b_ks[p_], b_qs[p_]], writes=[b_bank[sb]])
        S.op("act", lambda e: e.activation(out=pt[pi][:], in_=bank(sb), func=AF.Exp, scale=SC),
             reads=[b_bank[sb]], writes=[b_pt[pi]])

    load_head(0)
    for k4 in range(4):
        build_v(0, k4)
    pending = []
    seqc = [0]

    def defer(due, fn):
        seqc[0] += 1
        pending.append((due, seqc[0], fn))
        pending.sort(key=lambda t: (t[0], t[1]))

    score(0)
    score(1)
    for n in range(NS):
        h, qb, c, k2 = steps[n]
        p_ = h % 2
        first = (qb == 0 and c == 0 and k2 == 0)
        if first and h + 1 < NH:
            load_head(h + 1)
        while pending and pending[0][0] <= n:
            pending.pop(0)[2]()
        if n + 2 < NS:
            score(n + 2)
        pi = n % NPT
        for j in range(2):
            kt = 2 * k2 + j
            for qsub in range(2):
                ab = c * 2 + qsub
                S.op("pe", lambda e, ab=ab, pi=pi, qsub=qsub, kt=kt, p_=p_, j=j: e.matmul(
                    bank(ab)[:, 0:257], lhsT=pt[pi][:, j * 256 + qsub * 128:j * 256 + (qsub + 1) * 128], rhs=V1[p_][:, kt, :],
                    start=(kt == 0), stop=(kt == 15)),
                    reads=[b_pt[pi], b_V1[p_]], writes=[b_bank[ab]])
        if h + 1 < NH and c == 0 and k2 == 4 and 2 <= qb < 6:
            build_v(h + 1, qb - 2)
        kt = 15 if k2 == 7 else -1
        if kt == 15:
            for qsub in range(2):
                ab = c * 2 + qsub
                st_ = stat[qsub]
                bs = b_stat[qsub]
                S.op("dve", lambda e, ab=ab, st_=st_: e.reciprocal(out=st_[:, 0:1], in_=bank(ab)[:, 256:257]),
                     reads=[b_bank[ab]], writes=[bs])
                if c == 0:
                    S.op("dve", lambda e, ab=ab, st_=st_, qsub=qsub: e.tensor_scalar(
                        out=o1[qsub][:], in0=bank(ab)[:, 0:256], scalar1=st_[:, 0:1], scalar2=None, op0=ALU.mult),
                        reads=[b_bank[ab], bs], writes=[b_o1[qsub]])
                else:
                    S.op("dve", lambda e, st_=st_: e.tensor_tensor(out=st_[:, 1:2], in0=st_[:, 0:1], in1=NEGLAM, op=ALU.mult),
                         reads=[bs, b_small], writes=[bs])
                    S.op("dve", lambda e, ab=ab, st_=st_, qsub=qsub: e.scalar_tensor_tensor(
                        out=of[qsub][:], in0=bank(ab)[:, 0:256], scalar=st_[:, 1:2], in1=o1[qsub][:],
                        op0=ALU.mult, op1=ALU.add), reads=[b_bank[ab], bs, b_o1[qsub]], writes=[b_of[qsub]])
                    S.op("pool", lambda e, qsub=qsub: e.tensor_tensor(out=sq[qsub][:], in0=of[qsub][:], in1=of[qsub][:], op=ALU.mult),
                         reads=[b_of[qsub]], writes=[b_sq[qsub]])
                    S.op("dve", lambda e, st_=st_, qsub=qsub: e.tensor_reduce(out=st_[:, 2:3], in_=sq[qsub][:], axis=AX.X, op=ALU.add),
                         reads=[b_sq[qsub]], writes=[bs])
                    S.op("dve", lambda e, st_=st_: e.tensor_scalar(out=st_[:, 3:4], in0=st_[:, 2:3], scalar1=1.0 / 256.0,
                                                                   scalar2=EPS, op0=ALU.mult, op1=ALU.add),
                         reads=[bs], writes=[bs])
                    q0 = qb * 256 + qsub * 128

                    def stage_act(st_=st_, bs=bs):
                        S.op("act", lambda e: e.activation(out=st_[:, 4:5], in_=st_[:, 3:4], func=AF.Ln),
                             reads=[bs], writes=[bs])
                        S.op("act", lambda e: e.activation(out=st_[:, 5:6], in_=st_[:, 4:5], func=AF.Exp, scale=-0.5),
                             reads=[bs], writes=[bs])

                    def stage_onb(st_=st_, bs=bs, qsub=qsub):
                        S.op("dve", lambda e: e.scalar_tensor_tensor(
                            out=onb[qsub][:], in0=of[qsub][:], scalar=st_[:, 5:6], in1=g08[:], op0=ALU.mult, op1=ALU.mult),
                            reads=[b_of[qsub], bs, b_g08], writes=[b_onb[qsub]])

                    defer(n + 4 + qsub, stage_act)
                    defer(n + 6 + qsub, stage_onb)
                    defer(n + 9 + qsub, (lambda p_=p_, qsub=qsub, q0=q0: out_transposes(p_, qsub, q0)))
                    if qb == 7 and qsub == 1:
                        defer(n + 12, (lambda h=h: store_head(h)))
    while pending:
        pending.pop(0)[2]()
    S.barrier()
    if upto <= 4:
        return finish(nc, S)

    A.cur = mark
    yTs = A.alloc("yTs", [128, 16, T], BF16)
    b_yTs = [S.buf("yTs%d" % i) for i in range(4)]
    gt = [A.alloc("gt", [128, T], BF16) for _ in range(2)]
    tm = [A.alloc("tm", [128, T], F32) for _ in range(2)]
    b_gt = [S.buf("gt%d" % i) for i in range(2)]
    b_tm = [S.buf("tm%d" % i) for i in range(2)]

    def pre5a(i):
        sl = i % 2
        S.dma(lambda e: e.dma_start(out=gt[sl][:], in_=gfT_d[i * 128:(i + 1) * 128, :]), chan=b_gt[sl], writes=[b_gt[sl]])

    pre5a(0)

    def epi5a(i, info, pst, b_pst):
        sl = i % 2
        if i + 1 < 32:
            pre5a(i + 1)
        S.op("dve", lambda e: e.tensor_tensor(out=tm[sl][:], in0=pst[:], in1=gt[sl][:], op=ALU.mult),
             reads=[b_pst, b_gt[sl]], writes=[b_tm[sl]])
        S.dma(lambda e: e.dma_start(out=tmpT_d[i * 128:(i + 1) * 128, :], in_=tm[sl][:]), chan=b_tm[sl], reads=[b_tm[sl]])

    gemm_fm(yTs, b_yTs, 16, [(wf_d, ct * 128, False, None) for ct in range(32)], epi5a,
            act_loader=lambda: load_actT(yTs, b_yTs, yT_d, 0, 16))
    S.barrier()

    A.cur = mark
    oTa = A.alloc("oTa", [128, 16, T], BF16)
    b_oTa = [S.buf("oTa%d" % i) for i in range(4)]
    gt = [A.alloc("gt", [128, T], BF16) for _ in range(2)]
    tm = [A.alloc("tm", [128, T], F32) for _ in range(2)]
    m1 = A.alloc("m1", [128, T], F32)
    mo = [A.alloc("mo", [128, T], BF16) for _ in range(2)]
    b_gt = [S.buf("gt%d" % i) for i in range(2)]
    b_tm = [S.buf("tm%d" % i) for i in range(2)]
    b_m1 = S.buf("m1")
    b_mo = [S.buf("mo%d" % i) for i in range(2)]

    def pre5b(i):
        sl = i % 2
        S.dma(lambda e: e.dma_start(out=gt[sl][:], in_=gaT_d[i * 128:(i + 1) * 128, :]), chan=b_gt[sl], writes=[b_gt[sl]])
        S.dma(lambda e: e.dma_start(out=tm[sl][:], in_=tmpT_d[i * 128:(i + 1) * 128, :]), chan=b_tm[sl], writes=[b_tm[sl]])

    pre5b(0)

    def epi5b(i, info, pst, b_pst):
        sl = i % 2
        if i + 1 < 32:
            pre5b(i + 1)
        S.op("dve", lambda e: e.tensor_tensor(out=m1[:], in0=pst[:], in1=gt[sl][:], op=ALU.mult),
             reads=[b_pst, b_gt[sl]], writes=[b_m1])
        S.op("dve", lambda e: e.tensor_tensor(out=mo[sl][:], in0=m1[:], in1=tm[sl][:], op=ALU.add),
             reads=[b_m1, b_tm[sl]], writes=[b_mo[sl]])
        S.dma(lambda e: e.dma_start(out=mT_d[i * 128:(i + 1) * 128, :], in_=mo[sl][:]), chan=b_mo[sl], reads=[b_mo[sl]])

    gemm_fm(oTa, b_oTa, 16, [(wa_d, ct * 128, False, None) for ct in range(32)], epi5b,
            act_loader=lambda: load_actT(oTa, b_oTa, oT_d, 0, 16))
    S.barrier()

    A.cur = mark
    gemm_tm(mT_d, [16, 16], wo_d, x_d, h1_d)
    S.barrier()
    if upto <= 6:
        return finish(nc, S)

    A.cur = mark
    act2 = A.alloc("act2", [128, 32, T], BF16)
    b_act2 = S.buf("act2")
    m7 = A.cur
    norm_to_actT(h1_d, g2_d, act2, b_act2)
    S.barrier()
    A.cur = m7
    sgt = A.alloc("sgt", [128, T], F32)
    b_sgt = S.buf("sgt")
    ot8 = [A.alloc("ot8", [128, T], BF16) for _ in range(2)]
    b_ot8 = [S.buf("ot8%d" % i) for i in range(2)]

    def epi8(i, info, pst, b_pst):
        kind, j = info
        if kind == "g":
            S.op("act", lambda e: e.activation(out=sgt[:], in_=pst[:], func=AF.Silu), reads=[b_pst], writes=[b_sgt])
        else:
            os_ = j % 2
            S.op("dve", lambda e: e.tensor_tensor(out=ot8[os_][:], in0=pst[:], in1=sgt[:], op=ALU.mult),
                 reads=[b_pst, b_sgt], writes=[b_ot8[os_]])
            S.dma(lambda e: e.dma_start(out=hidT_d[j * 128:(j + 1) * 128, :], in_=ot8[os_][:]),
                  chan=b_ot8[os_], reads=[b_ot8[os_]])

    tiles8 = []
    for j in range(HC):
        tiles8.append((wg_d, j * 128, False, ("g", j)))
        tiles8.append((wu_d, j * 128, False, ("u", j)))
    gemm_fm(act2, b_act2, 32, tiles8, epi8, PK=8, NSTG=3)
    S.barrier()

    A.cur = mark
    gemm_tm(hidT_d, [22, 22, 21, 21], wd_d, h1_d, h2_d)
    S.barrier()

    A.cur = mark
    gbc = A.alloc("gbc", [128, D], F32)
    b_gbc = S.buf("gbc")
    S.dma(lambda e: e.dma_start(out=gbc[:], in_=g3_d.partition_broadcast(128)), chan=b_gbc, writes=[b_gbc])
    xt = [A.alloc("xt", [128, D], F32) for _ in range(2)]
    yo = [A.alloc("yo", [128, D], F32) for _ in range(2)]
    junk = A.alloc("junk", [128, D], BF16)
    st = A.alloc("st", [128, NT, 2], F32)
    b_xt = [S.buf("xt%d" % i) for i in range(2)]
    b_yo = [S.buf("yo%d" % i) for i in range(2)]
    b_junk, b_st = S.buf("junk"), S.buf("st")
    S.op("pool", lambda e: e.memset(st[:], 0.0), writes=[b_st])
    for tt in range(NT):
        sl = tt % 2
        S.dma(lambda e, sl=sl, tt=tt: e.dma_start(out=xt[sl][:], in_=h2_d[tt * 128:(tt + 1) * 128, :]),
              chan=b_xt[sl], writes=[b_xt[sl]])
        S.op("act", lambda e, sl=sl, tt=tt: e.activation(out=junk[:], in_=xt[sl][:], func=AF.Square, accum_out=st[:, tt, 0:1]),
             reads=[b_xt[sl], b_st], writes=[b_junk, b_st])
        S.op("dve", lambda e, tt=tt: e.tensor_scalar(out=st[:, tt, 1:2], in0=st[:, tt, 0:1], scalar1=1.0 / D, scalar2=EPS,
                                                     op0=ALU.mult, op1=ALU.add), reads=[b_st], writes=[b_st])
        S.op("act", lambda e, tt=tt: e.activation(out=st[:, tt, 1:2], in_=st[:, tt, 1:2], func=AF.Ln), reads=[b_st], writes=[b_st])
        S.op("act", lambda e, tt=tt: e.activation(out=st[:, tt, 1:2], in_=st[:, tt, 1:2], func=AF.Exp, scale=-0.5),
             reads=[b_st], writes=[b_st])
        S.op("dve", lambda e, sl=sl, tt=tt: e.scalar_tensor_tensor(out=yo[sl][:], in0=xt[sl][:], scalar=st[:, tt, 1:2],
                                                                   in1=gbc[:], op0=ALU.mult, op1=ALU.mult),
             reads=[b_xt[sl], b_st, b_gbc], writes=[b_yo[sl]])
        S.dma(lambda e, sl=sl, tt=tt: e.dma_start(out=out_d[tt * 128:(tt + 1) * 128, :], in_=yo[sl][:]),
              chan=b_yo[sl], reads=[b_yo[sl]])
    S.barrier()
    return finish(nc, S)


def finish(nc, S):
    S.finalize()
    return nc


_CACHE = {}


def kernel(**inputs):
    f32 = np.float32
    x = np.asarray(inputs["x"], dtype=f32)
    pos = np.asarray(inputs["positions"], dtype=np.int32)
    if "nc" not in _CACHE:
        _CACHE["nc"] = build()
        _CACHE["consts"] = _consts()
    nc = _CACHE["nc"]
    dftw, ccsc, ident, invf, alt = _CACHE["consts"]
    g = lambda k: np.ascontiguousarray(np.asarray(inputs[k], dtype=f32)[0])
    bg = np.ascontiguousarray(np.asarray(inputs["b_gate"], dtype=f32)[0].reshape(2, 32, 128).transpose(2, 0, 1).reshape(128, 64))
    shared = {"norm_mix_g": g("norm_mix_g"), "w_in": g("w_in"), "bgate": bg,
              "lambda_q1": g("lambda_q1"), "lambda_k1": g("lambda_k1"), "lambda_q2": g("lambda_q2"),
              "lambda_k2": g("lambda_k2"), "subln_g": g("subln_g"), "w_fourier_out": g("w_fourier_out"),
              "w_attn_out": g("w_attn_out"), "w_out": g("w_out"), "norm_ffn_g": g("norm_ffn_g"),
              "w_ffn_gate": g("w_ffn_gate"), "w_ffn_up": g("w_ffn_up"), "w_ffn_down": g("w_ffn_down"),
              "norm_final_g": np.ascontiguousarray(np.asarray(inputs["norm_final_g"], dtype=f32)),
              "dftw": dftw, "ccsc": ccsc, "ident": ident, "invf": invf, "alt": alt}
    in_maps = []
    for b in range(8):
        m = dict(shared)
        m["x"] = np.ascontiguousarray(x[b])
        m["positions"] = np.ascontiguousarray(pos[b])
        in_maps.append(m)
    res = run_bass_kernel_spmd(nc, in_maps, core_ids=list(range(8)))
    return np.stack([np.asarray(r["out"], dtype=f32) for r in res.results], axis=0)
```

```python
import ml_dtypes
import time
import numpy as np
import concourse.bass as bass
import concourse.mybir as mybir
from concourse.bass_utils import run_bass_kernel_spmd

F32 = mybir.dt.float32
BF16 = mybir.dt.bfloat16
I32 = mybir.dt.int32
AF = mybir.ActivationFunctionType
ALU = mybir.AluOpType
AX = mybir.AxisListType

COMPUTE = ("pe", "act", "dve", "pool")


class Buf:
    __slots__ = ("name", "lw", "rd", "sem")

    def __init__(self, name):
        self.name = name
        self.lw = None
        self.rd = {}
        self.sem = None


class Ins:
    __slots__ = ("stream", "src", "fn", "deps", "sig", "sigval", "dma")

    def __init__(self, stream, src, fn, dma):
        self.stream = stream
        self.src = src
        self.fn = fn
        self.deps = []
        self.sig = False
        self.sigval = 0
        self.dma = dma


class Sched:
    def __init__(self, nc):
        self.nc = nc
        self.streams = {"pe": [], "act": [], "dve": [], "pool": [], "sp": []}
        self.all = []
        self.sems = {e: nc.alloc_semaphore("sem_" + e) for e in COMPUTE}
        self.dma_sem_pool = []
        self.ndma = 0
        self.last = {}
        self.bufs_with_sem = []

    def buf(self, name):
        return Buf(name)

    def _dma_sem(self, b):
        if b.sem is None:
            if self.dma_sem_pool:
                b.sem = self.dma_sem_pool.pop()
            else:
                self.ndma += 1
                b.sem = self.nc.alloc_semaphore("dsem%d" % self.ndma)
            self.bufs_with_sem.append(b)
        return b.sem

    def _track(self, ins, reads, writes):
        deps = []
        for b in reads:
            if b.lw is not None:
                deps.append((b.lw, "raw"))
        for b in writes:
            if b.lw is not None:
                deps.append((b.lw, "waw"))
            for r in b.rd.values():
                deps.append((r, "war"))
        for d, kind in deps:
            if d is ins:
                continue
            if (not ins.dma) and (not d.dma) and d.src == ins.src:
                if ins.src == "pe" or kind != "raw":
                    continue
            d.sig = True
            ins.deps.append(d)
        for b in reads:
            b.rd[ins.src] = ins
        for b in writes:
            b.lw = ins
            b.rd = {}

    def op(self, eng, fn, reads=(), writes=()):
        ins = Ins(eng, eng, fn, False)
        self._track(ins, reads, writes)
        self.streams[eng].append(ins)
        self.all.append(ins)
        self.last[eng] = ins
        return ins

    def dma(self, fn, chan, reads=(), writes=(), q="sp"):
        sem = self._dma_sem(chan)
        ins = Ins(q, sem, fn, True)
        ins.sig = True
        self._track(ins, reads, writes)
        self.streams[q].append(ins)
        self.all.append(ins)
        self.last[sem] = ins
        return ins

    def barrier(self):
        lasts = list(self.last.values())
        for d in lasts:
            d.sig = True
        for s in self.streams:
            ins = Ins(s, None, None, False)
            ins.deps = list(lasts)
            self.streams[s].append(ins)
            self.all.append(ins)
        for b in self.bufs_with_sem:
            self.dma_sem_pool.append(b.sem)
            b.sem = None
        self.bufs_with_sem = []

    def finalize(self):
        nc = self.nc
        cnt = {}
        for ins in self.all:
            if ins.src is None:
                continue
            if ins.sig:
                cnt[ins.src] = cnt.get(ins.src, 0) + (16 if ins.dma else 1)
                ins.sigval = cnt[ins.src]
        sems = self.sems
        streams = self.streams

        def replay(sname):
            def run(e):
                waited = {}
                for ins in streams[sname]:
                    need = {}
                    for d in ins.deps:
                        k = d.src
                        if d.sigval > need.get(k, 0):
                            need[k] = d.sigval
                    for k, v in need.items():
                        if v > waited.get(k, 0):
                            waited[k] = v
                            e.wait_ge(sems[k] if isinstance(k, str) else k, v)
                    if ins.fn is None:
                        continue
                    bi = ins.fn(e)
                    if ins.sig:
                        if ins.dma:
                            bi.then_inc(ins.src, 16)
                        else:
                            bi.then_inc(sems[ins.src], 1)
            return run

        with nc.Block() as block:
            block.tensor(replay("pe"))
            block.scalar(replay("act"))
            block.vector(replay("dve"))
            block.gpsimd(replay("pool"))
            block.sync(replay("sp"))


class SbufAlloc:
    def __init__(self, nc, base=None, top=None):
        self.nc = nc
        self.base = ((nc.sbuf_base + 63) // 64) * 64 if base is None else base
        self.top = nc.sbuf_top if top is None else top
        self.cur = self.base
        self.n = 0

    def reset(self):
        self.cur = self.base

    def alloc(self, name, shape, dtype):
        esz = 4 if dtype in (F32, I32) else 2
        per = esz
        for s in shape[1:]:
            per *= s
        off = self.cur
        self.cur = ((off + per + 63) // 64) * 64
        assert self.cur <= self.top, ("SBUF overflow", name, self.cur, self.top)
        self.n += 1
        return self.nc.alloc_sbuf_tensor_at("%s_%d" % (name, self.n), list(shape), dtype, offset=off)


D = 4096
T = 2048
NT = T // 128
FW = 2048
AW = 2048
HID = 11008
HC = HID // 128
NH = 8
EPS = 1e-6
LAM_INIT = 0.2


def _consts():
    s = np.arange(2048, dtype=np.float64)
    sp = np.arange(1024, dtype=np.float64)
    ang = 2.0 * np.pi * np.outer(s, sp) / 2048.0
    dftw = np.concatenate([np.cos(ang), np.sin(ang)], 0) / np.sqrt(2048.0)
    alt = np.tile(((-1.0) ** np.arange(128)).reshape(128, 1), (1, 16)) / np.sqrt(2048.0)
    c = np.arange(256, dtype=np.float64)
    angc = 2.0 * np.pi * np.outer(c, c) / 256.0
    ccsc = np.concatenate([np.cos(angc), np.sin(angc)], 1) / 16.0
    ident = np.eye(128)
    j = np.arange(0, 128, 2, dtype=np.float32) / np.float32(128)
    inv = (np.float32(10000.0) ** (-j)).astype(np.float32)
    invf = np.concatenate([inv, inv]).reshape(128, 1).astype(np.float32)
    return (dftw.astype(ml_dtypes.bfloat16), ccsc.astype(ml_dtypes.bfloat16),
            ident.astype(ml_dtypes.bfloat16), invf, alt.astype(ml_dtypes.bfloat16))


class K:
    pass


def build(upto=99, dbg=False):
    nc = bass.Bass("TRN2", target_bir_lowering=False)
    S = Sched(nc)

    def din(name, shape, dt):
        return nc.dram_tensor(name, list(shape), dt, kind="ExternalInput").ap()

    def dscr(name, shape, dt):
        return nc.dram_tensor(name, list(shape), dt, kind=("ExternalOutput" if dbg else "Internal")).ap()

    x_d = din("x", [T, D], F32)
    pos_d = din("positions", [T], I32)
    g1_d = din("norm_mix_g", [D], F32)
    win_d = din("w_in", [D, 16384], F32)
    bg_d = din("bgate", [128, 64], F32)
    lq1_d = din("lambda_q1", [128], F32)
    lk1_d = din("lambda_k1", [128], F32)
    lq2_d = din("lambda_q2", [128], F32)
    lk2_d = din("lambda_k2", [128], F32)
    sg_d = din("subln_g", [256], F32)
    wf_d = din("w_fourier_out", [FW, D], F32)
    wa_d = din("w_attn_out", [AW, D], F32)
    wo_d = din("w_out", [D, D], F32)
    g2_d = din("norm_ffn_g", [D], F32)
    wg_d = din("w_ffn_gate", [D, HID], F32)
    wu_d = din("w_ffn_up", [D, HID], F32)
    wd_d = din("w_ffn_down", [HID, D], F32)
    g3_d = din("norm_final_g", [D], F32)
    dftw_d = din("dftw", [4096, 1024], BF16)
    alt_d = din("alt", [128, 16], BF16)
    ccsc_d = din("ccsc", [256, 512], BF16)
    ident_d = din("ident", [128, 128], BF16)
    invf_d = din("invf", [128, 1], F32)
    out_d = nc.dram_tensor("out", [T, D], F32, kind="ExternalOutput").ap()

    fT_d = dscr("fT", [FW, T], BF16)
    qT_d = dscr("qT", [AW, T], BF16)
    kT_d = dscr("kT", [AW, T], BF16)
    vT_d = dscr("vT", [AW, T], BF16)
    gfT_d = dscr("gfT", [D, T], BF16)
    gaT_d = dscr("gaT", [D, T], BF16)
    gw_d = dscr("gw", [4096, 2048], BF16)
    yT_d = dscr("yT", [FW, T], BF16)
    oT_d = dscr("oT", [AW, T], BF16)
    tmpT_d = dscr("tmpT", [D, T], F32)
    mT_d = dscr("mT", [D, T], BF16)
    h1_d = dscr("h1", [T, D], F32)
    hidT_d = dscr("hidT", [HID, T], BF16)
    h2_d = dscr("h2", [T, D], F32)

    A = SbufAlloc(nc)
    ps = [nc.alloc_psum_tensor("psA", [128, 2048], F32), nc.alloc_psum_tensor("psB", [128, 2048], F32)]
    psb = [p.bitcast(BF16) for p in ps]
    b_ps = [S.buf("psA"), S.buf("psB")]
    b_bank = [S.buf("bank%d" % i) for i in range(8)]

    def bank(i):
        return ps[i // 4][:, (i % 4) * 512:(i % 4 + 1) * 512]

    def bank_bf(i):
        return psb[i // 4][:, (i % 4) * 1024:(i % 4 + 1) * 1024]

    ident = A.alloc("ident", [128, 128], BF16)
    b_ident = S.buf("ident")
    S.dma(lambda e: e.dma_start(out=ident[:], in_=ident_d), chan=b_ident, writes=[b_ident])
    small = A.alloc("small", [128, 64], F32)
    b_small = S.buf("small")
    bgate = A.alloc("bgate", [128, 64], F32)
    b_bgate = S.buf("bgate")
    S.dma(lambda e: e.dma_start(out=bgate[:], in_=bg_d), chan=b_bgate, writes=[b_bgate])
    g08 = A.alloc("g08", [128, 256], F32)
    b_g08 = S.buf("g08")
    S.dma(lambda e: e.dma_start(out=g08[:], in_=sg_d.partition_broadcast(128)), chan=b_g08, writes=[b_g08])
    S.op("dve", lambda e: e.tensor_scalar(out=g08[:], in0=g08[:], scalar1=1.0 - LAM_INIT, scalar2=None, op0=ALU.mult),
         reads=[b_g08], writes=[b_g08])
    lam4 = A.alloc("lam4", [128, 4, 128], F32)
    b_lam4 = S.buf("lam4")
    for i, dd in enumerate([lq1_d, lk1_d, lq2_d, lk2_d]):
        S.dma(lambda e, i=i, dd=dd: e.dma_start(out=lam4[:, i, :], in_=dd.partition_broadcast(128)),
              chan=b_lam4, writes=[b_lam4])
    lamp = A.alloc("lamp", [128, 2, 128], F32)
    b_lamp = S.buf("lamp")
    S.op("dve", lambda e: e.tensor_tensor(out=lamp[:, 0, :], in0=lam4[:, 0, :], in1=lam4[:, 1, :], op=ALU.mult),
         reads=[b_lam4], writes=[b_lamp])
    S.op("dve", lambda e: e.tensor_tensor(out=lamp[:, 1, :], in0=lam4[:, 2, :], in1=lam4[:, 3, :], op=ALU.mult),
         reads=[b_lam4, b_lamp], writes=[b_lamp])
    S.op("dve", lambda e: e.tensor_reduce(out=small[:, 0:2], in_=lamp[:], axis=AX.X, op=ALU.add),
         reads=[b_lamp], writes=[b_small])
    S.op("act", lambda e: e.activation(out=small[:, 2:4], in_=small[:, 0:2], func=AF.Exp),
         reads=[b_small], writes=[b_small])
    S.op("dve", lambda e: e.tensor_tensor(out=small[:, 4:5], in0=small[:, 3:4], in1=small[:, 2:3], op=ALU.subtract),
         reads=[b_small], writes=[b_small])
    S.op("dve", lambda e: e.tensor_scalar(out=small[:, 5:6], in0=small[:, 4:5], scalar1=-LAM_INIT, scalar2=None, op0=ALU.add),
         reads=[b_small], writes=[b_small])
    NEGLAM = small[:, 5:6]
    A.base = A.cur

    K.nc, K.S, K.A = nc, S, A

    def norm_to_actT(src_d, g_d, actT, b_act):
        gbc = A.alloc("gbc", [128, D], F32)
        b_gbc = S.buf("gbc")
        S.dma(lambda e: e.dma_start(out=gbc[:], in_=g_d.partition_broadcast(128)), chan=b_gbc, writes=[b_gbc])
        xt = [A.alloc("xt", [128, D], F32) for _ in range(2)]
        b_xt = [S.buf("xt%d" % i) for i in range(2)]
        junk = A.alloc("junk", [128, D], BF16)
        b_junk = S.buf("junk")
        ub = [A.alloc("ub", [128, D], BF16) for _ in range(2)]
        b_ub = [S.buf("ub%d" % i) for i in range(2)]
        st = A.alloc("st", [128, NT, 2], F32)
        b_st = S.buf("st")
        S.op("pool", lambda e: e.memset(st[:], 0.0), writes=[b_st])
        for tt in range(NT):
            sl = tt % 2
            S.dma(lambda e, sl=sl, tt=tt: e.dma_start(out=xt[sl][:], in_=src_d[tt * 128:(tt + 1) * 128, :]),
                  chan=b_xt[sl], writes=[b_xt[sl]])
            S.op("act", lambda e, sl=sl, tt=tt: e.activation(out=junk[:], in_=xt[sl][:], func=AF.Square,
                                                             accum_out=st[:, tt, 0:1]),
                 reads=[b_xt[sl], b_st], writes=[b_junk, b_st])
            S.op("dve", lambda e, tt=tt: e.tensor_scalar(out=st[:, tt, 1:2], in0=st[:, tt, 0:1], scalar1=1.0 / D,
                                                         scalar2=EPS, op0=ALU.mult, op1=ALU.add),
                 reads=[b_st], writes=[b_st])
            S.op("act", lambda e, tt=tt: e.activation(out=st[:, tt, 1:2], in_=st[:, tt, 1:2], func=AF.Ln),
                 reads=[b_st], writes=[b_st])
            S.op("act", lambda e, tt=tt: e.activation(out=st[:, tt, 1:2], in_=st[:, tt, 1:2], func=AF.Exp, scale=-0.5),
                 reads=[b_st], writes=[b_st])
            S.op("dve", lambda e, sl=sl, tt=tt: e.scalar_tensor_tensor(out=ub[sl][:], in0=xt[sl][:], scalar=st[:, tt, 1:2],
                                                                       in1=gbc[:], op0=ALU.mult, op1=ALU.mult),
                 reads=[b_xt[sl], b_st, b_gbc], writes=[b_ub[sl]])
            for c8 in range(4):
                bi = (tt * 4 + c8) % 8
                for j in range(8):
                    kc = c8 * 8 + j
                    S.op("pe", lambda e, bi=bi, j=j, kc=kc, sl=sl: e.transpose(
                        bank_bf(bi)[:, j * 128:(j + 1) * 128], ub[sl][:, kc * 128:(kc + 1) * 128], ident[:]),
                        reads=[b_ub[sl], b_ident], writes=[b_bank[bi]])
                eng = "act" if c8 % 2 == 0 else "dve"
                if eng == "act":
                    S.op("act", lambda e, bi=bi, c8=c8, tt=tt: e.activation(
                        out=actT[:, c8 * 8:(c8 + 1) * 8, tt * 128:(tt + 1) * 128],
                        in_=bank_bf(bi).rearrange("p (k t) -> p k t", k=8), func=AF.Copy),
                        reads=[b_bank[bi]], writes=[b_act])
                else:
                    S.op("dve", lambda e, bi=bi, c8=c8, tt=tt: e.tensor_copy(
                        out=actT[:, c8 * 8:(c8 + 1) * 8, tt * 128:(tt + 1) * 128],
                        in_=bank_bf(bi).rearrange("p (k t) -> p k t", k=8)),
                        reads=[b_bank[bi]], writes=[b_act])

    def load_actT(actT, b_act, src_d, kc0, KC):
        v = src_d.rearrange("(k p) t -> p k t", p=128)
        for tb in range(4):
            bb = b_act[tb] if isinstance(b_act, list) else b_act
            S.dma(lambda e, tb=tb: e.dma_start(out=actT[:, 0:KC, tb * 512:(tb + 1) * 512],
                                               in_=v[:, kc0:kc0 + KC, tb * 512:(tb + 1) * 512]),
                  chan=bb, writes=[bb])

    def gemm_fm(actT, b_act, KC, tiles, epi, kc0=0, PK=16, NW=3, NSTG=2, DIST=2, act_loader=None):
        stg = [A.alloc("stg", [128, PK, 128], F32) for _ in range(NSTG)]
        b_stg = [S.buf("stg%d" % i) for i in range(NSTG)]
        wb = [A.alloc("wb", [128, KC, 128], BF16) for _ in range(NW)]
        b_wb = [S.buf("wb%d" % i) for i in range(NW)]
        state = {"pc": 0}

        def load(i):
            w_ap, c0, isbf, _ = tiles[i]
            ws = i % NW
            wv = w_ap.rearrange("(k p) n -> p k n", p=128)
            if isbf:
                S.dma(lambda e: e.dma_start(out=wb[ws][:], in_=wv[:, kc0:kc0 + KC, c0:c0 + 128]),
                      chan=b_wb[ws], writes=[b_wb[ws]])
                return
            for p0 in range(0, KC, PK):
                n = min(PK, KC - p0)
                sl = state["pc"] % NSTG
                state["pc"] += 1
                S.dma(lambda e, sl=sl, p0=p0, n=n: e.dma_start(
                    out=stg[sl][:, 0:n, :], in_=wv[:, kc0 + p0:kc0 + p0 + n, c0:c0 + 128]),
                    chan=b_stg[sl], writes=[b_stg[sl]])
                S.op("pool", lambda e, sl=sl, p0=p0, n=n: e.tensor_copy(out=wb[ws][:, p0:p0 + n, :], in_=stg[sl][:, 0:n, :]),
                     reads=[b_stg[sl]], writes=[b_wb[ws]])

        n = len(tiles)
        for i in range(min(DIST, n)):
            load(i)
        if act_loader is not None:
            act_loader()
        for i in range(n):
            if i + DIST < n:
                load(i + DIST)
            pp = i % 2
            ws = i % NW
            for tb in range(4):
                for kc in range(KC):
                    S.op("pe", lambda e, pp=pp, tb=tb, ws=ws, kc=kc: e.matmul(
                        ps[pp][:, tb * 512:(tb + 1) * 512], lhsT=wb[ws][:, kc, :],
                        rhs=actT[:, kc, tb * 512:(tb + 1) * 512], start=(kc == 0), stop=(kc == KC - 1)),
                        reads=[b_wb[ws], (b_act[tb] if isinstance(b_act, list) else b_act)], writes=[b_ps[pp]])
            epi(i, tiles[i][3], ps[pp], b_ps[pp])

    def gemm_tm(src_d, KCs, w_ap, resid0_d, out_d_, NW=2, NSTG=3, PK=4):
        KCm = max(KCs)
        NSL = 8
        aT = A.alloc("aT", [128, KCm, T], BF16)
        b_aT = [S.buf("aT%d" % i) for i in range(NSL)]
        stg = [A.alloc("stg", [128, PK, 512], F32) for _ in range(NSTG)]
        b_stg = [S.buf("stg%d" % i) for i in range(NSTG)]
        wb = [A.alloc("wb", [128, KCm, 512], BF16) for _ in range(NW)]
        b_wb = [S.buf("wb%d" % i) for i in range(NW)]
        NR = 8
        rt = [A.alloc("rt", [128, 512], F32) for _ in range(NR)]
        b_rt = [S.buf("rt%d" % i) for i in range(NR)]
        dt = {}
        for tt in range(NT):
            for cb in range(8):
                dt[(tt, cb)] = S.buf("dt")
        wv = w_ap.rearrange("(k p) n -> p k n", p=128)
        sv = src_d.rearrange("(k p) t -> p k t", p=128)
        state = {"pc": 0}
        kc0s = [sum(KCs[:i]) for i in range(len(KCs))]

        def load_slice(pi_, sl_):
            KC, kc0 = KCs[pi_], kc0s[pi_]
            S.dma(lambda e: e.dma_start(out=aT[:, 0:KC, sl_ * 256:(sl_ + 1) * 256],
                                        in_=sv[:, kc0:kc0 + KC, sl_ * 256:(sl_ + 1) * 256]),
                  chan=b_aT[sl_], writes=[b_aT[sl_]])

        def pieces_of(pi_):
            KC = KCs[pi_]
            return [(p0, min(PK, KC - p0)) for p0 in range(0, KC, PK)]

        def load_piece(gb, k):
            pi_, cb = gb // 8, gb % 8
            ws = gb % NW
            p0, n = pieces_of(pi_)[k]
            kc0 = kc0s[pi_]
            sl = state["pc"] % NSTG
            state["pc"] += 1
            S.dma(lambda e: e.dma_start(
                out=stg[sl][:, 0:n, :], in_=wv[:, kc0 + p0:kc0 + p0 + n, cb * 512:(cb + 1) * 512]),
                chan=b_stg[sl], writes=[b_stg[sl]])
            S.op("pool", lambda e: e.tensor_copy(out=wb[ws][:, p0:p0 + n, :], in_=stg[sl][:, 0:n, :]),
                 reads=[b_stg[sl]], writes=[b_wb[ws]])

        NP = len(KCs)
        for k in range(len(pieces_of(0))):
            load_piece(0, k)
        for sl_ in range(NSL):
            load_slice(0, sl_)
        LA = 3
        NIT = NP * 8 * NT

        def load_resid(j):
            gb_, tt_ = j // NT, j % NT
            pj, cbj = gb_ // 8, gb_ % 8
            rsj = j % NR
            src = resid0_d if pj == 0 else out_d_
            rd = [dt[(tt_, cbj)]] if pj > 0 else []
            S.dma(lambda e: e.dma_start(
                out=rt[rsj][:], in_=src[tt_ * 128:(tt_ + 1) * 128, cbj * 512:(cbj + 1) * 512]),
                chan=b_rt[rsj], reads=rd, writes=[b_rt[rsj]])

        for j in range(LA):
            load_resid(j)
        it = 0
        for gb in range(NP * 8):
            pi_, cb = gb // 8, gb % 8
            KC = KCs[pi_]
            ws = gb % NW
            for tt in range(NT):
                if gb + 1 < NP * 8 and tt % 2 == 0 and tt // 2 < len(pieces_of((gb + 1) // 8)):
                    load_piece(gb + 1, tt // 2)
                if it + LA < NIT:
                    load_resid(it + LA)
                bi = it % 8
                rs = it % NR
                it += 1
                for kc in range(KC):
                    S.op("pe", lambda e, bi=bi, ws=ws, kc=kc, tt=tt, KC=KC: e.matmul(
                        bank(bi), lhsT=aT[:, kc, tt * 128:(tt + 1) * 128], rhs=wb[ws][:, kc, :],
                        start=(kc == 0), stop=(kc == KC - 1)),
                        reads=[b_wb[ws], b_aT[tt // 2]], writes=[b_bank[bi]])
                S.op("dve", lambda e, bi=bi, rs=rs: e.tensor_tensor(out=rt[rs][:], in0=bank(bi), in1=rt[rs][:], op=ALU.add),
                     reads=[b_bank[bi], b_rt[rs]], writes=[b_rt[rs]])
                S.dma(lambda e, rs=rs, tt=tt, cb=cb: e.dma_start(
                    out=out_d_[tt * 128:(tt + 1) * 128, cb * 512:(cb + 1) * 512], in_=rt[rs][:]),
                    chan=b_rt[rs], reads=[b_rt[rs]], writes=[dt[(tt, cb)]])
                if cb == 7 and pi_ + 1 < NP and tt % 2 == 1:
                    load_slice(pi_ + 1, tt // 2)

    K.norm_to_actT, K.load_actT, K.gemm_fm, K.gemm_tm = norm_to_actT, load_actT, gemm_fm, gemm_tm
    K.bank, K.bank_bf, K.b_bank, K.ps, K.psb, K.b_ps = bank, bank_bf, b_bank, ps, psb, b_ps

    mark = A.cur
    actT = A.alloc("actT", [128, 32, T], BF16)
    b_act = S.buf("actT")
    mark1 = A.cur
    norm_to_actT(x_d, g1_d, actT, b_act)
    S.barrier()
    if upto <= 0:
        return finish(nc, S)

    A.cur = mark1
    cosT = A.alloc("cosT", [128, T], F32)
    sinT = A.alloc("sinT", [128, T], F32)
    b_rope = S.buf("rope")
    _pm = A.cur
    posi = A.alloc("posi", [128, T], I32)
    A.cur = _pm
    t1 = A.alloc("t1", [128, 1024], F32)
    t2 = A.alloc("t2", [128, 1024], F32)
    b_posi = S.buf("posi")
    invf = A.alloc("invf", [128, 1], F32)
    b_invf = S.buf("invf")
    S.dma(lambda e: e.dma_start(out=invf[:], in_=invf_d), chan=b_invf, writes=[b_invf])
    S.dma(lambda e: e.dma_start(out=posi[:], in_=pos_d.partition_broadcast(128)), chan=b_posi, writes=[b_posi])
    S.op("dve", lambda e: e.tensor_copy(out=cosT[:], in_=posi[:]), reads=[b_posi], writes=[b_rope])
    S.op("dve", lambda e: e.tensor_scalar(out=cosT[:], in0=cosT[:], scalar1=invf[:, 0:1], scalar2=None, op0=ALU.mult),
         reads=[b_rope, b_invf], writes=[b_rope])
    TWO_PI = 2.0 * np.pi
    PI_LO = 3.1415925
    _om = A.cur
    tmpk = A.alloc("tmpk", [128, T], F32)
    A.cur = _om
    ot = [A.alloc("ot", [128, T], BF16) for _ in range(2)]
    b_ot = [S.buf("ot%d" % i) for i in range(2)]

    def reduce_sin(dst):
        S.op("dve", lambda e: e.tensor_scalar(out=posi[:], in0=cosT[:], scalar1=float(1.0 / TWO_PI), scalar2=None, op0=ALU.mult),
             reads=[b_rope], writes=[b_posi])
        S.op("dve", lambda e: e.tensor_copy(out=tmpk[:], in_=posi[:]), reads=[b_posi], writes=b_ot)
        S.op("dve", lambda e: e.scalar_tensor_tensor(out=dst[:], in0=tmpk[:], scalar=-float(TWO_PI), in1=cosT[:],
                                                     op0=ALU.mult, op1=ALU.add), reads=b_ot + [b_rope], writes=[b_rope])
        S.op("dve", lambda e: e.tensor_scalar(out=dst[:], in0=dst[:], scalar1=PI_LO, scalar2=-PI_LO, op0=ALU.min, op1=ALU.max),
             reads=[b_rope], writes=[b_rope])
        S.op("act", lambda e: e.activation(out=dst[:], in_=dst[:], func=AF.Sin), reads=[b_rope], writes=[b_rope])

    reduce_sin(sinT)
    S.op("dve", lambda e: e.tensor_scalar(out=cosT[:], in0=cosT[:], scalar1=float(0.5 * np.pi), scalar2=None, op0=ALU.add),
         reads=[b_rope], writes=[b_rope])
    reduce_sin(cosT)

    qsw = A.alloc("qsw", [128, 1024], F32)
    b_qsw, b_t1, b_t2 = S.buf("qsw"), S.buf("t1"), S.buf("t2")

    def epi1(i, info, pst, b_pst):
        kind, dst, row0, bcol = info
        os_ = i % 2
        if kind == "copy":
            if i % 2 == 0:
                S.op("act", lambda e: e.activation(out=ot[os_][:], in_=pst[:], func=AF.Copy),
                     reads=[b_pst], writes=[b_ot[os_]])
            else:
                S.op("dve", lambda e: e.tensor_copy(out=ot[os_][:], in_=pst[:]), reads=[b_pst], writes=[b_ot[os_]])
        elif kind == "sig":
            S.op("act", lambda e: e.activation(out=ot[os_][:], in_=pst[:], func=AF.Sigmoid, bias=bgate[:, bcol:bcol + 1]),
                 reads=[b_pst, b_bgate], writes=[b_ot[os_]])
        else:
            for hf in range(2):
                sl = slice(hf * 1024, (hf + 1) * 1024)
                S.op("act", lambda e, sl=sl: e.activation(out=qsw[0:64, :], in_=pst[64:128, sl], func=AF.Copy, scale=-1.0),
                     reads=[b_pst], writes=[b_qsw])
                S.op("act", lambda e, sl=sl: e.activation(out=qsw[64:128, :], in_=pst[0:64, sl], func=AF.Copy),
                     reads=[b_pst], writes=[b_qsw])
                S.op("act", lambda e, sl=sl: e.activation(out=t1[:], in_=pst[:, sl], func=AF.Copy),
                     reads=[b_pst], writes=[b_t1])
                S.op("dve", lambda e, sl=sl: e.tensor_tensor(out=t1[:], in0=t1[:], in1=cosT[:, sl], op=ALU.mult),
                     reads=[b_t1, b_rope], writes=[b_t1])
                S.op("pool", lambda e, sl=sl: e.tensor_tensor(out=t2[:], in0=qsw[:], in1=sinT[:, sl], op=ALU.mult),
                     reads=[b_qsw, b_rope], writes=[b_t2])
                S.op("dve", lambda e, sl=sl: e.tensor_tensor(out=ot[os_][:, sl], in0=t1[:], in1=t2[:], op=ALU.add),
                     reads=[b_t1, b_t2], writes=[b_ot[os_]])
        S.dma(lambda e: e.dma_start(out=dst[row0:row0 + 128, :], in_=ot[os_][:]), chan=b_ot[os_], reads=[b_ot[os_]])

    tiles = []
    for ct in range(128):
        c0 = ct * 128
        if ct < 16:
            info = ("copy", fT_d, ct * 128, 0)
        elif ct < 32:
            info = ("rope", qT_d, (ct - 16) * 128, 0)
        elif ct < 48:
            info = ("rope", kT_d, (ct - 32) * 128, 0)
        elif ct < 64:
            info = ("copy", vT_d, (ct - 48) * 128, 0)
        elif ct < 96:
            info = ("sig", gfT_d, (ct - 64) * 128, ct - 64)
        else:
            info = ("sig", gaT_d, (ct - 96) * 128, 32 + ct - 96)
        tiles.append((win_d, c0, False, info))
    if dbg and upto == 1.5:
        tiles = tiles[0:2] + tiles[16:18] + tiles[32:34] + tiles[48:50] + tiles[64:66] + tiles[96:98]
    gemm_fm(actT, b_act, 32, tiles, epi1, PK=8, NSTG=3)
    S.barrier()
    if upto <= 2:
        return finish(nc, S)
    A.cur = mark
    fTs = A.alloc("fTs", [128, 16, T], BF16)
    b_fTs = [S.buf("fTs%d" % i) for i in range(4)]
    load_actT(fTs, b_fTs, fT_d, 0, 16)
    ccsc = A.alloc("ccsc", [128, 2, 512], BF16)
    b_ccsc = S.buf("ccsc")
    S.dma(lambda e: e.dma_start(out=ccsc[:], in_=ccsc_d.rearrange("(j p) n -> p j n", p=128)), chan=b_ccsc, writes=[b_ccsc])
    gsb = [A.alloc("gsb", [128, 2, T], BF16) for _ in range(2)]
    b_gsb = [S.buf("gsb%d" % i) for i in range(2)]
    it = 0
    for tt in range(NT):
        gs = tt % 2
        for g in range(8):
            bi = it % 8
            it += 1
            for j in range(2):
                S.op("pe", lambda e, bi=bi, g=g, j=j, tt=tt: e.matmul(
                    bank(bi), lhsT=fTs[:, 2 * g + j, tt * 128:(tt + 1) * 128], rhs=ccsc[:, j, :],
                    start=(j == 0), stop=(j == 1)), reads=[b_fTs[tt // 4], b_ccsc], writes=[b_bank[bi]])
            if it % 2 == 0:
                S.op("act", lambda e, bi=bi, g=g, gs=gs: e.activation(
                    out=gsb[gs][:, :, g * 256:(g + 1) * 256], in_=bank(bi).rearrange("p (c n) -> p c n", c=2), func=AF.Copy),
                    reads=[b_bank[bi]], writes=[b_gsb[gs]])
            else:
                S.op("dve", lambda e, bi=bi, g=g, gs=gs: e.tensor_copy(
                    out=gsb[gs][:, :, g * 256:(g + 1) * 256], in_=bank(bi).rearrange("p (c n) -> p c n", c=2)),
                    reads=[b_bank[bi]], writes=[b_gsb[gs]])
        for c in range(2):
            S.dma(lambda e, gs=gs, tt=tt, c=c: e.dma_start(
                out=gw_d[c * 2048 + tt * 128:c * 2048 + (tt + 1) * 128, :], in_=gsb[gs][:, c, :]),
                chan=b_gsb[gs], reads=[b_gsb[gs]])
    S.barrier()

    A.cur = mark
    dfc = A.alloc("dfc", [128, 16, 1024], BF16)
    dfs = A.alloc("dfs", [128, 16, 1024], BF16)
    b_dfc = [S.buf("dfc%d" % i) for i in range(2)]
    b_dfs = [S.buf("dfs%d" % i) for i in range(2)]
    altt = A.alloc("altt", [128, 16], BF16)
    b_alt = S.buf("alt")
    S.dma(lambda e: e.dma_start(out=altt[:], in_=alt_d), chan=b_alt, writes=[b_alt])
    dv = dftw_d.rearrange("(k p) t -> p k t", p=128)
    for (dst, bd, k0) in ((dfc, b_dfc, 0), (dfs, b_dfs, 16)):
        for hf in range(2):
            S.dma(lambda e, dst=dst, hf=hf, k0=k0: e.dma_start(out=dst[:, :, hf * 512:(hf + 1) * 512],
                                                                in_=dv[:, k0:k0 + 16, hf * 512:(hf + 1) * 512]),
                  chan=bd[hf], writes=[bd[hf]])
    ot3 = [A.alloc("ot3", [128, T], BF16) for _ in range(2)]
    b_ot3 = [S.buf("ot3%d" % i) for i in range(2)]
    Bs = [A.alloc("Bs", [128, 1024], F32) for _ in range(2)]
    b_Bs = [S.buf("Bs%d" % i) for i in range(2)]
    NW3 = 3
    wb3 = [A.alloc("wb3", [128, 32, 128], BF16) for _ in range(NW3)]
    b_wb3 = [S.buf("wb3%d" % i) for i in range(NW3)]
    gwv = gw_d.rearrange("(k p) n -> p k n", p=128)

    def load3(i):
        ws = i % NW3
        S.dma(lambda e: e.dma_start(out=wb3[ws][:], in_=gwv[:, :, i * 128:(i + 1) * 128]),
              chan=b_wb3[ws], writes=[b_wb3[ws]])

    load3(0)
    load3(1)
    for i in range(16):
        if i + 2 < 16:
            load3(i + 2)
        pp = i % 2
        ws = i % NW3
        pst = ps[pp]
        for (mat, bm, k0, c0) in ((dfc, b_dfc, 0, 0), (dfs, b_dfs, 16, 1024)):
            for hf in range(2):
                for kc in range(16):
                    S.op("pe", lambda e, pst=pst, ws=ws, kc=kc, hf=hf, mat=mat, k0=k0, c0=c0: e.matmul(
                        pst[:, c0 + hf * 512:c0 + (hf + 1) * 512], lhsT=wb3[ws][:, k0 + kc, :],
                        rhs=mat[:, kc, hf * 512:(hf + 1) * 512], start=(kc == 0), stop=(kc == 15)),
                        reads=[b_wb3[ws], bm[hf]], writes=[b_ps[pp]])
        for kc in range(16):
            S.op("pe", lambda e, pst=pst, ws=ws, kc=kc: e.matmul(
                pst[:, 1024:1025], lhsT=wb3[ws][:, kc, :], rhs=altt[:, kc:kc + 1], start=False, stop=(kc == 15),
                skip_group_check=True), reads=[b_wb3[ws], b_alt], writes=[b_ps[pp]])
        os_ = i % 2
        S.op("act", lambda e, pst=pst, os_=os_: e.activation(out=Bs[os_][:], in_=pst[:, 1024:2048], func=AF.Copy),
             reads=[b_ps[pp]], writes=[b_Bs[os_]])
        S.op("dve", lambda e, pst=pst, os_=os_: e.tensor_tensor(out=ot3[os_][:, 0:1024], in0=pst[:, 0:1024], in1=Bs[os_][:],
                                                                 op=ALU.subtract),
             reads=[b_ps[pp], b_Bs[os_]], writes=[b_ot3[os_]])
        pstride = ot3[os_][:].ap[0][0]
        rev = bass.AP(ot3[os_], 2047, [[pstride, 128], [-1, 1023]])
        S.op("dve", lambda e, pst=pst, os_=os_, rev=rev: e.tensor_tensor(out=rev, in0=pst[:, 1:1024], in1=Bs[os_][:, 1:1024],
                                                                          op=ALU.add),
             reads=[b_ps[pp], b_Bs[os_]], writes=[b_ot3[os_]])
        S.op("dve", lambda e, pst=pst, os_=os_: e.tensor_copy(out=ot3[os_][:, 0:1], in_=pst[:, 0:1]),
             reads=[b_ps[pp]], writes=[b_ot3[os_]])
        S.op("dve", lambda e, os_=os_: e.tensor_copy(out=ot3[os_][:, 1024:1025], in_=Bs[os_][:, 0:1]),
             reads=[b_Bs[os_]], writes=[b_ot3[os_]])
        S.dma(lambda e, os_=os_, i=i: e.dma_start(out=yT_d[i * 128:(i + 1) * 128, :], in_=ot3[os_][:]),
              chan=b_ot3[os_], reads=[b_ot3[os_]])
    S.barrier()
    if upto <= 3:
        return finish(nc, S)

    A.cur = mark
    qs = [A.alloc("qs", [128, 2, T], BF16) for _ in range(2)]
    ks = [A.alloc("ks", [128, 2, T], BF16) for _ in range(2)]
    vs = [A.alloc("vs", [128, 2, T], BF16) for _ in range(2)]
    V1 = [A.alloc("V1", [128, 16, 257], BF16) for _ in range(2)]
    oTs = [A.alloc("oTs", [128, 2, T], BF16) for _ in range(2)]
    b_qs = [S.buf("qs%d" % i) for i in range(2)]
    b_ks = [S.buf("ks%d" % i) for i in range(2)]
    b_vs = [S.buf("vs%d" % i) for i in range(2)]
    b_V1 = [S.buf("V1%d" % i) for i in range(2)]
    b_oTs = [S.buf("oTs%d" % i) for i in range(2)]
    NPT = 3
    pt = [A.alloc("pt", [128, 512], BF16) for _ in range(NPT)]
    b_pt = [S.buf("pt%d" % i) for i in range(NPT)]
    o1 = [A.alloc("o1", [128, 256], F32) for _ in range(2)]
    of = [A.alloc("of", [128, 256], F32) for _ in range(2)]
    sq = [A.alloc("sq", [128, 256], F32) for _ in range(2)]
    onb = [A.alloc("onb", [128, 256], BF16) for _ in range(2)]
    stat = [A.alloc("stat", [128, 8], F32) for _ in range(2)]
    b_o1 = [S.buf("o1%d" % i) for i in range(2)]
    b_of = [S.buf("of%d" % i) for i in range(2)]
    b_sq = [S.buf("sq%d" % i) for i in range(2)]
    b_onb = [S.buf("onb%d" % i) for i in range(2)]
    b_stat = [S.buf("stat%d" % i) for i in range(2)]
    for par in range(2):
        S.op("pool", lambda e, par=par: e.memset(V1[par][:, :, 256:257], 1.0), writes=[b_V1[par]])
    SC = 1.0 / float(np.sqrt(128.0))

    def load_head(h):
        p_ = h % 2
        r0 = h * 256
        for (dst, b_dst, src) in ((qs, b_qs, qT_d), (ks, b_ks, kT_d), (vs, b_vs, vT_d)):
            S.dma(lambda e, dst=dst, src=src: e.dma_start(
                out=dst[p_][:], in_=src[r0:r0 + 256, :].rearrange("(c p) t -> p c t", p=128)),
                chan=b_dst[p_], writes=[b_dst[p_]])

    def build_v(h, k4):
        p_ = h % 2
        for kk in range(4):
            kt = k4 * 4 + kk
            for vc in range(2):
                S.op("pe", lambda e, kk=kk, vc=vc, kt=kt: e.transpose(
                    bank_bf(7)[:, (kk * 2 + vc) * 128:(kk * 2 + vc + 1) * 128],
                    vs[p_][:, vc, kt * 128:(kt + 1) * 128], ident[:]),
                    reads=[b_vs[p_], b_ident], writes=[b_bank[7]])
        S.op("dve", lambda e: e.tensor_copy(
            out=V1[p_][:, k4 * 4:(k4 + 1) * 4, 0:256], in_=bank_bf(7).rearrange("p (k n) -> p k n", k=4)),
            reads=[b_bank[7]], writes=[b_V1[p_]])

    def out_transposes(p_, qsub, q0):
        for vc in range(2):
            S.op("pe", lambda e, vc=vc: e.transpose(
                bank_bf(7)[:, vc * 128:(vc + 1) * 128], onb[qsub][:, vc * 128:(vc + 1) * 128], ident[:]),
                reads=[b_onb[qsub], b_ident], writes=[b_bank[7]])
        S.op("dve", lambda e: e.tensor_copy(
            out=oTs[p_][:, :, q0:q0 + 128], in_=bank_bf(7)[:, 0:256].rearrange("p (c t) -> p c t", c=2)),
            reads=[b_bank[7]], writes=[b_oTs[p_]])

    def store_head(h):
        p_ = h % 2
        r0 = h * 256
        S.dma(lambda e: e.dma_start(
            out=oT_d[r0:r0 + 256, :].rearrange("(c p) t -> p c t", p=128), in_=oTs[p_][:]),
            chan=b_oTs[p_], reads=[b_oTs[p_]])

    steps = [(h, qb, c, k2) for h in range(NH) for qb in range(8) for c in range(2) for k2 in range(8)]
    NS = len(steps)

    def score(n):
        h, qb, c, k2 = steps[n]
        p_ = h % 2
        sb = 4 + n % 3
        pi = n % NPT
        for j in range(2):
            kt = 2 * k2 + j
            S.op("pe", lambda e, j=j, kt=kt: e.matmul(bank(sb)[:, j * 256:(j + 1) * 256], lhsT=ks[p_][:, c, kt * 128:(kt + 1) * 128],
                                                      rhs=qs[p_][:, c, qb * 256:(qb + 1) * 256], start=True, stop=True),
                 reads=[b_ks[p_], b_qs[p_]], writes=[b_bank[sb]])
        S.op("act", lambda e: e.activation(out=pt[pi][:], in_=bank(sb), func=AF.Exp, scale=SC),
             reads=[b_bank[sb]], writes=[b_pt[pi]])

    load_head(0)
    for k4 in range(4):
        build_v(0, k4)
    pending = []
    seqc = [0]

    def defer(due, fn):
        seqc[0] += 1
        pending.append((due, seqc[0], fn))
        pending.sort(key=lambda t: (t[0], t[1]))

    score(0)
    score(1)
    for n in range(NS):
        h, qb, c, k2 = steps[n]
        p_ = h % 2
        first = (qb == 0 and c == 0 and k2 == 0)
        if first and h + 1 < NH:
            load_head(h + 1)
        while pending and pending[0][0] <= n:
            pending.pop(0)[2]()
        if n + 2 < NS:
            score(n + 2)
        pi = n % NPT
        for j in range(2):
            kt = 2 * k2 + j
            for qsub in range(2):
                ab = c * 2 + qsub
                S.op("pe", lambda e, ab=ab, pi=pi, qsub=qsub, kt=kt, p_=p_, j=j: e.matmul(
                    bank(ab)[:, 0:257], lhsT=pt[pi][:, j * 256 + qsub * 128:j * 256 + (qsub + 1) * 128], rhs=V1[p_][:, kt, :],
                    start=(kt == 0), stop=(kt == 15)),
                    reads=[b_pt[pi], b_V1[p_]], writes=[b_bank[ab]])
        if h + 1 < NH and c == 0 and k2 == 4 and 2 <= qb < 6:
            build_v(h + 1, qb - 2)
        kt = 15 if k2 == 7 else -1
        if kt == 15:
            for qsub in range(2):
                ab = c * 2 + qsub
                st_ = stat[qsub]
                bs = b_stat[qsub]
                S.op("dve", lambda e, ab=ab, st_=st_: e.reciprocal(out=st_[:, 0:1], in_=bank(ab)[:, 256:257]),
                     reads=[b_bank[ab]], writes=[bs])
                if c == 0:
                    S.op("dve", lambda e, ab=ab, st_=st_, qsub=qsub: e.tensor_scalar(
                        out=o1[qsub][:], in0=bank(ab)[:, 0:256], scalar1=st_[:, 0:1], scalar2=None, op0=ALU.mult),
                        reads=[b_bank[ab], bs], writes=[b_o1[qsub]])
                else:
                    S.op("dve", lambda e, st_=st_: e.tensor_tensor(out=st_[:, 1:2], in0=st_[:, 0:1], in1=NEGLAM, op=ALU.mult),
                         reads=[bs, b_small], writes=[bs])
                    S.op("dve", lambda e, ab=ab, st_=st_, qsub=qsub: e.scalar_tensor_tensor(
                        out=of[qsub][:], in0=bank(ab)[:, 0:256], scalar=st_[:, 1:2], in1=o1[qsub][:],
                        op0=ALU.mult, op1=ALU.add), reads=[b_bank[ab], bs, b_o1[qsub]], writes=[b_of[qsub]])
                    S.op("pool", lambda e, qsub=qsub: e.tensor_tensor(out=sq[qsub][:], in0=of[qsub][:], in1=of[qsub][:], op=ALU.mult),
                         reads=[b_of[qsub]], writes=[b_sq[qsub]])
                    S.op("dve", lambda e, st_=st_, qsub=qsub: e.tensor_reduce(out=st_[:, 2:3], in_=sq[qsub][:], axis=AX.X, op=ALU.add),
                         reads=[b_sq[qsub]], writes=[bs])
                    S.op("dve", lambda e, st_=st_: e.tensor_scalar(out=st_[:, 3:4], in0=st_[:, 2:3], scalar1=1.0 / 256.0,
                                                                   scalar2=EPS, op0=ALU.mult, op1=ALU.add),
                         reads=[bs], writes=[bs])
                    q0 = qb * 256 + qsub * 128

                    def stage_act(st_=st_, bs=bs):
                        S.op("act", lambda e: e.activation(out=st_[:, 4:5], in_=st_[:, 3:4], func=AF.Ln),
                             reads=[bs], writes=[bs])
                        S.op("act", lambda e: e.activation(out=st_[:, 5:6], in_=st_[:, 4:5], func=AF.Exp, scale=-0.5),
                             reads=[bs], writes=[bs])

                    def stage_onb(st_=st_, bs=bs, qsub=qsub):
                        S.op("dve", lambda e: e.scalar_tensor_tensor(
                            out=onb[qsub][:], in0=of[qsub][:], scalar=st_[:, 5:6], in1=g08[:], op0=ALU.mult, op1=ALU.mult),
                            reads=[b_of[qsub], bs, b_g08], writes=[b_onb[qsub]])

                    defer(n + 4 + qsub, stage_act)
                    defer(n + 6 + qsub, stage_onb)
                    defer(n + 9 + qsub, (lambda p_=p_, qsub=qsub, q0=q0: out_transposes(p_, qsub, q0)))
                    if qb == 7 and qsub == 1:
                        defer(n + 12, (lambda h=h: store_head(h)))
    while pending:
        pending.pop(0)[2]()
    S.barrier()
    if upto <= 4:
        return finish(nc, S)

    A.cur = mark
    yTs = A.alloc("yTs", [128, 16, T], BF16)
    b_yTs = [S.buf("yTs%d" % i) for i in range(4)]
    gt = [A.alloc("gt", [128, T], BF16) for _ in range(2)]
    tm = [A.alloc("tm", [128, T], F32) for _ in range(2)]
    b_gt = [S.buf("gt%d" % i) for i in range(2)]
    b_tm = [S.buf("tm%d" % i) for i in range(2)]

    def pre5a(i):
        sl = i % 2
        S.dma(lambda e: e.dma_start(out=gt[sl][:], in_=gfT_d[i * 128:(i + 1) * 128, :]), chan=b_gt[sl], writes=[b_gt[sl]])

    pre5a(0)

    def epi5a(i, info, pst, b_pst):
        sl = i % 2
        if i + 1 < 32:
            pre5a(i + 1)
        S.op("dve", lambda e: e.tensor_tensor(out=tm[sl][:], in0=pst[:], in1=gt[sl][:], op=ALU.mult),
             reads=[b_pst, b_gt[sl]], writes=[b_tm[sl]])
        S.dma(lambda e: e.dma_start(out=tmpT_d[i * 128:(i + 1) * 128, :], in_=tm[sl][:]), chan=b_tm[sl], reads=[b_tm[sl]])

    gemm_fm(yTs, b_yTs, 16, [(wf_d, ct * 128, False, None) for ct in range(32)], epi5a,
            act_loader=lambda: load_actT(yTs, b_yTs, yT_d, 0, 16))
    S.barrier()

    A.cur = mark
    oTa = A.alloc("oTa", [128, 16, T], BF16)
    b_oTa = [S.buf("oTa%d" % i) for i in range(4)]
    gt = [A.alloc("gt", [128, T], BF16) for _ in range(2)]
    tm = [A.alloc("tm", [128, T], F32) for _ in range(2)]
    m1 = A.alloc("m1", [128, T], F32)
    mo = [A.alloc("mo", [128, T], BF16) for _ in range(2)]
    b_gt = [S.buf("gt%d" % i) for i in range(2)]
    b_tm = [S.buf("tm%d" % i) for i in range(2)]
    b_m1 = S.buf("m1")
    b_mo = [S.buf("mo%d" % i) for i in range(2)]

    def pre5b(i):
        sl = i % 2
        S.dma(lambda e: e.dma_start(out=gt[sl][:], in_=gaT_d[i * 128:(i + 1) * 128, :]), chan=b_gt[sl], writes=[b_gt[sl]])
        S.dma(lambda e: e.dma_start(out=tm[sl][:], in_=tmpT_d[i * 128:(i + 1) * 128, :]), chan=b_tm[sl], writes=[b_tm[sl]])

    pre5b(0)

    def epi5b(i, info, pst, b_pst):
        sl = i % 2
        if i + 1 < 32:
            pre5b(i + 1)
        S.op("dve", lambda e: e.tensor_tensor(out=m1[:], in0=pst[:], in1=gt[sl][:], op=ALU.mult),
             reads=[b_pst, b_gt[sl]], writes=[b_m1])
        S.op("dve", lambda e: e.tensor_tensor(out=mo[sl][:], in0=m1[:], in1=tm[sl][:], op=ALU.add),
             reads=[b_m1, b_tm[sl]], writes=[b_mo[sl]])
        S.dma(lambda e: e.dma_start(out=mT_d[i * 128:(i + 1) * 128, :], in_=mo[sl][:]), chan=b_mo[sl], reads=[b_mo[sl]])

    gemm_fm(oTa, b_oTa, 16, [(wa_d, ct * 128, False, None) for ct in range(32)], epi5b,
            act_loader=lambda: load_actT(oTa, b_oTa, oT_d, 0, 16))
    S.barrier()

    A.cur = mark
    gemm_tm(mT_d, [16, 16], wo_d, x_d, h1_d)
    S.barrier()
    if upto <= 6:
        return finish(nc, S)

    A.cur = mark
    act2 = A.alloc("act2", [128, 32, T], BF16)
    b_act2 = S.buf("act2")
    m7 = A.cur
    norm_to_actT(h1_d, g2_d, act2, b_act2)
    S.barrier()
    A.cur = m7
    sgt = A.alloc("sgt", [128, T], F32)
    b_sgt = S.buf("sgt")
    ot8 = [A.alloc("ot8", [128, T], BF16) for _ in range(2)]
    b_ot8 = [S.buf("ot8%d" % i) for i in range(2)]

    def epi8(i, info, pst, b_pst):
        kind, j = info
        if kind == "g":
            S.op("act", lambda e: e.activation(out=sgt[:], in_=pst[:], func=AF.Silu), reads=[b_pst], writes=[b_sgt])
        else:
            os_ = j % 2
            S.op("dve", lambda e: e.tensor_tensor(out=ot8[os_][:], in0=pst[:], in1=sgt[:], op=ALU.mult),
                 reads=[b_pst, b_sgt], writes=[b_ot8[os_]])
            S.dma(lambda e: e.dma_start(out=hidT_d[j * 128:(j + 1) * 128, :], in_=ot8[os_][:]),
                  chan=b_ot8[os_], reads=[b_ot8[os_]])

    tiles8 = []
    for j in range(HC):
        tiles8.append((wg_d, j * 128, False, ("g", j)))
        tiles8.append((wu_d, j * 128, False, ("u", j)))
    gemm_fm(act2, b_act2, 32, tiles8, epi8, PK=8, NSTG=3)
    S.barrier()

    A.cur = mark
    gemm_tm(hidT_d, [22, 22, 21, 21], wd_d, h1_d, h2_d)
    S.barrier()

    A.cur = mark
    gbc = A.alloc("gbc", [128, D], F32)
    b_gbc = S.buf("gbc")
    S.dma(lambda e: e.dma_start(out=gbc[:], in_=g3_d.partition_broadcast(128)), chan=b_gbc, writes=[b_gbc])
    xt = [A.alloc("xt", [128, D], F32) for _ in range(2)]
    yo = [A.alloc("yo", [128, D], F32) for _ in range(2)]
    junk = A.alloc("junk", [128, D], BF16)
    st = A.alloc("st", [128, NT, 2], F32)
    b_xt = [S.buf("xt%d" % i) for i in range(2)]
    b_yo = [S.buf("yo%d" % i) for i in range(2)]
    b_junk, b_st = S.buf("junk"), S.buf("st")
    S.op("pool", lambda e: e.memset(st[:], 0.0), writes=[b_st])
    for tt in range(NT):
        sl = tt % 2
        S.dma(lambda e, sl=sl, tt=tt: e.dma_start(out=xt[sl][:], in_=h2_d[tt * 128:(tt + 1) * 128, :]),
              chan=b_xt[sl], writes=[b_xt[sl]])
        S.op("act", lambda e, sl=sl, tt=tt: e.activation(out=junk[:], in_=xt[sl][:], func=AF.Square, accum_out=st[:, tt, 0:1]),
             reads=[b_xt[sl], b_st], writes=[b_junk, b_st])
        S.op("dve", lambda e, tt=tt: e.tensor_scalar(out=st[:, tt, 1:2], in0=st[:, tt, 0:1], scalar1=1.0 / D, scalar2=EPS,
                                                     op0=ALU.mult, op1=ALU.add), reads=[b_st], writes=[b_st])
        S.op("act", lambda e, tt=tt: e.activation(out=st[:, tt, 1:2], in_=st[:, tt, 1:2], func=AF.Ln), reads=[b_st], writes=[b_st])
        S.op("act", lambda e, tt=tt: e.activation(out=st[:, tt, 1:2], in_=st[:, tt, 1:2], func=AF.Exp, scale=-0.5),
             reads=[b_st], writes=[b_st])
        S.op("dve", lambda e, sl=sl, tt=tt: e.scalar_tensor_tensor(out=yo[sl][:], in0=xt[sl][:], scalar=st[:, tt, 1:2],
                                                                   in1=gbc[:], op0=ALU.mult, op1=ALU.mult),
             reads=[b_xt[sl], b_st, b_gbc], writes=[b_yo[sl]])
        S.dma(lambda e, sl=sl, tt=tt: e.dma_start(out=out_d[tt * 128:(tt + 1) * 128, :], in_=yo[sl][:]),
              chan=b_yo[sl], reads=[b_yo[sl]])
    S.barrier()
    return finish(nc, S)


def finish(nc, S):
    S.finalize()
    return nc


_CACHE = {}


def kernel(**inputs):
    f32 = np.float32
    x = np.asarray(inputs["x"], dtype=f32)
    pos = np.asarray(inputs["positions"], dtype=np.int32)
    if "consts" not in _CACHE:
        _CACHE["consts"] = _consts()
    nc = build()
    dftw, ccsc, ident, invf, alt = _CACHE["consts"]
    g = lambda k: np.ascontiguousarray(np.asarray(inputs[k], dtype=f32)[0])
    bg = np.ascontiguousarray(np.asarray(inputs["b_gate"], dtype=f32)[0].reshape(2, 32, 128).transpose(2, 0, 1).reshape(128, 64))
    shared = {"norm_mix_g": g("norm_mix_g"), "w_in": g("w_in"), "bgate": bg,
              "lambda_q1": g("lambda_q1"), "lambda_k1": g("lambda_k1"), "lambda_q2": g("lambda_q2"),
              "lambda_k2": g("lambda_k2"), "subln_g": g("subln_g"), "w_fourier_out": g("w_fourier_out"),
              "w_attn_out": g("w_attn_out"), "w_out": g("w_out"), "norm_ffn_g": g("norm_ffn_g"),
              "w_ffn_gate": g("w_ffn_gate"), "w_ffn_up": g("w_ffn_up"), "w_ffn_down": g("w_ffn_down"),
              "norm_final_g": np.ascontiguousarray(np.asarray(inputs["norm_final_g"], dtype=f32)),
              "dftw": dftw, "ccsc": ccsc, "ident": ident, "invf": invf, "alt": alt}
    in_maps = []
    for b in range(8):
        m = dict(shared)
        m["x"] = np.ascontiguousarray(x[b])
        m["positions"] = np.ascontiguousarray(pos[b])
        in_maps.append(m)
    res = run_bass_kernel_spmd(nc, in_maps, core_ids=list(range(8)))
    return np.stack([np.asarray(r["out"], dtype=f32) for r in res.results], axis=0)
```

```python
import ml_dtypes
import time
import numpy as np
import concourse.bass as bass
import concourse.mybir as mybir
from concourse.bass_utils import run_bass_kernel_spmd

F32 = mybir.dt.float32
BF16 = mybir.dt.bfloat16
I32 = mybir.dt.int32
AF = mybir.ActivationFunctionType
ALU = mybir.AluOpType
AX = mybir.AxisListType

COMPUTE = ("pe", "act", "dve", "pool")


class Buf:
    __slots__ = ("name", "lw", "rd", "sem")

    def __init__(self, name):
        self.name = name
        self.lw = None
        self.rd = {}
        self.sem = None


class Ins:
    __slots__ = ("stream", "src", "fn", "deps", "sig", "sigval", "dma")

    def __init__(self, stream, src, fn, dma):
        self.stream = stream
        self.src = src
        self.fn = fn
        self.deps = []
        self.sig = False
        self.sigval = 0
        self.dma = dma


class Sched:
    def __init__(self, nc):
        self.nc = nc
        self.streams = {"pe": [], "act": [], "dve": [], "pool": [], "sp": []}
        self.all = []
        self.sems = {e: nc.alloc_semaphore("sem_" + e) for e in COMPUTE}
        self.dma_sem_pool = []
        self.ndma = 0
        self.last = {}
        self.bufs_with_sem = []

    def buf(self, name):
        return Buf(name)

    def _dma_sem(self, b):
        if b.sem is None:
            if self.dma_sem_pool:
                b.sem = self.dma_sem_pool.pop()
            else:
                self.ndma += 1
                b.sem = self.nc.alloc_semaphore("dsem%d" % self.ndma)
            self.bufs_with_sem.append(b)
        return b.sem

    def _track(self, ins, reads, writes):
        deps = []
        for b in reads:
            if b.lw is not None:
                deps.append((b.lw, "raw"))
        for b in writes:
            if b.lw is not None:
                deps.append((b.lw, "waw"))
            for r in b.rd.values():
                deps.append((r, "war"))
        for d, kind in deps:
            if d is ins:
                continue
            if (not ins.dma) and (not d.dma) and d.src == ins.src:
                if ins.src == "pe" or kind != "raw":
                    continue
            d.sig = True
            ins.deps.append(d)
        for b in reads:
            b.rd[ins.src] = ins
        for b in writes:
            b.lw = ins
            b.rd = {}

    def op(self, eng, fn, reads=(), writes=()):
        ins = Ins(eng, eng, fn, False)
        self._track(ins, reads, writes)
        self.streams[eng].append(ins)
        self.all.append(ins)
        self.last[eng] = ins
        return ins

    def dma(self, fn, chan, reads=(), writes=(), q="sp"):
        sem = self._dma_sem(chan)
        ins = Ins(q, sem, fn, True)
        ins.sig = True
        self._track(ins, reads, writes)
        self.streams[q].append(ins)
        self.all.append(ins)
        self.last[sem] = ins
        return ins

    def barrier(self):
        lasts = list(self.last.values())
        for d in lasts:
            d.sig = True
        for s in self.streams:
            ins = Ins(s, None, None, False)
            ins.deps = list(lasts)
            self.streams[s].append(ins)
            self.all.append(ins)
        for b in self.bufs_with_sem:
            self.dma_sem_pool.append(b.sem)
            b.sem = None
        self.bufs_with_sem = []

    def finalize(self):
        nc = self.nc
        cnt = {}
        for ins in self.all:
            if ins.src is None:
                continue
            if ins.sig:
                cnt[ins.src] = cnt.get(ins.src, 0) + (16 if ins.dma else 1)
                ins.sigval = cnt[ins.src]
        sems = self.sems
        streams = self.streams

        def replay(sname):
            def run(e):
                waited = {}
                for ins in streams[sname]:
                    need = {}
                    for d in ins.deps:
                        k = d.src
                        if d.sigval > need.get(k, 0):
                            need[k] = d.sigval
                    for k, v in need.items():
                        if v > waited.get(k, 0):
                            waited[k] = v
                            e.wait_ge(sems[k] if isinstance(k, str) else k, v)
                    if ins.fn is None:
                        continue
                    bi = ins.fn(e)
                    if ins.sig:
                        if ins.dma:
                            bi.then_inc(ins.src, 16)
                        else:
                            bi.then_inc(sems[ins.src], 1)
            return run

        with nc.Block() as block:
            block.tensor(replay("pe"))
            block.scalar(replay("act"))
            block.vector(replay("dve"))
            block.gpsimd(replay("pool"))
            block.sync(replay("sp"))


class SbufAlloc:
    def __init__(self, nc, base=None, top=None):
        self.nc = nc
        self.base = ((nc.sbuf_base + 63) // 64) * 64 if base is None else base
        self.top = nc.sbuf_top if top is None else top
        self.cur = self.base
        self.n = 0

    def reset(self):
        self.cur = self.base

    def alloc(self, name, shape, dtype):
        esz = 4 if dtype in (F32, I32) else 2
        per = esz
        for s in shape[1:]:
            per *= s
        off = self.cur
        self.cur = ((off + per + 63) // 64) * 64
        assert self.cur <= self.top, ("SBUF overflow", name, self.cur, self.top)
        self.n += 1
        return self.nc.alloc_sbuf_tensor_at("%s_%d" % (name, self.n), list(shape), dtype, offset=off)


D = 4096
T = 2048
NT = T // 128
FW = 2048
AW = 2048
HID = 11008
HC = HID // 128
NH = 8
EPS = 1e-6
LAM_INIT = 0.2


def _consts():
    s = np.arange(2048, dtype=np.float64)
    sp = np.arange(1024, dtype=np.float64)
    ang = 2.0 * np.pi * np.outer(s, sp) / 2048.0
    dftw = np.concatenate([np.cos(ang), np.sin(ang)], 0) / np.sqrt(2048.0)
    alt = np.tile(((-1.0) ** np.arange(128)).reshape(128, 1), (1, 16)) / np.sqrt(2048.0)
    c = np.arange(256, dtype=np.float64)
    angc = 2.0 * np.pi * np.outer(c, c) / 256.0
    ccsc = np.concatenate([np.cos(angc), np.sin(angc)], 1) / 16.0
    ident = np.eye(128)
    j = np.arange(0, 128, 2, dtype=np.float32) / np.float32(128)
    inv = (np.float32(10000.0) ** (-j)).astype(np.float32)
    invf = np.concatenate([inv, inv]).reshape(128, 1).astype(np.float32)
    return (dftw.astype(ml_dtypes.bfloat16), ccsc.astype(ml_dtypes.bfloat16),
            ident.astype(ml_dtypes.bfloat16), invf, alt.astype(ml_dtypes.bfloat16))


class K:
    pass


def build(upto=99, dbg=False):
    nc = bass.Bass("TRN2", target_bir_lowering=False)
    S = Sched(nc)

    def din(name, shape, dt):
        return nc.dram_tensor(name, list(shape), dt, kind="ExternalInput").ap()

    def dscr(name, shape, dt):
        return nc.dram_tensor(name, list(shape), dt, kind=("ExternalOutput" if dbg else "Internal")).ap()

    x_d = din("x", [T, D], F32)
    pos_d = din("positions", [T], I32)
    g1_d = din("norm_mix_g", [D], F32)
    win_d = din("w_in", [D, 16384], F32)
    bg_d = din("bgate", [128, 64], F32)
    lq1_d = din("lambda_q1", [128], F32)
    lk1_d = din("lambda_k1", [128], F32)
    lq2_d = din("lambda_q2", [128], F32)
    lk2_d = din("lambda_k2", [128], F32)
    sg_d = din("subln_g", [256], F32)
    wf_d = din("w_fourier_out", [FW, D], F32)
    wa_d = din("w_attn_out", [AW, D], F32)
    wo_d = din("w_out", [D, D], F32)
    g2_d = din("norm_ffn_g", [D], F32)
    wg_d = din("w_ffn_gate", [D, HID], F32)
    wu_d = din("w_ffn_up", [D, HID], F32)
    wd_d = din("w_ffn_down", [HID, D], F32)
    g3_d = din("norm_final_g", [D], F32)
    dftw_d = din("dftw", [4096, 1024], BF16)
    alt_d = din("alt", [128, 16], BF16)
    ccsc_d = din("ccsc", [256, 512], BF16)
    ident_d = din("ident", [128, 128], BF16)
    invf_d = din("invf", [128, 1], F32)
    out_d = nc.dram_tensor("out", [T, D], F32, kind="ExternalOutput").ap()

    fT_d = dscr("fT", [FW, T], BF16)
    qT_d = dscr("qT", [AW, T], BF16)
    kT_d = dscr("kT", [AW, T], BF16)
    vT_d = dscr("vT", [AW, T], BF16)
    gfT_d = dscr("gfT", [D, T], BF16)
    gaT_d = dscr("gaT", [D, T], BF16)
    gw_d = dscr("gw", [4096, 2048], BF16)
    yT_d = dscr("yT", [FW, T], BF16)
    oT_d = dscr("oT", [AW, T], BF16)
    tmpT_d = dscr("tmpT", [D, T], F32)
    mT_d = dscr("mT", [D, T], BF16)
    h1_d = dscr("h1", [T, D], F32)
    hidT_d = dscr("hidT", [HID, T], BF16)
    h2_d = dscr("h2", [T, D], F32)

    A = SbufAlloc(nc)
    ps = [nc.alloc_psum_tensor("psA", [128, 2048], F32), nc.alloc_psum_tensor("psB", [128, 2048], F32)]
    psb = [p.bitcast(BF16) for p in ps]
    b_ps = [S.buf("psA"), S.buf("psB")]
    b_bank = [S.buf("bank%d" % i) for i in range(8)]

    def bank(i):
        return ps[i // 4][:, (i % 4) * 512:(i % 4 + 1) * 512]

    def bank_bf(i):
        return psb[i // 4][:, (i % 4) * 1024:(i % 4 + 1) * 1024]

    ident = A.alloc("ident", [128, 128], BF16)
    b_ident = S.buf("ident")
    S.dma(lambda e: e.dma_start(out=ident[:], in_=ident_d), chan=b_ident, writes=[b_ident])
    small = A.alloc("small", [128, 64], F32)
    b_small = S.buf("small")
    bgate = A.alloc("bgate", [128, 64], F32)
    b_bgate = S.buf("bgate")
    S.dma(lambda e: e.dma_start(out=bgate[:], in_=bg_d), chan=b_bgate, writes=[b_bgate])
    g08 = A.alloc("g08", [128, 256], F32)
    b_g08 = S.buf("g08")
    S.dma(lambda e: e.dma_start(out=g08[:], in_=sg_d.partition_broadcast(128)), chan=b_g08, writes=[b_g08])
    S.op("dve", lambda e: e.tensor_scalar(out=g08[:], in0=g08[:], scalar1=1.0 - LAM_INIT, scalar2=None, op0=ALU.mult),
         reads=[b_g08], writes=[b_g08])
    lam4 = A.alloc("lam4", [128, 4, 128], F32)
    b_lam4 = S.buf("lam4")
    for i, dd in enumerate([lq1_d, lk1_d, lq2_d, lk2_d]):
        S.dma(lambda e, i=i, dd=dd: e.dma_start(out=lam4[:, i, :], in_=dd.partition_broadcast(128)),
              chan=b_lam4, writes=[b_lam4])
    lamp = A.alloc("lamp", [128, 2, 128], F32)
    b_lamp = S.buf("lamp")
    S.op("dve", lambda e: e.tensor_tensor(out=lamp[:, 0, :], in0=lam4[:, 0, :], in1=lam4[:, 1, :], op=ALU.mult),
         reads=[b_lam4], writes=[b_lamp])
    S.op("dve", lambda e: e.tensor_tensor(out=lamp[:, 1, :], in0=lam4[:, 2, :], in1=lam4[:, 3, :], op=ALU.mult),
         reads=[b_lam4, b_lamp], writes=[b_lamp])
    S.op("dve", lambda e: e.tensor_reduce(out=small[:, 0:2], in_=lamp[:], axis=AX.X, op=ALU.add),
         reads=[b_lamp], writes=[b_small])
    S.op("act", lambda e: e.activation(out=small[:, 2:4], in_=small[:, 0:2], func=AF.Exp),
         reads=[b_small], writes=[b_small])
    S.op("dve", lambda e: e.tensor_tensor(out=small[:, 4:5], in0=small[:, 3:4], in1=small[:, 2:3], op=ALU.subtract),
         reads=[b_small], writes=[b_small])
    S.op("dve", lambda e: e.tensor_scalar(out=small[:, 5:6], in0=small[:, 4:5], scalar1=-LAM_INIT, scalar2=None, op0=ALU.add),
         reads=[b_small], writes=[b_small])
    NEGLAM = small[:, 5:6]
    A.base = A.cur

    K.nc, K.S, K.A = nc, S, A

    def norm_to_actT(src_d, g_d, actT, b_act):
        gbc = A.alloc("gbc", [128, D], F32)
        b_gbc = S.buf("gbc")
        S.dma(lambda e: e.dma_start(out=gbc[:], in_=g_d.partition_broadcast(128)), chan=b_gbc, writes=[b_gbc])
        xt = [A.alloc("xt", [128, D], F32) for _ in range(2)]
        b_xt = [S.buf("xt%d" % i) for i in range(2)]
        junk = A.alloc("junk", [128, D], BF16)
        b_junk = S.buf("junk")
        ub = [A.alloc("ub", [128, D], BF16) for _ in range(2)]
        b_ub = [S.buf("ub%d" % i) for i in range(2)]
        st = A.alloc("st", [128, NT, 2], F32)
        b_stl = [S.buf("st%d" % i) for i in range(NT)]
        S.op("pool", lambda e: e.memset(st[:], 0.0), writes=b_stl)
        for tt in range(NT):
            sl = tt % 2
            b_st = b_stl[tt]
            S.dma(lambda e, sl=sl, tt=tt: e.dma_start(out=xt[sl][:], in_=src_d[tt * 128:(tt + 1) * 128, :]),
                  chan=b_xt[sl], writes=[b_xt[sl]])
            S.op("act", lambda e, sl=sl, tt=tt: e.activation(out=junk[:], in_=xt[sl][:], func=AF.Square,
                                                             accum_out=st[:, tt, 0:1]),
                 reads=[b_xt[sl], b_st], writes=[b_junk, b_st])
            S.op("pool", lambda e, tt=tt: e.tensor_scalar(out=st[:, tt, 1:2], in0=st[:, tt, 0:1], scalar1=1.0 / D,
                                                          scalar2=EPS, op0=ALU.mult, op1=ALU.add),
                 reads=[b_st], writes=[b_st])
            S.op("act", lambda e, tt=tt: e.activation(out=st[:, tt, 1:2], in_=st[:, tt, 1:2], func=AF.Ln),
                 reads=[b_st], writes=[b_st])
            S.op("act", lambda e, tt=tt: e.activation(out=st[:, tt, 1:2], in_=st[:, tt, 1:2], func=AF.Exp, scale=-0.5),
                 reads=[b_st], writes=[b_st])
            S.op("dve", lambda e, sl=sl, tt=tt: e.scalar_tensor_tensor(out=ub[sl][:], in0=xt[sl][:], scalar=st[:, tt, 1:2],
                                                                       in1=gbc[:], op0=ALU.mult, op1=ALU.mult),
                 reads=[b_xt[sl], b_st, b_gbc], writes=[b_ub[sl]])
            for c8 in range(4):
                bi = (tt * 4 + c8) % 8
                for j in range(8):
                    kc = c8 * 8 + j
                    S.op("pe", lambda e, bi=bi, j=j, kc=kc, sl=sl: e.transpose(
                        bank_bf(bi)[:, j * 128:(j + 1) * 128], ub[sl][:, kc * 128:(kc + 1) * 128], ident[:]),
                        reads=[b_ub[sl], b_ident], writes=[b_bank[bi]])
                eng = "dve"
                if eng == "act":
                    S.op("act", lambda e, bi=bi, c8=c8, tt=tt: e.activation(
                        out=actT[:, c8 * 8:(c8 + 1) * 8, tt * 128:(tt + 1) * 128],
                        in_=bank_bf(bi).rearrange("p (k t) -> p k t", k=8), func=AF.Copy),
                        reads=[b_bank[bi]], writes=[b_act])
                else:
                    S.op("dve", lambda e, bi=bi, c8=c8, tt=tt: e.tensor_copy(
                        out=actT[:, c8 * 8:(c8 + 1) * 8, tt * 128:(tt + 1) * 128],
                        in_=bank_bf(bi).rearrange("p (k t) -> p k t", k=8)),
                        reads=[b_bank[bi]], writes=[b_act])

    def load_actT(actT, b_act, src_d, kc0, KC):
        v = src_d.rearrange("(k p) t -> p k t", p=128)
        for tb in range(4):
            bb = b_act[tb] if isinstance(b_act, list) else b_act
            S.dma(lambda e, tb=tb: e.dma_start(out=actT[:, 0:KC, tb * 512:(tb + 1) * 512],
                                               in_=v[:, kc0:kc0 + KC, tb * 512:(tb + 1) * 512]),
                  chan=bb, writes=[bb])

    def gemm_fm(actT, b_act, KC, tiles, epi, kc0=0, PK=16, NW=3, NSTG=2, DIST=2, act_loader=None):
        stg = [A.alloc("stg", [128, PK, 128], F32) for _ in range(NSTG)]
        b_stg = [S.buf("stg%d" % i) for i in range(NSTG)]
        wb = [A.alloc("wb", [128, KC, 128], BF16) for _ in range(NW)]
        b_wb = [S.buf("wb%d" % i) for i in range(NW)]
        state = {"pc": 0}

        def load(i):
            w_ap, c0, isbf, _ = tiles[i]
            ws = i % NW
            wv = w_ap.rearrange("(k p) n -> p k n", p=128)
            if isbf:
                S.dma(lambda e: e.dma_start(out=wb[ws][:], in_=wv[:, kc0:kc0 + KC, c0:c0 + 128]),
                      chan=b_wb[ws], writes=[b_wb[ws]])
                return
            for p0 in range(0, KC, PK):
                n = min(PK, KC - p0)
                sl = state["pc"] % NSTG
                state["pc"] += 1
                S.dma(lambda e, sl=sl, p0=p0, n=n: e.dma_start(
                    out=stg[sl][:, 0:n, :], in_=wv[:, kc0 + p0:kc0 + p0 + n, c0:c0 + 128]),
                    chan=b_stg[sl], writes=[b_stg[sl]])
                S.op("pool", lambda e, sl=sl, p0=p0, n=n: e.tensor_copy(out=wb[ws][:, p0:p0 + n, :], in_=stg[sl][:, 0:n, :]),
                     reads=[b_stg[sl]], writes=[b_wb[ws]])

        n = len(tiles)
        for i in range(min(DIST, n)):
            load(i)
        if act_loader is not None:
            act_loader()
        for i in range(n):
            if i + DIST < n:
                load(i + DIST)
            pp = i % 2
            ws = i % NW
            for tb in range(4):
                for kc in range(KC):
                    S.op("pe", lambda e, pp=pp, tb=tb, ws=ws, kc=kc: e.matmul(
                        ps[pp][:, tb * 512:(tb + 1) * 512], lhsT=wb[ws][:, kc, :],
                        rhs=actT[:, kc, tb * 512:(tb + 1) * 512], start=(kc == 0), stop=(kc == KC - 1)),
                        reads=[b_wb[ws], (b_act[tb] if isinstance(b_act, list) else b_act)], writes=[b_ps[pp]])
            epi(i, tiles[i][3], ps[pp], b_ps[pp])

    def gemm_tm(src_d, KCs, w_ap, resid0_d, out_d_, NW=2, NSTG=3, PK=4):
        KCm = max(KCs)
        NSL = 8
        aT = A.alloc("aT", [128, KCm, T], BF16)
        b_aT = [S.buf("aT%d" % i) for i in range(NSL)]
        stg = [A.alloc("stg", [128, PK, 512], F32) for _ in range(NSTG)]
        b_stg = [S.buf("stg%d" % i) for i in range(NSTG)]
        wb = [A.alloc("wb", [128, KCm, 512], BF16) for _ in range(NW)]
        b_wb = [S.buf("wb%d" % i) for i in range(NW)]
        NR = 8
        rt = [A.alloc("rt", [128, 512], F32) for _ in range(NR)]
        b_rt = [S.buf("rt%d" % i) for i in range(NR)]
        dt = {}
        for tt in range(NT):
            for cb in range(8):
                dt[(tt, cb)] = S.buf("dt")
        wv = w_ap.rearrange("(k p) n -> p k n", p=128)
        sv = src_d.rearrange("(k p) t -> p k t", p=128)
        state = {"pc": 0}
        kc0s = [sum(KCs[:i]) for i in range(len(KCs))]

        def load_slice(pi_, sl_):
            KC, kc0 = KCs[pi_], kc0s[pi_]
            S.dma(lambda e: e.dma_start(out=aT[:, 0:KC, sl_ * 256:(sl_ + 1) * 256],
                                        in_=sv[:, kc0:kc0 + KC, sl_ * 256:(sl_ + 1) * 256]),
                  chan=b_aT[sl_], writes=[b_aT[sl_]])

        def pieces_of(pi_):
            KC = KCs[pi_]
            return [(p0, min(PK, KC - p0)) for p0 in range(0, KC, PK)]

        def load_piece(gb, k):
            pi_, cb = gb // 8, gb % 8
            ws = gb % NW
            p0, n = pieces_of(pi_)[k]
            kc0 = kc0s[pi_]
            sl = state["pc"] % NSTG
            state["pc"] += 1
            S.dma(lambda e: e.dma_start(
                out=stg[sl][:, 0:n, :], in_=wv[:, kc0 + p0:kc0 + p0 + n, cb * 512:(cb + 1) * 512]),
                chan=b_stg[sl], writes=[b_stg[sl]])
            S.op("pool", lambda e: e.tensor_copy(out=wb[ws][:, p0:p0 + n, :], in_=stg[sl][:, 0:n, :]),
                 reads=[b_stg[sl]], writes=[b_wb[ws]])

        NP = len(KCs)
        for k in range(len(pieces_of(0))):
            load_piece(0, k)
        for sl_ in range(NSL):
            load_slice(0, sl_)
        LA = 3
        NIT = NP * 8 * NT

        def load_resid(j):
            gb_, tt_ = j // NT, j % NT
            pj, cbj = gb_ // 8, gb_ % 8
            rsj = j % NR
            src = resid0_d if pj == 0 else out_d_
            rd = [dt[(tt_, cbj)]] if pj > 0 else []
            S.dma(lambda e: e.dma_start(
                out=rt[rsj][:], in_=src[tt_ * 128:(tt_ + 1) * 128, cbj * 512:(cbj + 1) * 512]),
                chan=b_rt[rsj], reads=rd, writes=[b_rt[rsj]])

        for j in range(LA):
            load_resid(j)
        it = 0
        for gb in range(NP * 8):
            pi_, cb = gb // 8, gb % 8
            KC = KCs[pi_]
            ws = gb % NW
            for tt in range(NT):
                if gb + 1 < NP * 8 and tt % 2 == 0 and tt // 2 < len(pieces_of((gb + 1) // 8)):
                    load_piece(gb + 1, tt // 2)
                if it + LA < NIT:
                    load_resid(it + LA)
                bi = it % 8
                rs = it % NR
                it += 1
                for kc in range(KC):
                    S.op("pe", lambda e, bi=bi, ws=ws, kc=kc, tt=tt, KC=KC: e.matmul(
                        bank(bi), lhsT=aT[:, kc, tt * 128:(tt + 1) * 128], rhs=wb[ws][:, kc, :],
                        start=(kc == 0), stop=(kc == KC - 1)),
                        reads=[b_wb[ws], b_aT[tt // 2]], writes=[b_bank[bi]])
                S.op("dve", lambda e, bi=bi, rs=rs: e.tensor_tensor(out=rt[rs][:], in0=bank(bi), in1=rt[rs][:], op=ALU.add),
                     reads=[b_bank[bi], b_rt[rs]], writes=[b_rt[rs]])
                S.dma(lambda e, rs=rs, tt=tt, cb=cb: e.dma_start(
                    out=out_d_[tt * 128:(tt + 1) * 128, cb * 512:(cb + 1) * 512], in_=rt[rs][:]),
                    chan=b_rt[rs], reads=[b_rt[rs]], writes=[dt[(tt, cb)]])
                if cb == 7 and pi_ + 1 < NP and tt % 2 == 1:
                    load_slice(pi_ + 1, tt // 2)

    K.norm_to_actT, K.load_actT, K.gemm_fm, K.gemm_tm = norm_to_actT, load_actT, gemm_fm, gemm_tm
    K.bank, K.bank_bf, K.b_bank, K.ps, K.psb, K.b_ps = bank, bank_bf, b_bank, ps, psb, b_ps

    mark = A.cur
    actT = A.alloc("actT", [128, 32, T], BF16)
    b_act = S.buf("actT")
    mark1 = A.cur
    norm_to_actT(x_d, g1_d, actT, b_act)
    S.barrier()
    if upto <= 0:
        return finish(nc, S)

    A.cur = mark1
    cosT = A.alloc("cosT", [128, T], F32)
    sinT = A.alloc("sinT", [128, T], F32)
    b_rope = S.buf("rope")
    _pm = A.cur
    posi = A.alloc("posi", [128, T], I32)
    A.cur = _pm
    t1 = A.alloc("t1", [128, 1024], F32)
    t2 = A.alloc("t2", [128, 1024], F32)
    b_posi = S.buf("posi")
    invf = A.alloc("invf", [128, 1], F32)
    b_invf = S.buf("invf")
    S.dma(lambda e: e.dma_start(out=invf[:], in_=invf_d), chan=b_invf, writes=[b_invf])
    S.dma(lambda e: e.dma_start(out=posi[:], in_=pos_d.partition_broadcast(128)), chan=b_posi, writes=[b_posi])
    S.op("dve", lambda e: e.tensor_copy(out=cosT[:], in_=posi[:]), reads=[b_posi], writes=[b_rope])
    S.op("dve", lambda e: e.tensor_scalar(out=cosT[:], in0=cosT[:], scalar1=invf[:, 0:1], scalar2=None, op0=ALU.mult),
         reads=[b_rope, b_invf], writes=[b_rope])
    TWO_PI = 2.0 * np.pi
    PI_LO = 3.1415925
    _om = A.cur
    tmpk = A.alloc("tmpk", [128, T], F32)
    A.cur = _om
    ot = [A.alloc("ot", [128, T], BF16) for _ in range(2)]
    b_ot = [S.buf("ot%d" % i) for i in range(2)]

    def reduce_sin(dst):
        S.op("dve", lambda e: e.tensor_scalar(out=posi[:], in0=cosT[:], scalar1=float(1.0 / TWO_PI), scalar2=None, op0=ALU.mult),
             reads=[b_rope], writes=[b_posi])
        S.op("dve", lambda e: e.tensor_copy(out=tmpk[:], in_=posi[:]), reads=[b_posi], writes=b_ot)
        S.op("dve", lambda e: e.scalar_tensor_tensor(out=dst[:], in0=tmpk[:], scalar=-float(TWO_PI), in1=cosT[:],
                                                     op0=ALU.mult, op1=ALU.add), reads=b_ot + [b_rope], writes=[b_rope])
        S.op("dve", lambda e: e.tensor_scalar(out=dst[:], in0=dst[:], scalar1=PI_LO, scalar2=-PI_LO, op0=ALU.min, op1=ALU.max),
             reads=[b_rope], writes=[b_rope])
        S.op("act", lambda e: e.activation(out=dst[:], in_=dst[:], func=AF.Sin), reads=[b_rope], writes=[b_rope])

    reduce_sin(sinT)
    S.op("dve", lambda e: e.tensor_scalar(out=cosT[:], in0=cosT[:], scalar1=float(0.5 * np.pi), scalar2=None, op0=ALU.add),
         reads=[b_rope], writes=[b_rope])
    reduce_sin(cosT)

    qsw = A.alloc("qsw", [128, 1024], F32)
    b_qsw, b_t1, b_t2 = S.buf("qsw"), S.buf("t1"), S.buf("t2")

    def epi1(i, info, pst, b_pst):
        kind, dst, row0, bcol = info
        os_ = i % 2
        if kind == "copy":
            if i % 2 == 0:
                S.op("act", lambda e: e.activation(out=ot[os_][:], in_=pst[:], func=AF.Copy),
                     reads=[b_pst], writes=[b_ot[os_]])
            else:
                S.op("dve", lambda e: e.tensor_copy(out=ot[os_][:], in_=pst[:]), reads=[b_pst], writes=[b_ot[os_]])
        elif kind == "sig":
            S.op("act", lambda e: e.activation(out=ot[os_][:], in_=pst[:], func=AF.Sigmoid, bias=bgate[:, bcol:bcol + 1]),
                 reads=[b_pst, b_bgate], writes=[b_ot[os_]])
        else:
            for hf in range(2):
                sl = slice(hf * 1024, (hf + 1) * 1024)
                S.op("act", lambda e, sl=sl: e.activation(out=qsw[0:64, :], in_=pst[64:128, sl], func=AF.Copy, scale=-1.0),
                     reads=[b_pst], writes=[b_qsw])
                S.op("act", lambda e, sl=sl: e.activation(out=qsw[64:128, :], in_=pst[0:64, sl], func=AF.Copy),
                     reads=[b_pst], writes=[b_qsw])
                S.op("act", lambda e, sl=sl: e.activation(out=t1[:], in_=pst[:, sl], func=AF.Copy),
                     reads=[b_pst], writes=[b_t1])
                S.op("dve", lambda e, sl=sl: e.tensor_tensor(out=t1[:], in0=t1[:], in1=cosT[:, sl], op=ALU.mult),
                     reads=[b_t1, b_rope], writes=[b_t1])
                S.op("pool", lambda e, sl=sl: e.tensor_tensor(out=t2[:], in0=qsw[:], in1=sinT[:, sl], op=ALU.mult),
                     reads=[b_qsw, b_rope], writes=[b_t2])
                S.op("dve", lambda e, sl=sl: e.tensor_tensor(out=ot[os_][:, sl], in0=t1[:], in1=t2[:], op=ALU.add),
                     reads=[b_t1, b_t2], writes=[b_ot[os_]])
        S.dma(lambda e: e.dma_start(out=dst[row0:row0 + 128, :], in_=ot[os_][:]), chan=b_ot[os_], reads=[b_ot[os_]])

    tiles = []
    for ct in range(128):
        c0 = ct * 128
        if ct < 16:
            info = ("copy", fT_d, ct * 128, 0)
        elif ct < 32:
            info = ("rope", qT_d, (ct - 16) * 128, 0)
        elif ct < 48:
            info = ("rope", kT_d, (ct - 32) * 128, 0)
        elif ct < 64:
            info = ("copy", vT_d, (ct - 48) * 128, 0)
        elif ct < 96:
            info = ("sig", gfT_d, (ct - 64) * 128, ct - 64)
        else:
            info = ("sig", gaT_d, (ct - 96) * 128, 32 + ct - 96)
        tiles.append((win_d, c0, False, info))
    if dbg and upto == 1.5:
        tiles = tiles[0:2] + tiles[16:18] + tiles[32:34] + tiles[48:50] + tiles[64:66] + tiles[96:98]
    gemm_fm(actT, b_act, 32, tiles, epi1, PK=8, NSTG=3)
    S.barrier()
    if upto <= 2:
        return finish(nc, S)
    A.cur = mark
    fTs = A.alloc("fTs", [128, 16, T], BF16)
    b_fTs = [S.buf("fTs%d" % i) for i in range(4)]
    load_actT(fTs, b_fTs, fT_d, 0, 16)
    ccsc = A.alloc("ccsc", [128, 2, 512], BF16)
    b_ccsc = S.buf("ccsc")
    S.dma(lambda e: e.dma_start(out=ccsc[:], in_=ccsc_d.rearrange("(j p) n -> p j n", p=128)), chan=b_ccsc, writes=[b_ccsc])
    gsb = [A.alloc("gsb", [128, 2, T], BF16) for _ in range(2)]
    b_gsb = [S.buf("gsb%d" % i) for i in range(2)]
    it = 0
    for tt in range(NT):
        gs = tt % 2
        for g in range(8):
            bi = it % 8
            it += 1
            for j in range(2):
                S.op("pe", lambda e, bi=bi, g=g, j=j, tt=tt: e.matmul(
                    bank(bi), lhsT=fTs[:, 2 * g + j, tt * 128:(tt + 1) * 128], rhs=ccsc[:, j, :],
                    start=(j == 0), stop=(j == 1)), reads=[b_fTs[tt // 4], b_ccsc], writes=[b_bank[bi]])
            if it % 2 == 0:
                S.op("act", lambda e, bi=bi, g=g, gs=gs: e.activation(
                    out=gsb[gs][:, :, g * 256:(g + 1) * 256], in_=bank(bi).rearrange("p (c n) -> p c n", c=2), func=AF.Copy),
                    reads=[b_bank[bi]], writes=[b_gsb[gs]])
            else:
                S.op("dve", lambda e, bi=bi, g=g, gs=gs: e.tensor_copy(
                    out=gsb[gs][:, :, g * 256:(g + 1) * 256], in_=bank(bi).rearrange("p (c n) -> p c n", c=2)),
                    reads=[b_bank[bi]], writes=[b_gsb[gs]])
        for c in range(2):
            S.dma(lambda e, gs=gs, tt=tt, c=c: e.dma_start(
                out=gw_d[c * 2048 + tt * 128:c * 2048 + (tt + 1) * 128, :], in_=gsb[gs][:, c, :]),
                chan=b_gsb[gs], reads=[b_gsb[gs]])
    S.barrier()

    A.cur = mark
    dfc = A.alloc("dfc", [128, 16, 1024], BF16)
    dfs = A.alloc("dfs", [128, 16, 1024], BF16)
    b_dfc = [S.buf("dfc%d" % i) for i in range(2)]
    b_dfs = [S.buf("dfs%d" % i) for i in range(2)]
    altt = A.alloc("altt", [128, 16], BF16)
    b_alt = S.buf("alt")
    S.dma(lambda e: e.dma_start(out=altt[:], in_=alt_d), chan=b_alt, writes=[b_alt])
    dv = dftw_d.rearrange("(k p) t -> p k t", p=128)
    for (dst, bd, k0) in ((dfc, b_dfc, 0), (dfs, b_dfs, 16)):
        for hf in range(2):
            S.dma(lambda e, dst=dst, hf=hf, k0=k0: e.dma_start(out=dst[:, :, hf * 512:(hf + 1) * 512],
                                                                in_=dv[:, k0:k0 + 16, hf * 512:(hf + 1) * 512]),
                  chan=bd[hf], writes=[bd[hf]])
    ot3 = [A.alloc("ot3", [128, T], BF16) for _ in range(2)]
    b_ot3 = [S.buf("ot3%d" % i) for i in range(2)]
    Bs = [A.alloc("Bs", [128, 1024], F32) for _ in range(2)]
    b_Bs = [S.buf("Bs%d" % i) for i in range(2)]
    NW3 = 3
    wb3 = [A.alloc("wb3", [128, 32, 128], BF16) for _ in range(NW3)]
    b_wb3 = [S.buf("wb3%d" % i) for i in range(NW3)]
    gwv = gw_d.rearrange("(k p) n -> p k n", p=128)

    def load3(i):
        ws = i % NW3
        S.dma(lambda e: e.dma_start(out=wb3[ws][:], in_=gwv[:, :, i * 128:(i + 1) * 128]),
              chan=b_wb3[ws], writes=[b_wb3[ws]])

    load3(0)
    load3(1)
    for i in range(16):
        if i + 2 < 16:
            load3(i + 2)
        pp = i % 2
        ws = i % NW3
        pst = ps[pp]
        for (mat, bm, k0, c0) in ((dfc, b_dfc, 0, 0), (dfs, b_dfs, 16, 1024)):
            for hf in range(2):
                for kc in range(16):
                    S.op("pe", lambda e, pst=pst, ws=ws, kc=kc, hf=hf, mat=mat, k0=k0, c0=c0: e.matmul(
                        pst[:, c0 + hf * 512:c0 + (hf + 1) * 512], lhsT=wb3[ws][:, k0 + kc, :],
                        rhs=mat[:, kc, hf * 512:(hf + 1) * 512], start=(kc == 0), stop=(kc == 15)),
                        reads=[b_wb3[ws], bm[hf]], writes=[b_ps[pp]])
        for kc in range(16):
            S.op("pe", lambda e, pst=pst, ws=ws, kc=kc: e.matmul(
                pst[:, 1024:1025], lhsT=wb3[ws][:, kc, :], rhs=altt[:, kc:kc + 1], start=False, stop=(kc == 15),
                skip_group_check=True), reads=[b_wb3[ws], b_alt], writes=[b_ps[pp]])
        os_ = i % 2
        S.op("act", lambda e, pst=pst, os_=os_: e.activation(out=Bs[os_][:], in_=pst[:, 1024:2048], func=AF.Copy),
             reads=[b_ps[pp]], writes=[b_Bs[os_]])
        S.op("dve", lambda e, pst=pst, os_=os_: e.tensor_tensor(out=ot3[os_][:, 0:1024], in0=pst[:, 0:1024], in1=Bs[os_][:],
                                                                 op=ALU.subtract),
             reads=[b_ps[pp], b_Bs[os_]], writes=[b_ot3[os_]])
        pstride = ot3[os_][:].ap[0][0]
        rev = bass.AP(ot3[os_], 2047, [[pstride, 128], [-1, 1023]])
        S.op("dve", lambda e, pst=pst, os_=os_, rev=rev: e.tensor_tensor(out=rev, in0=pst[:, 1:1024], in1=Bs[os_][:, 1:1024],
                                                                          op=ALU.add),
             reads=[b_ps[pp], b_Bs[os_]], writes=[b_ot3[os_]])
        S.op("dve", lambda e, pst=pst, os_=os_: e.tensor_copy(out=ot3[os_][:, 0:1], in_=pst[:, 0:1]),
             reads=[b_ps[pp]], writes=[b_ot3[os_]])
        S.op("dve", lambda e, os_=os_: e.tensor_copy(out=ot3[os_][:, 1024:1025], in_=Bs[os_][:, 0:1]),
             reads=[b_Bs[os_]], writes=[b_ot3[os_]])
        S.dma(lambda e, os_=os_, i=i: e.dma_start(out=yT_d[i * 128:(i + 1) * 128, :], in_=ot3[os_][:]),
              chan=b_ot3[os_], reads=[b_ot3[os_]])
    S.barrier()
    if upto <= 3:
        return finish(nc, S)

    A.cur = mark
    qs = [A.alloc("qs", [128, 2, T], BF16) for _ in range(2)]
    ks = [A.alloc("ks", [128, 2, T], BF16) for _ in range(2)]
    vs = [A.alloc("vs", [128, 2, T], BF16) for _ in range(2)]
    V1 = [A.alloc("V1", [128, 16, 257], BF16) for _ in range(2)]
    oTs = [A.alloc("oTs", [128, 2, T], BF16) for _ in range(2)]
    b_qs = [S.buf("qs%d" % i) for i in range(2)]
    b_ks = [S.buf("ks%d" % i) for i in range(2)]
    b_vs = [S.buf("vs%d" % i) for i in range(2)]
    b_V1 = [S.buf("V1%d" % i) for i in range(2)]
    b_oTs = [S.buf("oTs%d" % i) for i in range(2)]
    NPT = 3
    pt = [A.alloc("pt", [128, 512], BF16) for _ in range(NPT)]
    b_pt = [S.buf("pt%d" % i) for i in range(NPT)]
    o1 = [A.alloc("o1", [128, 256], F32) for _ in range(2)]
    of = [A.alloc("of", [128, 256], F32) for _ in range(2)]
    sq = [A.alloc("sq", [128, 256], F32) for _ in range(2)]
    onb = [A.alloc("onb", [128, 256], BF16) for _ in range(2)]
    stat = [A.alloc("stat", [128, 8], F32) for _ in range(2)]
    b_o1 = [S.buf("o1%d" % i) for i in range(2)]
    b_of = [S.buf("of%d" % i) for i in range(2)]
    b_sq = [S.buf("sq%d" % i) for i in range(2)]
    b_onb = [S.buf("onb%d" % i) for i in range(2)]
    b_stat = [S.buf("stat%d" % i) for i in range(2)]
    for par in range(2):
        S.op("pool", lambda e, par=par: e.memset(V1[par][:, :, 256:257], 1.0), writes=[b_V1[par]])
    SC = 1.0 / float(np.sqrt(128.0))

    def load_head(h):
        p_ = h % 2
        r0 = h * 256
        for (dst, b_dst, src) in ((qs, b_qs, qT_d), (ks, b_ks, kT_d), (vs, b_vs, vT_d)):
            S.dma(lambda e, dst=dst, src=src: e.dma_start(
                out=dst[p_][:], in_=src[r0:r0 + 256, :].rearrange("(c p) t -> p c t", p=128)),
                chan=b_dst[p_], writes=[b_dst[p_]])

    def build_v(h, k4):
        p_ = h % 2
        for kk in range(4):
            kt = k4 * 4 + kk
            for vc in range(2):
                S.op("pe", lambda e, kk=kk, vc=vc, kt=kt: e.transpose(
                    bank_bf(7)[:, (kk * 2 + vc) * 128:(kk * 2 + vc + 1) * 128],
                    vs[p_][:, vc, kt * 128:(kt + 1) * 128], ident[:]),
                    reads=[b_vs[p_], b_ident], writes=[b_bank[7]])
        S.op("dve", lambda e: e.tensor_copy(
            out=V1[p_][:, k4 * 4:(k4 + 1) * 4, 0:256], in_=bank_bf(7).rearrange("p (k n) -> p k n", k=4)),
            reads=[b_bank[7]], writes=[b_V1[p_]])

    def out_transposes(p_, qsub, q0):
        for vc in range(2):
            S.op("pe", lambda e, vc=vc: e.transpose(
                bank_bf(7)[:, vc * 128:(vc + 1) * 128], onb[qsub][:, vc * 128:(vc + 1) * 128], ident[:]),
                reads=[b_onb[qsub], b_ident], writes=[b_bank[7]])
        S.op("dve", lambda e: e.tensor_copy(
            out=oTs[p_][:, :, q0:q0 + 128], in_=bank_bf(7)[:, 0:256].rearrange("p (c t) -> p c t", c=2)),
            reads=[b_bank[7]], writes=[b_oTs[p_]])

    def store_head(h):
        p_ = h % 2
        r0 = h * 256
        S.dma(lambda e: e.dma_start(
            out=oT_d[r0:r0 + 256, :].rearrange("(c p) t -> p c t", p=128), in_=oTs[p_][:]),
            chan=b_oTs[p_], reads=[b_oTs[p_]])

    steps = [(h, qb, c, k2) for h in range(NH) for qb in range(8) for c in range(2) for k2 in range(8)]
    NS = len(steps)

    def score(n):
        h, qb, c, k2 = steps[n]
        p_ = h % 2
        sb = 4 + n % 3
        pi = n % NPT
        for j in range(2):
            kt = 2 * k2 + j
            S.op("pe", lambda e, j=j, kt=kt: e.matmul(bank(sb)[:, j * 256:(j + 1) * 256], lhsT=ks[p_][:, c, kt * 128:(kt + 1) * 128],
                                                      rhs=qs[p_][:, c, qb * 256:(qb + 1) * 256], start=True, stop=True),
                 reads=[b_ks[p_], b_qs[p_]], writes=[b_bank[sb]])
        S.op("act", lambda e: e.activation(out=pt[pi][:], in_=bank(sb), func=AF.Exp, scale=SC),
             reads=[b_bank[sb]], writes=[b_pt[pi]])

    load_head(0)
    for k4 in range(4):
        build_v(0, k4)
    pending = []
    seqc = [0]

    def defer(due, fn):
        seqc[0] += 1
        pending.append((due, seqc[0], fn))
        pending.sort(key=lambda t: (t[0], t[1]))

    score(0)
    score(1)
    for n in range(NS):
        h, qb, c, k2 = steps[n]
        p_ = h % 2
        first = (qb == 0 and c == 0 and k2 == 0)
        if first and h + 1 < NH:
            load_head(h + 1)
        while pending and pending[0][0] <= n:
            pending.pop(0)[2]()
        if n + 2 < NS:
            score(n + 2)
        pi = n % NPT
        for j in range(2):
            kt = 2 * k2 + j
            for qsub in range(2):
                ab = c * 2 + qsub
                S.op("pe", lambda e, ab=ab, pi=pi, qsub=qsub, kt=kt, p_=p_, j=j: e.matmul(
                    bank(ab)[:, 0:257], lhsT=pt[pi][:, j * 256 + qsub * 128:j * 256 + (qsub + 1) * 128], rhs=V1[p_][:, kt, :],
                    start=(kt == 0), stop=(kt == 15)),
                    reads=[b_pt[pi], b_V1[p_]], writes=[b_bank[ab]])
        if h + 1 < NH and c == 0 and k2 == 4 and 2 <= qb < 6:
            build_v(h + 1, qb - 2)
        kt = 15 if k2 == 7 else -1
        if kt == 15:
            for qsub in range(2):
                ab = c * 2 + qsub
                st_ = stat[qsub]
                bs = b_stat[qsub]
                S.op("dve", lambda e, ab=ab, st_=st_: e.reciprocal(out=st_[:, 0:1], in_=bank(ab)[:, 256:257]),
                     reads=[b_bank[ab]], writes=[bs])
                if c == 0:
                    S.op("dve", lambda e, ab=ab, st_=st_, qsub=qsub: e.tensor_scalar(
                        out=o1[qsub][:], in0=bank(ab)[:, 0:256], scalar1=st_[:, 0:1], scalar2=None, op0=ALU.mult),
                        reads=[b_bank[ab], bs], writes=[b_o1[qsub]])
                else:
                    S.op("dve", lambda e, st_=st_: e.tensor_tensor(out=st_[:, 1:2], in0=st_[:, 0:1], in1=NEGLAM, op=ALU.mult),
                         reads=[bs, b_small], writes=[bs])
                    S.op("dve", lambda e, ab=ab, st_=st_, qsub=qsub: e.scalar_tensor_tensor(
                        out=of[qsub][:], in0=bank(ab)[:, 0:256], scalar=st_[:, 1:2], in1=o1[qsub][:],
                        op0=ALU.mult, op1=ALU.add), reads=[b_bank[ab], bs, b_o1[qsub]], writes=[b_of[qsub]])
                    S.op("pool", lambda e, qsub=qsub: e.tensor_tensor(out=sq[qsub][:], in0=of[qsub][:], in1=of[qsub][:], op=ALU.mult),
                         reads=[b_of[qsub]], writes=[b_sq[qsub]])
                    S.op("dve", lambda e, st_=st_, qsub=qsub: e.tensor_reduce(out=st_[:, 2:3], in_=sq[qsub][:], axis=AX.X, op=ALU.add),
                         reads=[b_sq[qsub]], writes=[bs])
                    S.op("dve", lambda e, st_=st_: e.tensor_scalar(out=st_[:, 3:4], in0=st_[:, 2:3], scalar1=1.0 / 256.0,
                                                                   scalar2=EPS, op0=ALU.mult, op1=ALU.add),
                         reads=[bs], writes=[bs])
                    q0 = qb * 256 + qsub * 128

                    def stage_act(st_=st_, bs=bs):
                        S.op("act", lambda e: e.activation(out=st_[:, 4:5], in_=st_[:, 3:4], func=AF.Ln),
                             reads=[bs], writes=[bs])
                        S.op("act", lambda e: e.activation(out=st_[:, 5:6], in_=st_[:, 4:5], func=AF.Exp, scale=-0.5),
                             reads=[bs], writes=[bs])

                    def stage_onb(st_=st_, bs=bs, qsub=qsub):
                        S.op("dve", lambda e: e.scalar_tensor_tensor(
                            out=onb[qsub][:], in0=of[qsub][:], scalar=st_[:, 5:6], in1=g08[:], op0=ALU.mult, op1=ALU.mult),
                            reads=[b_of[qsub], bs, b_g08], writes=[b_onb[qsub]])

                    defer(n + 4 + qsub, stage_act)
                    defer(n + 6 + qsub, stage_onb)
                    defer(n + 9 + qsub, (lambda p_=p_, qsub=qsub, q0=q0: out_transposes(p_, qsub, q0)))
                    if qb == 7 and qsub == 1:
                        defer(n + 12, (lambda h=h: store_head(h)))
    while pending:
        pending.pop(0)[2]()
    S.barrier()
    if upto <= 4:
        return finish(nc, S)

    A.cur = mark
    yTs = A.alloc("yTs", [128, 16, T], BF16)
    b_yTs = [S.buf("yTs%d" % i) for i in range(4)]
    gt = [A.alloc("gt", [128, T], BF16) for _ in range(2)]
    tm = [A.alloc("tm", [128, T], F32) for _ in range(2)]
    b_gt = [S.buf("gt%d" % i) for i in range(2)]
    b_tm = [S.buf("tm%d" % i) for i in range(2)]

    def pre5a(i):
        sl = i % 2
        S.dma(lambda e: e.dma_start(out=gt[sl][:], in_=gfT_d[i * 128:(i + 1) * 128, :]), chan=b_gt[sl], writes=[b_gt[sl]])

    pre5a(0)

    def epi5a(i, info, pst, b_pst):
        sl = i % 2
        if i + 1 < 32:
            pre5a(i + 1)
        S.op("dve", lambda e: e.tensor_tensor(out=tm[sl][:], in0=pst[:], in1=gt[sl][:], op=ALU.mult),
             reads=[b_pst, b_gt[sl]], writes=[b_tm[sl]])
        S.dma(lambda e: e.dma_start(out=tmpT_d[i * 128:(i + 1) * 128, :], in_=tm[sl][:]), chan=b_tm[sl], reads=[b_tm[sl]])

    gemm_fm(yTs, b_yTs, 16, [(wf_d, ct * 128, False, None) for ct in range(32)], epi5a,
            act_loader=lambda: load_actT(yTs, b_yTs, yT_d, 0, 16))
    S.barrier()

    A.cur = mark
    oTa = A.alloc("oTa", [128, 16, T], BF16)
    b_oTa = [S.buf("oTa%d" % i) for i in range(4)]
    gt = [A.alloc("gt", [128, T], BF16) for _ in range(2)]
    tm = [A.alloc("tm", [128, T], F32) for _ in range(2)]
    m1 = A.alloc("m1", [128, T], F32)
    mo = [A.alloc("mo", [128, T], BF16) for _ in range(2)]
    b_gt = [S.buf("gt%d" % i) for i in range(2)]
    b_tm = [S.buf("tm%d" % i) for i in range(2)]
    b_m1 = S.buf("m1")
    b_mo = [S.buf("mo%d" % i) for i in range(2)]

    def pre5b(i):
        sl = i % 2
        S.dma(lambda e: e.dma_start(out=gt[sl][:], in_=gaT_d[i * 128:(i + 1) * 128, :]), chan=b_gt[sl], writes=[b_gt[sl]])
        S.dma(lambda e: e.dma_start(out=tm[sl][:], in_=tmpT_d[i * 128:(i + 1) * 128, :]), chan=b_tm[sl], writes=[b_tm[sl]])

    pre5b(0)

    def epi5b(i, info, pst, b_pst):
        sl = i % 2
        if i + 1 < 32:
            pre5b(i + 1)
        S.op("dve", lambda e: e.tensor_tensor(out=m1[:], in0=pst[:], in1=gt[sl][:], op=ALU.mult),
             reads=[b_pst, b_gt[sl]], writes=[b_m1])
        S.op("dve", lambda e: e.tensor_tensor(out=mo[sl][:], in0=m1[:], in1=tm[sl][:], op=ALU.add),
             reads=[b_m1, b_tm[sl]], writes=[b_mo[sl]])
        S.dma(lambda e: e.dma_start(out=mT_d[i * 128:(i + 1) * 128, :], in_=mo[sl][:]), chan=b_mo[sl], reads=[b_mo[sl]])

    gemm_fm(oTa, b_oTa, 16, [(wa_d, ct * 128, False, None) for ct in range(32)], epi5b,
            act_loader=lambda: load_actT(oTa, b_oTa, oT_d, 0, 16))
    S.barrier()

    A.cur = mark
    gemm_tm(mT_d, [16, 16], wo_d, x_d, h1_d)
    S.barrier()
    if upto <= 6:
        return finish(nc, S)

    A.cur = mark
    act2 = A.alloc("act2", [128, 32, T], BF16)
    b_act2 = S.buf("act2")
    m7 = A.cur
    norm_to_actT(h1_d, g2_d, act2, b_act2)
    S.barrier()
    A.cur = m7
    sgt = A.alloc("sgt", [128, T], F32)
    b_sgt = S.buf("sgt")
    ot8 = [A.alloc("ot8", [128, T], BF16) for _ in range(2)]
    b_ot8 = [S.buf("ot8%d" % i) for i in range(2)]

    def epi8(i, info, pst, b_pst):
        kind, j = info
        if kind == "g":
            S.op("act", lambda e: e.activation(out=sgt[:], in_=pst[:], func=AF.Silu), reads=[b_pst], writes=[b_sgt])
        else:
            os_ = j % 2
            S.op("dve", lambda e: e.tensor_tensor(out=ot8[os_][:], in0=pst[:], in1=sgt[:], op=ALU.mult),
                 reads=[b_pst, b_sgt], writes=[b_ot8[os_]])
            S.dma(lambda e: e.dma_start(out=hidT_d[j * 128:(j + 1) * 128, :], in_=ot8[os_][:]),
                  chan=b_ot8[os_], reads=[b_ot8[os_]])

    tiles8 = []
    for j in range(HC):
        tiles8.append((wg_d, j * 128, False, ("g", j)))
        tiles8.append((wu_d, j * 128, False, ("u", j)))
    gemm_fm(act2, b_act2, 32, tiles8, epi8, PK=8, NSTG=3)
    S.barrier()

    A.cur = mark
    gemm_tm(hidT_d, [22, 22, 21, 21], wd_d, h1_d, h2_d)
    S.barrier()

    A.cur = mark
    gbc = A.alloc("gbc", [128, D], F32)
    b_gbc = S.buf("gbc")
    S.dma(lambda e: e.dma_start(out=gbc[:], in_=g3_d.partition_broadcast(128)), chan=b_gbc, writes=[b_gbc])
    xt = [A.alloc("xt", [128, D], F32) for _ in range(2)]
    yo = [A.alloc("yo", [128, D], F32) for _ in range(2)]
    junk = A.alloc("junk", [128, D], BF16)
    st = A.alloc("st", [128, NT, 2], F32)
    b_xt = [S.buf("xt%d" % i) for i in range(2)]
    b_yo = [S.buf("yo%d" % i) for i in range(2)]
    b_junk, b_st = S.buf("junk"), S.buf("st")
    S.op("pool", lambda e: e.memset(st[:], 0.0), writes=[b_st])
    for tt in range(NT):
        sl = tt % 2
        S.dma(lambda e, sl=sl, tt=tt: e.dma_start(out=xt[sl][:], in_=h2_d[tt * 128:(tt + 1) * 128, :]),
              chan=b_xt[sl], writes=[b_xt[sl]])
        S.op("act", lambda e, sl=sl, tt=tt: e.activation(out=junk[:], in_=xt[sl][:], func=AF.Square, accum_out=st[:, tt, 0:1]),
             reads=[b_xt[sl], b_st], writes=[b_junk, b_st])
        S.op("dve", lambda e, tt=tt: e.tensor_scalar(out=st[:, tt, 1:2], in0=st[:, tt, 0:1], scalar1=1.0 / D, scalar2=EPS,
                                                     op0=ALU.mult, op1=ALU.add), reads=[b_st], writes=[b_st])
        S.op("act", lambda e, tt=tt: e.activation(out=st[:, tt, 1:2], in_=st[:, tt, 1:2], func=AF.Ln), reads=[b_st], writes=[b_st])
        S.op("act", lambda e, tt=tt: e.activation(out=st[:, tt, 1:2], in_=st[:, tt, 1:2], func=AF.Exp, scale=-0.5),
             reads=[b_st], writes=[b_st])
        S.op("dve", lambda e, sl=sl, tt=tt: e.scalar_tensor_tensor(out=yo[sl][:], in0=xt[sl][:], scalar=st[:, tt, 1:2],
                                                                   in1=gbc[:], op0=ALU.mult, op1=ALU.mult),
             reads=[b_xt[sl], b_st, b_gbc], writes=[b_yo[sl]])
        S.dma(lambda e, sl=sl, tt=tt: e.dma_start(out=out_d[tt * 128:(tt + 1) * 128, :], in_=yo[sl][:]),
              chan=b_yo[sl], reads=[b_yo[sl]])
    S.barrier()
    return finish(nc, S)


def finish(nc, S):
    S.finalize()
    return nc


_CACHE = {}


def kernel(**inputs):
    f32 = np.float32
    x = np.asarray(inputs["x"], dtype=f32)
    pos = np.asarray(inputs["positions"], dtype=np.int32)
    if "consts" not in _CACHE:
        _CACHE["consts"] = _consts()
    nc = build()
    dftw, ccsc, ident, invf, alt = _CACHE["consts"]
    g = lambda k: np.ascontiguousarray(np.asarray(inputs[k], dtype=f32)[0])
    bg = np.ascontiguousarray(np.asarray(inputs["b_gate"], dtype=f32)[0].reshape(2, 32, 128).transpose(2, 0, 1).reshape(128, 64))
    shared = {"norm_mix_g": g("norm_mix_g"), "w_in": g("w_in"), "bgate": bg,
              "lambda_q1": g("lambda_q1"), "lambda_k1": g("lambda_k1"), "lambda_q2": g("lambda_q2"),
              "lambda_k2": g("lambda_k2"), "subln_g": g("subln_g"), "w_fourier_out": g("w_fourier_out"),
              "w_attn_out": g("w_attn_out"), "w_out": g("w_out"), "norm_ffn_g": g("norm_ffn_g"),
              "w_ffn_gate": g("w_ffn_gate"), "w_ffn_up": g("w_ffn_up"), "w_ffn_down": g("w_ffn_down"),
              "norm_final_g": np.ascontiguousarray(np.asarray(inputs["norm_final_g"], dtype=f32)),
              "dftw": dftw, "ccsc": ccsc, "ident": ident, "invf": invf, "alt": alt}
    in_maps = []
    for b in range(8):
        m = dict(shared)
        m["x"] = np.ascontiguousarray(x[b])
        m["positions"] = np.ascontiguousarray(pos[b])
        in_maps.append(m)
    res = run_bass_kernel_spmd(nc, in_maps, core_ids=list(range(8)))
    return np.stack([np.asarray(r["out"], dtype=f32) for r in res.results], axis=0)
```

```python
import ml_dtypes
import time
import numpy as np
import concourse.bass as bass
import concourse.mybir as mybir
from concourse.bass_utils import run_bass_kernel_spmd

F32 = mybir.dt.float32
BF16 = mybir.dt.bfloat16
I32 = mybir.dt.int32
AF = mybir.ActivationFunctionType
ALU = mybir.AluOpType
AX = mybir.AxisListType

COMPUTE = ("pe", "act", "dve", "pool")


class Buf:
    __slots__ = ("name", "lw", "rd", "sem")

    def __init__(self, name):
        self.name = name
        self.lw = None
        self.rd = {}
        self.sem = None


class Ins:
    __slots__ = ("stream", "src", "fn", "deps", "sig", "sigval", "dma")

    def __init__(self, stream, src, fn, dma):
        self.stream = stream
        self.src = src
        self.fn = fn
        self.deps = []
        self.sig = False
        self.sigval = 0
        self.dma = dma


class Sched:
    def __init__(self, nc):
        self.nc = nc
        self.streams = {"pe": [], "act": [], "dve": [], "pool": [], "sp": []}
        self.all = []
        self.sems = {e: nc.alloc_semaphore("sem_" + e) for e in COMPUTE}
        self.dma_sem_pool = []
        self.ndma = 0
        self.last = {}
        self.bufs_with_sem = []

    def buf(self, name):
        return Buf(name)

    def _dma_sem(self, b):
        if b.sem is None:
            if self.dma_sem_pool:
                b.sem = self.dma_sem_pool.pop()
            else:
                self.ndma += 1
                b.sem = self.nc.alloc_semaphore("dsem%d" % self.ndma)
            self.bufs_with_sem.append(b)
        return b.sem

    def _track(self, ins, reads, writes):
        deps = []
        for b in reads:
            if b.lw is not None:
                deps.append((b.lw, "raw"))
        for b in writes:
            if b.lw is not None:
                deps.append((b.lw, "waw"))
            for r in b.rd.values():
                deps.append((r, "war"))
        for d, kind in deps:
            if d is ins:
                continue
            if (not ins.dma) and (not d.dma) and d.src == ins.src:
                if ins.src == "pe" or kind != "raw":
                    continue
            d.sig = True
            ins.deps.append(d)
        for b in reads:
            b.rd[ins.src] = ins
        for b in writes:
            b.lw = ins
            b.rd = {}

    def op(self, eng, fn, reads=(), writes=()):
        ins = Ins(eng, eng, fn, False)
        self._track(ins, reads, writes)
        self.streams[eng].append(ins)
        self.all.append(ins)
        self.last[eng] = ins
        return ins

    def dma(self, fn, chan, reads=(), writes=(), q="sp"):
        sem = self._dma_sem(chan)
        ins = Ins(q, sem, fn, True)
        ins.sig = True
        self._track(ins, reads, writes)
        self.streams[q].append(ins)
        self.all.append(ins)
        self.last[sem] = ins
        return ins

    def barrier(self):
        lasts = list(self.last.values())
        for d in lasts:
            d.sig = True
        for s in self.streams:
            ins = Ins(s, None, None, False)
            ins.deps = list(lasts)
            self.streams[s].append(ins)
            self.all.append(ins)
        for b in self.bufs_with_sem:
            self.dma_sem_pool.append(b.sem)
            b.sem = None
        self.bufs_with_sem = []

    def finalize(self):
        nc = self.nc
        cnt = {}
        for ins in self.all:
            if ins.src is None:
                continue
            if ins.sig:
                cnt[ins.src] = cnt.get(ins.src, 0) + (16 if ins.dma else 1)
                ins.sigval = cnt[ins.src]
        sems = self.sems
        streams = self.streams

        def replay(sname):
            def run(e):
                waited = {}
                for ins in streams[sname]:
                    need = {}
                    for d in ins.deps:
                        k = d.src
                        if d.sigval > need.get(k, 0):
                            need[k] = d.sigval
                    for k, v in need.items():
                        if v > waited.get(k, 0):
                            waited[k] = v
                            e.wait_ge(sems[k] if isinstance(k, str) else k, v)
                    if ins.fn is None:
                        continue
                    bi = ins.fn(e)
                    if ins.sig:
                        if ins.dma:
                            bi.then_inc(ins.src, 16)
                        else:
                            bi.then_inc(sems[ins.src], 1)
            return run

        with nc.Block() as block:
            block.tensor(replay("pe"))
            block.scalar(replay("act"))
            block.vector(replay("dve"))
            block.gpsimd(replay("pool"))
            block.sync(replay("sp"))


class SbufAlloc:
    def __init__(self, nc, base=None, top=None):
        self.nc = nc
        self.base = ((nc.sbuf_base + 63) // 64) * 64 if base is None else base
        self.top = nc.sbuf_top if top is None else top
        self.cur = self.base
        self.n = 0

    def reset(self):
        self.cur = self.base

    def alloc(self, name, shape, dtype):
        esz = 4 if dtype in (F32, I32) else 2
        per = esz
        for s in shape[1:]:
            per *= s
        off = self.cur
        self.cur = ((off + per + 63) // 64) * 64
        assert self.cur <= self.top, ("SBUF overflow", name, self.cur, self.top)
        self.n += 1
        return self.nc.alloc_sbuf_tensor_at("%s_%d" % (name, self.n), list(shape), dtype, offset=off)


D = 4096
T = 2048
NT = T // 128
FW = 2048
AW = 2048
HID = 11008
HC = HID // 128
NH = 8
EPS = 1e-6
LAM_INIT = 0.2


def _consts():
    s = np.arange(2048, dtype=np.float64)
    sp = np.arange(1024, dtype=np.float64)
    ang = 2.0 * np.pi * np.outer(s, sp) / 2048.0
    dftw = np.concatenate([np.cos(ang), np.sin(ang)], 0) / np.sqrt(2048.0)
    alt = np.tile(((-1.0) ** np.arange(128)).reshape(128, 1), (1, 16)) / np.sqrt(2048.0)
    c = np.arange(256, dtype=np.float64)
    angc = 2.0 * np.pi * np.outer(c, c) / 256.0
    ccsc = np.concatenate([np.cos(angc), np.sin(angc)], 1) / 16.0
    ident = np.eye(128)
    j = np.arange(0, 128, 2, dtype=np.float32) / np.float32(128)
    inv = (np.float32(10000.0) ** (-j)).astype(np.float32)
    invf = np.concatenate([inv, inv]).reshape(128, 1).astype(np.float32)
    return (dftw.astype(ml_dtypes.bfloat16), ccsc.astype(ml_dtypes.bfloat16),
            ident.astype(ml_dtypes.bfloat16), invf, alt.astype(ml_dtypes.bfloat16))


class K:
    pass


def build(upto=99, dbg=False):
    nc = bass.Bass("TRN2", target_bir_lowering=False)
    S = Sched(nc)

    def din(name, shape, dt):
        return nc.dram_tensor(name, list(shape), dt, kind="ExternalInput").ap()

    def dscr(name, shape, dt):
        return nc.dram_tensor(name, list(shape), dt, kind=("ExternalOutput" if dbg else "Internal")).ap()

    x_d = din("x", [T, D], F32)
    pos_d = din("positions", [T], I32)
    g1_d = din("norm_mix_g", [D], F32)
    win_d = din("w_in", [D, 16384], F32)
    bg_d = din("bgate", [128, 64], F32)
    lq1_d = din("lambda_q1", [128], F32)
    lk1_d = din("lambda_k1", [128], F32)
    lq2_d = din("lambda_q2", [128], F32)
    lk2_d = din("lambda_k2", [128], F32)
    sg_d = din("subln_g", [256], F32)
    wf_d = din("w_fourier_out", [FW, D], F32)
    wa_d = din("w_attn_out", [AW, D], F32)
    wo_d = din("w_out", [D, D], F32)
    g2_d = din("norm_ffn_g", [D], F32)
    wg_d = din("w_ffn_gate", [D, HID], F32)
    wu_d = din("w_ffn_up", [D, HID], F32)
    wd_d = din("w_ffn_down", [HID, D], F32)
    g3_d = din("norm_final_g", [D], F32)
    dftw_d = din("dftw", [4096, 1024], BF16)
    alt_d = din("alt", [128, 16], BF16)
    ccsc_d = din("ccsc", [256, 512], BF16)
    ident_d = din("ident", [128, 128], BF16)
    invf_d = din("invf", [128, 1], F32)
    out_d = nc.dram_tensor("out", [T, D], F32, kind="ExternalOutput").ap()

    fT_d = dscr("fT", [FW, T], BF16)
    qT_d = dscr("qT", [AW, T], BF16)
    kT_d = dscr("kT", [AW, T], BF16)
    vT_d = dscr("vT", [AW, T], BF16)
    gfT_d = dscr("gfT", [D, T], BF16)
    gaT_d = dscr("gaT", [D, T], BF16)
    gw_d = dscr("gw", [4096, 2048], BF16)
    yT_d = dscr("yT", [FW, T], BF16)
    oT_d = dscr("oT", [AW, T], BF16)
    tmpT_d = dscr("tmpT", [D, T], F32)
    mT_d = dscr("mT", [D, T], BF16)
    h1_d = dscr("h1", [T, D], F32)
    hidT_d = dscr("hidT", [HID, T], BF16)
    h2_d = dscr("h2", [T, D], F32)

    A = SbufAlloc(nc)
    ps = [nc.alloc_psum_tensor("psA", [128, 2048], F32), nc.alloc_psum_tensor("psB", [128, 2048], F32)]
    psb = [p.bitcast(BF16) for p in ps]
    b_ps = [S.buf("psA"), S.buf("psB")]
    b_bank = [S.buf("bank%d" % i) for i in range(8)]

    def bank(i):
        return ps[i // 4][:, (i % 4) * 512:(i % 4 + 1) * 512]

    def bank_bf(i):
        return psb[i // 4][:, (i % 4) * 1024:(i % 4 + 1) * 1024]

    ident = A.alloc("ident", [128, 128], BF16)
    b_ident = S.buf("ident")
    S.dma(lambda e: e.dma_start(out=ident[:], in_=ident_d), chan=b_ident, writes=[b_ident])
    small = A.alloc("small", [128, 64], F32)
    b_small = S.buf("small")
    bgate = A.alloc("bgate", [128, 64], F32)
    b_bgate = S.buf("bgate")
    S.dma(lambda e: e.dma_start(out=bgate[:], in_=bg_d), chan=b_bgate, writes=[b_bgate])
    g08 = A.alloc("g08", [128, 256], F32)
    b_g08 = S.buf("g08")
    S.dma(lambda e: e.dma_start(out=g08[:], in_=sg_d.partition_broadcast(128)), chan=b_g08, writes=[b_g08])
    S.op("dve", lambda e: e.tensor_scalar(out=g08[:], in0=g08[:], scalar1=1.0 - LAM_INIT, scalar2=None, op0=ALU.mult),
         reads=[b_g08], writes=[b_g08])
    lam4 = A.alloc("lam4", [128, 4, 128], F32)
    b_lam4 = S.buf("lam4")
    for i, dd in enumerate([lq1_d, lk1_d, lq2_d, lk2_d]):
        S.dma(lambda e, i=i, dd=dd: e.dma_start(out=lam4[:, i, :], in_=dd.partition_broadcast(128)),
              chan=b_lam4, writes=[b_lam4])
    lamp = A.alloc("lamp", [128, 2, 128], F32)
    b_lamp = S.buf("lamp")
    S.op("dve", lambda e: e.tensor_tensor(out=lamp[:, 0, :], in0=lam4[:, 0, :], in1=lam4[:, 1, :], op=ALU.mult),
         reads=[b_lam4], writes=[b_lamp])
    S.op("dve", lambda e: e.tensor_tensor(out=lamp[:, 1, :], in0=lam4[:, 2, :], in1=lam4[:, 3, :], op=ALU.mult),
         reads=[b_lam4, b_lamp], writes=[b_lamp])
    S.op("dve", lambda e: e.tensor_reduce(out=small[:, 0:2], in_=lamp[:], axis=AX.X, op=ALU.add),
         reads=[b_lamp], writes=[b_small])
    S.op("act", lambda e: e.activation(out=small[:, 2:4], in_=small[:, 0:2], func=AF.Exp),
         reads=[b_small], writes=[b_small])
    S.op("dve", lambda e: e.tensor_tensor(out=small[:, 4:5], in0=small[:, 3:4], in1=small[:, 2:3], op=ALU.subtract),
         reads=[b_small], writes=[b_small])
    S.op("dve", lambda e: e.tensor_scalar(out=small[:, 5:6], in0=small[:, 4:5], scalar1=-LAM_INIT, scalar2=None, op0=ALU.add),
         reads=[b_small], writes=[b_small])
    NEGLAM = small[:, 5:6]
    A.base = A.cur

    K.nc, K.S, K.A = nc, S, A

    def norm_to_actT(src_d, g_d, actT, b_act):
        gbc = A.alloc("gbc", [128, D], F32)
        b_gbc = S.buf("gbc")
        S.dma(lambda e: e.dma_start(out=gbc[:], in_=g_d.partition_broadcast(128)), chan=b_gbc, writes=[b_gbc])
        xt = [A.alloc("xt", [128, D], F32) for _ in range(2)]
        b_xt = [S.buf("xt%d" % i) for i in range(2)]
        junk = A.alloc("junk", [128, D], BF16)
        b_junk = S.buf("junk")
        ub = [A.alloc("ub", [128, D], BF16) for _ in range(2)]
        b_ub = [S.buf("ub%d" % i) for i in range(2)]
        st = A.alloc("st", [128, NT, 2], F32)
        b_stl = [S.buf("st%d" % i) for i in range(NT)]
        S.op("pool", lambda e: e.memset(st[:], 0.0), writes=b_stl)
        for tt in range(NT):
            sl = tt % 2
            b_st = b_stl[tt]
            S.dma(lambda e, sl=sl, tt=tt: e.dma_start(out=xt[sl][:], in_=src_d[tt * 128:(tt + 1) * 128, :]),
                  chan=b_xt[sl], writes=[b_xt[sl]])
            S.op("act", lambda e, sl=sl, tt=tt: e.activation(out=junk[:], in_=xt[sl][:], func=AF.Square,
                                                             accum_out=st[:, tt, 0:1]),
                 reads=[b_xt[sl], b_st], writes=[b_junk, b_st])
            S.op("pool", lambda e, tt=tt: e.tensor_scalar(out=st[:, tt, 1:2], in0=st[:, tt, 0:1], scalar1=1.0 / D,
                                                          scalar2=EPS, op0=ALU.mult, op1=ALU.add),
                 reads=[b_st], writes=[b_st])
            S.op("act", lambda e, tt=tt: e.activation(out=st[:, tt, 1:2], in_=st[:, tt, 1:2], func=AF.Ln),
                 reads=[b_st], writes=[b_st])
            S.op("act", lambda e, tt=tt: e.activation(out=st[:, tt, 1:2], in_=st[:, tt, 1:2], func=AF.Exp, scale=-0.5),
                 reads=[b_st], writes=[b_st])
            S.op("dve", lambda e, sl=sl, tt=tt: e.scalar_tensor_tensor(out=ub[sl][:], in0=xt[sl][:], scalar=st[:, tt, 1:2],
                                                                       in1=gbc[:], op0=ALU.mult, op1=ALU.mult),
                 reads=[b_xt[sl], b_st, b_gbc], writes=[b_ub[sl]])
            for c8 in range(4):
                bi = (tt * 4 + c8) % 8
                for j in range(8):
                    kc = c8 * 8 + j
                    S.op("pe", lambda e, bi=bi, j=j, kc=kc, sl=sl: e.transpose(
                        bank_bf(bi)[:, j * 128:(j + 1) * 128], ub[sl][:, kc * 128:(kc + 1) * 128], ident[:]),
                        reads=[b_ub[sl], b_ident], writes=[b_bank[bi]])
                eng = "dve"
                if eng == "act":
                    S.op("act", lambda e, bi=bi, c8=c8, tt=tt: e.activation(
                        out=actT[:, c8 * 8:(c8 + 1) * 8, tt * 128:(tt + 1) * 128],
                        in_=bank_bf(bi).rearrange("p (k t) -> p k t", k=8), func=AF.Copy),
                        reads=[b_bank[bi]], writes=[b_act])
                else:
                    S.op("dve", lambda e, bi=bi, c8=c8, tt=tt: e.tensor_copy(
                        out=actT[:, c8 * 8:(c8 + 1) * 8, tt * 128:(tt + 1) * 128],
                        in_=bank_bf(bi).rearrange("p (k t) -> p k t", k=8)),
                        reads=[b_bank[bi]], writes=[b_act])

    def load_actT(actT, b_act, src_d, kc0, KC):
        v = src_d.rearrange("(k p) t -> p k t", p=128)
        for tb in range(4):
            bb = b_act[tb] if isinstance(b_act, list) else b_act
            S.dma(lambda e, tb=tb: e.dma_start(out=actT[:, 0:KC, tb * 512:(tb + 1) * 512],
                                               in_=v[:, kc0:kc0 + KC, tb * 512:(tb + 1) * 512]),
                  chan=bb, writes=[bb])

    def gemm_fm(actT, b_act, KC, tiles, epi, kc0=0, PK=16, NW=3, NSTG=2, DIST=2, act_loader=None):
        stg = [A.alloc("stg", [128, PK, 128], F32) for _ in range(NSTG)]
        b_stg = [S.buf("stg%d" % i) for i in range(NSTG)]
        wb = [A.alloc("wb", [128, KC, 128], BF16) for _ in range(NW)]
        b_wb = [S.buf("wb%d" % i) for i in range(NW)]
        state = {"pc": 0}

        def load(i):
            w_ap, c0, isbf, _ = tiles[i]
            ws = i % NW
            wv = w_ap.rearrange("(k p) n -> p k n", p=128)
            if isbf:
                S.dma(lambda e: e.dma_start(out=wb[ws][:], in_=wv[:, kc0:kc0 + KC, c0:c0 + 128]),
                      chan=b_wb[ws], writes=[b_wb[ws]])
                return
            for p0 in range(0, KC, PK):
                n = min(PK, KC - p0)
                sl = state["pc"] % NSTG
                state["pc"] += 1
                S.dma(lambda e, sl=sl, p0=p0, n=n: e.dma_start(
                    out=stg[sl][:, 0:n, :], in_=wv[:, kc0 + p0:kc0 + p0 + n, c0:c0 + 128]),
                    chan=b_stg[sl], writes=[b_stg[sl]])
                S.op("pool", lambda e, sl=sl, p0=p0, n=n: e.tensor_copy(out=wb[ws][:, p0:p0 + n, :], in_=stg[sl][:, 0:n, :]),
                     reads=[b_stg[sl]], writes=[b_wb[ws]])

        n = len(tiles)
        for i in range(min(DIST, n)):
            load(i)
        if act_loader is not None:
            act_loader()
        for i in range(n):
            if i + DIST < n:
                load(i + DIST)
            pp = i % 2
            ws = i % NW
            for tb in range(4):
                for kc in range(KC):
                    S.op("pe", lambda e, pp=pp, tb=tb, ws=ws, kc=kc: e.matmul(
                        ps[pp][:, tb * 512:(tb + 1) * 512], lhsT=wb[ws][:, kc, :],
                        rhs=actT[:, kc, tb * 512:(tb + 1) * 512], start=(kc == 0), stop=(kc == KC - 1)),
                        reads=[b_wb[ws], (b_act[tb] if isinstance(b_act, list) else b_act)], writes=[b_ps[pp]])
            epi(i, tiles[i][3], ps[pp], b_ps[pp])

    def gemm_tm(src_d, KCs, w_ap, resid0_d, out_d_, NW=2, NSTG=3, PK=4):
        KCm = max(KCs)
        NSL = 8
        aT = A.alloc("aT", [128, KCm, T], BF16)
        b_aT = [S.buf("aT%d" % i) for i in range(NSL)]
        stg = [A.alloc("stg", [128, PK, 512], F32) for _ in range(NSTG)]
        b_stg = [S.buf("stg%d" % i) for i in range(NSTG)]
        wb = [A.alloc("wb", [128, KCm, 512], BF16) for _ in range(NW)]
        b_wb = [S.buf("wb%d" % i) for i in range(NW)]
        NR = 8
        rt = [A.alloc("rt", [128, 512], F32) for _ in range(NR)]
        b_rt = [S.buf("rt%d" % i) for i in range(NR)]
        dt = {}
        for tt in range(NT):
            for cb in range(8):
                dt[(tt, cb)] = S.buf("dt")
        wv = w_ap.rearrange("(k p) n -> p k n", p=128)
        sv = src_d.rearrange("(k p) t -> p k t", p=128)
        state = {"pc": 0}
        kc0s = [sum(KCs[:i]) for i in range(len(KCs))]

        def load_slice(pi_, sl_):
            KC, kc0 = KCs[pi_], kc0s[pi_]
            S.dma(lambda e: e.dma_start(out=aT[:, 0:KC, sl_ * 256:(sl_ + 1) * 256],
                                        in_=sv[:, kc0:kc0 + KC, sl_ * 256:(sl_ + 1) * 256]),
                  chan=b_aT[sl_], writes=[b_aT[sl_]])

        def pieces_of(pi_):
            KC = KCs[pi_]
            return [(p0, min(PK, KC - p0)) for p0 in range(0, KC, PK)]

        def load_piece(gb, k):
            pi_, cb = gb // 8, gb % 8
            ws = gb % NW
            p0, n = pieces_of(pi_)[k]
            kc0 = kc0s[pi_]
            sl = state["pc"] % NSTG
            state["pc"] += 1
            S.dma(lambda e: e.dma_start(
                out=stg[sl][:, 0:n, :], in_=wv[:, kc0 + p0:kc0 + p0 + n, cb * 512:(cb + 1) * 512]),
                chan=b_stg[sl], writes=[b_stg[sl]])
            S.op("pool", lambda e: e.tensor_copy(out=wb[ws][:, p0:p0 + n, :], in_=stg[sl][:, 0:n, :]),
                 reads=[b_stg[sl]], writes=[b_wb[ws]])

        NP = len(KCs)
        for k in range(len(pieces_of(0))):
            load_piece(0, k)
        for sl_ in range(NSL):
            load_slice(0, sl_)
        LA = 3
        NIT = NP * 8 * NT

        def load_resid(j):
            gb_, tt_ = j // NT, j % NT
            pj, cbj = gb_ // 8, gb_ % 8
            rsj = j % NR
            src = resid0_d if pj == 0 else out_d_
            rd = [dt[(tt_, cbj)]] if pj > 0 else []
            S.dma(lambda e: e.dma_start(
                out=rt[rsj][:], in_=src[tt_ * 128:(tt_ + 1) * 128, cbj * 512:(cbj + 1) * 512]),
                chan=b_rt[rsj], reads=rd, writes=[b_rt[rsj]])

        for j in range(LA):
            load_resid(j)
        it = 0
        for gb in range(NP * 8):
            pi_, cb = gb // 8, gb % 8
            KC = KCs[pi_]
            ws = gb % NW
            for tt in range(NT):
                if gb + 1 < NP * 8 and tt % 2 == 0 and tt // 2 < len(pieces_of((gb + 1) // 8)):
                    load_piece(gb + 1, tt // 2)
                if it + LA < NIT:
                    load_resid(it + LA)
                bi = it % 8
                rs = it % NR
                it += 1
                for kc in range(KC):
                    S.op("pe", lambda e, bi=bi, ws=ws, kc=kc, tt=tt, KC=KC: e.matmul(
                        bank(bi), lhsT=aT[:, kc, tt * 128:(tt + 1) * 128], rhs=wb[ws][:, kc, :],
                        start=(kc == 0), stop=(kc == KC - 1)),
                        reads=[b_wb[ws], b_aT[tt // 2]], writes=[b_bank[bi]])
                S.op("dve", lambda e, bi=bi, rs=rs: e.tensor_tensor(out=rt[rs][:], in0=bank(bi), in1=rt[rs][:], op=ALU.add),
                     reads=[b_bank[bi], b_rt[rs]], writes=[b_rt[rs]])
                S.dma(lambda e, rs=rs, tt=tt, cb=cb: e.dma_start(
                    out=out_d_[tt * 128:(tt + 1) * 128, cb * 512:(cb + 1) * 512], in_=rt[rs][:]),
                    chan=b_rt[rs], reads=[b_rt[rs]], writes=[dt[(tt, cb)]])
                if cb == 7 and pi_ + 1 < NP and tt % 2 == 1:
                    load_slice(pi_ + 1, tt // 2)

    K.norm_to_actT, K.load_actT, K.gemm_fm, K.gemm_tm = norm_to_actT, load_actT, gemm_fm, gemm_tm
    K.bank, K.bank_bf, K.b_bank, K.ps, K.psb, K.b_ps = bank, bank_bf, b_bank, ps, psb, b_ps

    mark = A.cur
    actT = A.alloc("actT", [128, 32, T], BF16)
    b_act = S.buf("actT")
    mark1 = A.cur
    norm_to_actT(x_d, g1_d, actT, b_act)
    S.barrier()
    if upto <= 0:
        return finish(nc, S)

    A.cur = mark1
    cosT = A.alloc("cosT", [128, T], F32)
    sinT = A.alloc("sinT", [128, T], F32)
    b_rope = S.buf("rope")
    _pm = A.cur
    posi = A.alloc("posi", [128, T], I32)
    A.cur = _pm
    t1 = A.alloc("t1", [128, 1024], F32)
    t2 = A.alloc("t2", [128, 1024], F32)
    b_posi = S.buf("posi")
    invf = A.alloc("invf", [128, 1], F32)
    b_invf = S.buf("invf")
    S.dma(lambda e: e.dma_start(out=invf[:], in_=invf_d), chan=b_invf, writes=[b_invf])
    S.dma(lambda e: e.dma_start(out=posi[:], in_=pos_d.partition_broadcast(128)), chan=b_posi, writes=[b_posi])
    S.op("dve", lambda e: e.tensor_copy(out=cosT[:], in_=posi[:]), reads=[b_posi], writes=[b_rope])
    S.op("dve", lambda e: e.tensor_scalar(out=cosT[:], in0=cosT[:], scalar1=invf[:, 0:1], scalar2=None, op0=ALU.mult),
         reads=[b_rope, b_invf], writes=[b_rope])
    TWO_PI = 2.0 * np.pi
    PI_LO = 3.1415925
    _om = A.cur
    tmpk = A.alloc("tmpk", [128, T], F32)
    A.cur = _om
    ot = [A.alloc("ot", [128, T], BF16) for _ in range(2)]
    b_ot = [S.buf("ot%d" % i) for i in range(2)]

    def reduce_sin(dst):
        S.op("dve", lambda e: e.tensor_scalar(out=posi[:], in0=cosT[:], scalar1=float(1.0 / TWO_PI), scalar2=None, op0=ALU.mult),
             reads=[b_rope], writes=[b_posi])
        S.op("dve", lambda e: e.tensor_copy(out=tmpk[:], in_=posi[:]), reads=[b_posi], writes=b_ot)
        S.op("dve", lambda e: e.scalar_tensor_tensor(out=dst[:], in0=tmpk[:], scalar=-float(TWO_PI), in1=cosT[:],
                                                     op0=ALU.mult, op1=ALU.add), reads=b_ot + [b_rope], writes=[b_rope])
        S.op("dve", lambda e: e.tensor_scalar(out=dst[:], in0=dst[:], scalar1=PI_LO, scalar2=-PI_LO, op0=ALU.min, op1=ALU.max),
             reads=[b_rope], writes=[b_rope])
        S.op("act", lambda e: e.activation(out=dst[:], in_=dst[:], func=AF.Sin), reads=[b_rope], writes=[b_rope])

    reduce_sin(sinT)
    S.op("dve", lambda e: e.tensor_scalar(out=cosT[:], in0=cosT[:], scalar1=float(0.5 * np.pi), scalar2=None, op0=ALU.add),
         reads=[b_rope], writes=[b_rope])
    reduce_sin(cosT)

    qsw = A.alloc("qsw", [128, 1024], F32)
    b_qsw, b_t1, b_t2 = S.buf("qsw"), S.buf("t1"), S.buf("t2")

    def epi1(i, info, pst, b_pst):
        kind, dst, row0, bcol = info
        os_ = i % 2
        if kind == "copy":
            if i % 2 == 0:
                S.op("act", lambda e: e.activation(out=ot[os_][:], in_=pst[:], func=AF.Copy),
                     reads=[b_pst], writes=[b_ot[os_]])
            else:
                S.op("dve", lambda e: e.tensor_copy(out=ot[os_][:], in_=pst[:]), reads=[b_pst], writes=[b_ot[os_]])
        elif kind == "sig":
            S.op("act", lambda e: e.activation(out=ot[os_][:], in_=pst[:], func=AF.Sigmoid, bias=bgate[:, bcol:bcol + 1]),
                 reads=[b_pst, b_bgate], writes=[b_ot[os_]])
        else:
            for hf in range(2):
                sl = slice(hf * 1024, (hf + 1) * 1024)
                S.op("act", lambda e, sl=sl: e.activation(out=qsw[0:64, :], in_=pst[64:128, sl], func=AF.Copy, scale=-1.0),
                     reads=[b_pst], writes=[b_qsw])
                S.op("act", lambda e, sl=sl: e.activation(out=qsw[64:128, :], in_=pst[0:64, sl], func=AF.Copy),
                     reads=[b_pst], writes=[b_qsw])
                S.op("act", lambda e, sl=sl: e.activation(out=t1[:], in_=pst[:, sl], func=AF.Copy),
                     reads=[b_pst], writes=[b_t1])
                S.op("dve", lambda e, sl=sl: e.tensor_tensor(out=t1[:], in0=t1[:], in1=cosT[:, sl], op=ALU.mult),
                     reads=[b_t1, b_rope], writes=[b_t1])
                S.op("dve", lambda e, sl=sl: e.tensor_tensor(out=t2[:], in0=qsw[:], in1=sinT[:, sl], op=ALU.mult),
                     reads=[b_qsw, b_rope], writes=[b_t2])
                S.op("dve", lambda e, sl=sl: e.tensor_tensor(out=ot[os_][:, sl], in0=t1[:], in1=t2[:], op=ALU.add),
                     reads=[b_t1, b_t2], writes=[b_ot[os_]])
        S.dma(lambda e: e.dma_start(out=dst[row0:row0 + 128, :], in_=ot[os_][:]), chan=b_ot[os_], reads=[b_ot[os_]])

    tiles = []
    for ct in range(128):
        c0 = ct * 128
        if ct < 16:
            info = ("copy", fT_d, ct * 128, 0)
        elif ct < 32:
            info = ("rope", qT_d, (ct - 16) * 128, 0)
        elif ct < 48:
            info = ("rope", kT_d, (ct - 32) * 128, 0)
        elif ct < 64:
            info = ("copy", vT_d, (ct - 48) * 128, 0)
        elif ct < 96:
            info = ("sig", gfT_d, (ct - 64) * 128, ct - 64)
        else:
            info = ("sig", gaT_d, (ct - 96) * 128, 32 + ct - 96)
        tiles.append((win_d, c0, False, info))
    if dbg and upto == 1.5:
        tiles = tiles[0:2] + tiles[16:18] + tiles[32:34] + tiles[48:50] + tiles[64:66] + tiles[96:98]
    gemm_fm(actT, b_act, 32, tiles, epi1, PK=8, NSTG=3)
    S.barrier()
    if upto <= 2:
        return finish(nc, S)
    A.cur = mark
    fTs = A.alloc("fTs", [128, 16, T], BF16)
    b_fTs = [S.buf("fTs%d" % i) for i in range(4)]
    load_actT(fTs, b_fTs, fT_d, 0, 16)
    ccsc = A.alloc("ccsc", [128, 2, 512], BF16)
    b_ccsc = S.buf("ccsc")
    S.dma(lambda e: e.dma_start(out=ccsc[:], in_=ccsc_d.rearrange("(j p) n -> p j n", p=128)), chan=b_ccsc, writes=[b_ccsc])
    gsb = [A.alloc("gsb", [128, 2, T], BF16) for _ in range(2)]
    b_gsb = [S.buf("gsb%d" % i) for i in range(2)]
    it = 0
    for tt in range(NT):
        gs = tt % 2
        for g in range(8):
            bi = it % 8
            it += 1
            for j in range(2):
                S.op("pe", lambda e, bi=bi, g=g, j=j, tt=tt: e.matmul(
                    bank(bi), lhsT=fTs[:, 2 * g + j, tt * 128:(tt + 1) * 128], rhs=ccsc[:, j, :],
                    start=(j == 0), stop=(j == 1)), reads=[b_fTs[tt // 4], b_ccsc], writes=[b_bank[bi]])
            if it % 2 == 0:
                S.op("act", lambda e, bi=bi, g=g, gs=gs: e.activation(
                    out=gsb[gs][:, :, g * 256:(g + 1) * 256], in_=bank(bi).rearrange("p (c n) -> p c n", c=2), func=AF.Copy),
                    reads=[b_bank[bi]], writes=[b_gsb[gs]])
            else:
                S.op("dve", lambda e, bi=bi, g=g, gs=gs: e.tensor_copy(
                    out=gsb[gs][:, :, g * 256:(g + 1) * 256], in_=bank(bi).rearrange("p (c n) -> p c n", c=2)),
                    reads=[b_bank[bi]], writes=[b_gsb[gs]])
        for c in range(2):
            S.dma(lambda e, gs=gs, tt=tt, c=c: e.dma_start(
                out=gw_d[c * 2048 + tt * 128:c * 2048 + (tt + 1) * 128, :], in_=gsb[gs][:, c, :]),
                chan=b_gsb[gs], reads=[b_gsb[gs]])
    S.barrier()

    A.cur = mark
    dfc = A.alloc("dfc", [128, 16, 1024], BF16)
    dfs = A.alloc("dfs", [128, 16, 1024], BF16)
    b_dfc = [S.buf("dfc%d" % i) for i in range(2)]
    b_dfs = [S.buf("dfs%d" % i) for i in range(2)]
    altt = A.alloc("altt", [128, 16], BF16)
    b_alt = S.buf("alt")
    S.dma(lambda e: e.dma_start(out=altt[:], in_=alt_d), chan=b_alt, writes=[b_alt])
    dv = dftw_d.rearrange("(k p) t -> p k t", p=128)
    for (dst, bd, k0) in ((dfc, b_dfc, 0), (dfs, b_dfs, 16)):
        for hf in range(2):
            S.dma(lambda e, dst=dst, hf=hf, k0=k0: e.dma_start(out=dst[:, :, hf * 512:(hf + 1) * 512],
                                                                in_=dv[:, k0:k0 + 16, hf * 512:(hf + 1) * 512]),
                  chan=bd[hf], writes=[bd[hf]])
    ot3 = [A.alloc("ot3", [128, T], BF16) for _ in range(2)]
    b_ot3 = [S.buf("ot3%d" % i) for i in range(2)]
    Bs = [A.alloc("Bs", [128, 1024], F32) for _ in range(2)]
    b_Bs = [S.buf("Bs%d" % i) for i in range(2)]
    NW3 = 3
    wb3 = [A.alloc("wb3", [128, 32, 128], BF16) for _ in range(NW3)]
    b_wb3 = [S.buf("wb3%d" % i) for i in range(NW3)]
    gwv = gw_d.rearrange("(k p) n -> p k n", p=128)

    def load3(i):
        ws = i % NW3
        S.dma(lambda e: e.dma_start(out=wb3[ws][:], in_=gwv[:, :, i * 128:(i + 1) * 128]),
              chan=b_wb3[ws], writes=[b_wb3[ws]])

    load3(0)
    load3(1)
    for i in range(16):
        if i + 2 < 16:
            load3(i + 2)
        pp = i % 2
        ws = i % NW3
        pst = ps[pp]
        for (mat, bm, k0, c0) in ((dfc, b_dfc, 0, 0), (dfs, b_dfs, 16, 1024)):
            for hf in range(2):
                for kc in range(16):
                    S.op("pe", lambda e, pst=pst, ws=ws, kc=kc, hf=hf, mat=mat, k0=k0, c0=c0: e.matmul(
                        pst[:, c0 + hf * 512:c0 + (hf + 1) * 512], lhsT=wb3[ws][:, k0 + kc, :],
                        rhs=mat[:, kc, hf * 512:(hf + 1) * 512], start=(kc == 0), stop=(kc == 15)),
                        reads=[b_wb3[ws], bm[hf]], writes=[b_ps[pp]])
        for kc in range(16):
            S.op("pe", lambda e, pst=pst, ws=ws, kc=kc: e.matmul(
                pst[:, 1024:1025], lhsT=wb3[ws][:, kc, :], rhs=altt[:, kc:kc + 1], start=False, stop=(kc == 15),
                skip_group_check=True), reads=[b_wb3[ws], b_alt], writes=[b_ps[pp]])
        os_ = i % 2
        S.op("act", lambda e, pst=pst, os_=os_: e.activation(out=Bs[os_][:], in_=pst[:, 1024:2048], func=AF.Copy),
             reads=[b_ps[pp]], writes=[b_Bs[os_]])
        S.op("dve", lambda e, pst=pst, os_=os_: e.tensor_tensor(out=ot3[os_][:, 0:1024], in0=pst[:, 0:1024], in1=Bs[os_][:],
                                                                 op=ALU.subtract),
             reads=[b_ps[pp], b_Bs[os_]], writes=[b_ot3[os_]])
        pstride = ot3[os_][:].ap[0][0]
        rev = bass.AP(ot3[os_], 2047, [[pstride, 128], [-1, 1023]])
        S.op("dve", lambda e, pst=pst, os_=os_, rev=rev: e.tensor_tensor(out=rev, in0=pst[:, 1:1024], in1=Bs[os_][:, 1:1024],
                                                                          op=ALU.add),
             reads=[b_ps[pp], b_Bs[os_]], writes=[b_ot3[os_]])
        S.op("dve", lambda e, pst=pst, os_=os_: e.tensor_copy(out=ot3[os_][:, 0:1], in_=pst[:, 0:1]),
             reads=[b_ps[pp]], writes=[b_ot3[os_]])
        S.op("dve", lambda e, os_=os_: e.tensor_copy(out=ot3[os_][:, 1024:1025], in_=Bs[os_][:, 0:1]),
             reads=[b_Bs[os_]], writes=[b_ot3[os_]])
        S.dma(lambda e, os_=os_, i=i: e.dma_start(out=yT_d[i * 128:(i + 1) * 128, :], in_=ot3[os_][:]),
              chan=b_ot3[os_], reads=[b_ot3[os_]])
    S.barrier()
    if upto <= 3:
        return finish(nc, S)

    A.cur = mark
    qs = [A.alloc("qs", [128, 2, T], BF16) for _ in range(2)]
    ks = [A.alloc("ks", [128, 2, T], BF16) for _ in range(2)]
    vs = [A.alloc("vs", [128, 2, T], BF16) for _ in range(2)]
    V1 = [A.alloc("V1", [128, 16, 257], BF16) for _ in range(2)]
    oTs = [A.alloc("oTs", [128, 2, T], BF16) for _ in range(2)]
    b_qs = [S.buf("qs%d" % i) for i in range(2)]
    b_ks = [S.buf("ks%d" % i) for i in range(2)]
    b_vs = [S.buf("vs%d" % i) for i in range(2)]
    b_V1 = [S.buf("V1%d" % i) for i in range(2)]
    b_oTs = [S.buf("oTs%d" % i) for i in range(2)]
    NPT = 3
    pt = [A.alloc("pt", [128, 512], BF16) for _ in range(NPT)]
    b_pt = [S.buf("pt%d" % i) for i in range(NPT)]
    o1 = [A.alloc("o1", [128, 256], F32) for _ in range(2)]
    of = [A.alloc("of", [128, 256], F32) for _ in range(2)]
    sq = [A.alloc("sq", [128, 256], F32) for _ in range(2)]
    onb = [A.alloc("onb", [128, 256], BF16) for _ in range(2)]
    stat = [A.alloc("stat", [128, 8], F32) for _ in range(2)]
    b_o1 = [S.buf("o1%d" % i) for i in range(2)]
    b_of = [S.buf("of%d" % i) for i in range(2)]
    b_sq = [S.buf("sq%d" % i) for i in range(2)]
    b_onb = [S.buf("onb%d" % i) for i in range(2)]
    b_stat = [S.buf("stat%d" % i) for i in range(2)]
    for par in range(2):
        S.op("pool", lambda e, par=par: e.memset(V1[par][:, :, 256:257], 1.0), writes=[b_V1[par]])
    SC = 1.0 / float(np.sqrt(128.0))

    def load_head(h):
        p_ = h % 2
        r0 = h * 256
        for (dst, b_dst, src) in ((qs, b_qs, qT_d), (ks, b_ks, kT_d), (vs, b_vs, vT_d)):
            S.dma(lambda e, dst=dst, src=src: e.dma_start(
                out=dst[p_][:], in_=src[r0:r0 + 256, :].rearrange("(c p) t -> p c t", p=128)),
                chan=b_dst[p_], writes=[b_dst[p_]])

    def build_v(h, k4):
        p_ = h % 2
        for kk in range(4):
            kt = k4 * 4 + kk
            for vc in range(2):
                S.op("pe", lambda e, kk=kk, vc=vc, kt=kt: e.transpose(
                    bank_bf(7)[:, (kk * 2 + vc) * 128:(kk * 2 + vc + 1) * 128],
                    vs[p_][:, vc, kt * 128:(kt + 1) * 128], ident[:]),
                    reads=[b_vs[p_], b_ident], writes=[b_bank[7]])
        S.op("dve", lambda e: e.tensor_copy(
            out=V1[p_][:, k4 * 4:(k4 + 1) * 4, 0:256], in_=bank_bf(7).rearrange("p (k n) -> p k n", k=4)),
            reads=[b_bank[7]], writes=[b_V1[p_]])

    def out_transposes(p_, qsub, q0):
        for vc in range(2):
            S.op("pe", lambda e, vc=vc: e.transpose(
                bank_bf(7)[:, vc * 128:(vc + 1) * 128], onb[qsub][:, vc * 128:(vc + 1) * 128], ident[:]),
                reads=[b_onb[qsub], b_ident], writes=[b_bank[7]])
        S.op("dve", lambda e: e.tensor_copy(
            out=oTs[p_][:, :, q0:q0 + 128], in_=bank_bf(7)[:, 0:256].rearrange("p (c t) -> p c t", c=2)),
            reads=[b_bank[7]], writes=[b_oTs[p_]])

    def store_head(h):
        p_ = h % 2
        r0 = h * 256
        S.dma(lambda e: e.dma_start(
            out=oT_d[r0:r0 + 256, :].rearrange("(c p) t -> p c t", p=128), in_=oTs[p_][:]),
            chan=b_oTs[p_], reads=[b_oTs[p_]])

    steps = [(h, qb, c, k2) for h in range(NH) for qb in range(8) for c in range(2) for k2 in range(8)]
    NS = len(steps)

    def score(n):
        h, qb, c, k2 = steps[n]
        p_ = h % 2
        sb = 4 + n % 3
        pi = n % NPT
        for j in range(2):
            kt = 2 * k2 + j
            S.op("pe", lambda e, j=j, kt=kt: e.matmul(bank(sb)[:, j * 256:(j + 1) * 256], lhsT=ks[p_][:, c, kt * 128:(kt + 1) * 128],
                                                      rhs=qs[p_][:, c, qb * 256:(qb + 1) * 256], start=True, stop=True),
                 reads=[b_ks[p_], b_qs[p_]], writes=[b_bank[sb]])
        S.op("act", lambda e: e.activation(out=pt[pi][:], in_=bank(sb), func=AF.Exp, scale=SC),
             reads=[b_bank[sb]], writes=[b_pt[pi]])

    load_head(0)
    for k4 in range(4):
        build_v(0, k4)
    pending = []
    seqc = [0]

    def defer(due, fn):
        seqc[0] += 1
        pending.append((due, seqc[0], fn))
        pending.sort(key=lambda t: (t[0], t[1]))

    score(0)
    score(1)
    for n in range(NS):
        h, qb, c, k2 = steps[n]
        p_ = h % 2
        first = (qb == 0 and c == 0 and k2 == 0)
        if first and h + 1 < NH:
            load_head(h + 1)
        while pending and pending[0][0] <= n:
            pending.pop(0)[2]()
        if n + 2 < NS:
            score(n + 2)
        pi = n % NPT
        for j in range(2):
            kt = 2 * k2 + j
            for qsub in range(2):
                ab = c * 2 + qsub
                S.op("pe", lambda e, ab=ab, pi=pi, qsub=qsub, kt=kt, p_=p_, j=j: e.matmul(
                    bank(ab)[:, 0:257], lhsT=pt[pi][:, j * 256 + qsub * 128:j * 256 + (qsub + 1) * 128], rhs=V1[p_][:, kt, :],
                    start=(kt == 0), stop=(kt == 15)),
                    reads=[b_pt[pi], b_V1[p_]], writes=[b_bank[ab]])
        if h + 1 < NH and c == 0 and k2 == 4 and 2 <= qb < 6:
            build_v(h + 1, qb - 2)
        kt = 15 if k2 == 7 else -1
        if kt == 15:
            for qsub in range(2):
                ab = c * 2 + qsub
                st_ = stat[qsub]
                bs = b_stat[qsub]
                S.op("dve", lambda e, ab=ab, st_=st_: e.reciprocal(out=st_[:, 0:1], in_=bank(ab)[:, 256:257]),
                     reads=[b_bank[ab]], writes=[bs])
                if c == 0:
                    S.op("dve", lambda e, ab=ab, st_=st_, qsub=qsub: e.tensor_scalar(
                        out=o1[qsub][:], in0=bank(ab)[:, 0:256], scalar1=st_[:, 0:1], scalar2=None, op0=ALU.mult),
                        reads=[b_bank[ab], bs], writes=[b_o1[qsub]])
                else:
                    S.op("dve", lambda e, st_=st_: e.tensor_tensor(out=st_[:, 1:2], in0=st_[:, 0:1], in1=NEGLAM, op=ALU.mult),
                         reads=[bs, b_small], writes=[bs])
                    S.op("dve", lambda e, ab=ab, st_=st_, qsub=qsub: e.scalar_tensor_tensor(
                        out=of[qsub][:], in0=bank(ab)[:, 0:256], scalar=st_[:, 1:2], in1=o1[qsub][:],
                        op0=ALU.mult, op1=ALU.add), reads=[b_bank[ab], bs, b_o1[qsub]], writes=[b_of[qsub]])
                    S.op("pool", lambda e, qsub=qsub: e.tensor_tensor(out=sq[qsub][:], in0=of[qsub][:], in1=of[qsub][:], op=ALU.mult),
                         reads=[b_of[qsub]], writes=[b_sq[qsub]])
                    S.op("dve", lambda e, st_=st_, qsub=qsub: e.tensor_reduce(out=st_[:, 2:3], in_=sq[qsub][:], axis=AX.X, op=ALU.add),
                         reads=[b_sq[qsub]], writes=[bs])
                    S.op("dve", lambda e, st_=st_: e.tensor_scalar(out=st_[:, 3:4], in0=st_[:, 2:3], scalar1=1.0 / 256.0,
                                                                   scalar2=EPS, op0=ALU.mult, op1=ALU.add),
                         reads=[bs], writes=[bs])
                    q0 = qb * 256 + qsub * 128

                    def stage_act(st_=st_, bs=bs):
                        S.op("act", lambda e: e.activation(out=st_[:, 4:5], in_=st_[:, 3:4], func=AF.Ln),
                             reads=[bs], writes=[bs])
                        S.op("act", lambda e: e.activation(out=st_[:, 5:6], in_=st_[:, 4:5], func=AF.Exp, scale=-0.5),
                             reads=[bs], writes=[bs])

                    def stage_onb(st_=st_, bs=bs, qsub=qsub):
                        S.op("dve", lambda e: e.scalar_tensor_tensor(
                            out=onb[qsub][:], in0=of[qsub][:], scalar=st_[:, 5:6], in1=g08[:], op0=ALU.mult, op1=ALU.mult),
                            reads=[b_of[qsub], bs, b_g08], writes=[b_onb[qsub]])

                    defer(n + 4 + qsub, stage_act)
                    defer(n + 6 + qsub, stage_onb)
                    defer(n + 9 + qsub, (lambda p_=p_, qsub=qsub, q0=q0: out_transposes(p_, qsub, q0)))
                    if qb == 7 and qsub == 1:
                        defer(n + 12, (lambda h=h: store_head(h)))
    while pending:
        pending.pop(0)[2]()
    S.barrier()
    if upto <= 4:
        return finish(nc, S)

    A.cur = mark
    yTs = A.alloc("yTs", [128, 16, T], BF16)
    b_yTs = [S.buf("yTs%d" % i) for i in range(4)]
    gt = [A.alloc("gt", [128, T], BF16) for _ in range(2)]
    tm = [A.alloc("tm", [128, T], F32) for _ in range(2)]
    b_gt = [S.buf("gt%d" % i) for i in range(2)]
    b_tm = [S.buf("tm%d" % i) for i in range(2)]

    def pre5a(i):
        sl = i % 2
        S.dma(lambda e: e.dma_start(out=gt[sl][:], in_=gfT_d[i * 128:(i + 1) * 128, :]), chan=b_gt[sl], writes=[b_gt[sl]])

    pre5a(0)

    def epi5a(i, info, pst, b_pst):
        sl = i % 2
        if i + 1 < 32:
            pre5a(i + 1)
        S.op("dve", lambda e: e.tensor_tensor(out=tm[sl][:], in0=pst[:], in1=gt[sl][:], op=ALU.mult),
             reads=[b_pst, b_gt[sl]], writes=[b_tm[sl]])
        S.dma(lambda e: e.dma_start(out=tmpT_d[i * 128:(i + 1) * 128, :], in_=tm[sl][:]), chan=b_tm[sl], reads=[b_tm[sl]])

    gemm_fm(yTs, b_yTs, 16, [(wf_d, ct * 128, False, None) for ct in range(32)], epi5a,
            act_loader=lambda: load_actT(yTs, b_yTs, yT_d, 0, 16))
    S.barrier()

    A.cur = mark
    oTa = A.alloc("oTa", [128, 16, T], BF16)
    b_oTa = [S.buf("oTa%d" % i) for i in range(4)]
    gt = [A.alloc("gt", [128, T], BF16) for _ in range(2)]
    tm = [A.alloc("tm", [128, T], F32) for _ in range(2)]
    m1 = A.alloc("m1", [128, T], F32)
    mo = [A.alloc("mo", [128, T], BF16) for _ in range(2)]
    b_gt = [S.buf("gt%d" % i) for i in range(2)]
    b_tm = [S.buf("tm%d" % i) for i in range(2)]
    b_m1 = S.buf("m1")
    b_mo = [S.buf("mo%d" % i) for i in range(2)]

    def pre5b(i):
        sl = i % 2
        S.dma(lambda e: e.dma_start(out=gt[sl][:], in_=gaT_d[i * 128:(i + 1) * 128, :]), chan=b_gt[sl], writes=[b_gt[sl]])
        S.dma(lambda e: e.dma_start(out=tm[sl][:], in_=tmpT_d[i * 128:(i + 1) * 128, :]), chan=b_tm[sl], writes=[b_tm[sl]])

    pre5b(0)

    def epi5b(i, info, pst, b_pst):
        sl = i % 2
        if i + 1 < 32:
            pre5b(i + 1)
        S.op("dve", lambda e: e.tensor_tensor(out=m1[:], in0=pst[:], in1=gt[sl][:], op=ALU.mult),
             reads=[b_pst, b_gt[sl]], writes=[b_m1])
        S.op("dve", lambda e: e.tensor_tensor(out=mo[sl][:], in0=m1[:], in1=tm[sl][:], op=ALU.add),
             reads=[b_m1, b_tm[sl]], writes=[b_mo[sl]])
        S.dma(lambda e: e.dma_start(out=mT_d[i * 128:(i + 1) * 128, :], in_=mo[sl][:]), chan=b_mo[sl], reads=[b_mo[sl]])

    gemm_fm(oTa, b_oTa, 16, [(wa_d, ct * 128, False, None) for ct in range(32)], epi5b,
            act_loader=lambda: load_actT(oTa, b_oTa, oT_d, 0, 16))
    S.barrier()

    A.cur = mark
    gemm_tm(mT_d, [16, 16], wo_d, x_d, h1_d)
    S.barrier()
    if upto <= 6:
        return finish(nc, S)

    A.cur = mark
    act2 = A.alloc("act2", [128, 32, T], BF16)
    b_act2 = S.buf("act2")
    m7 = A.cur
    norm_to_actT(h1_d, g2_d, act2, b_act2)
    S.barrier()
    A.cur = m7
    sgt = A.alloc("sgt", [128, T], F32)
    b_sgt = S.buf("sgt")
    ot8 = [A.alloc("ot8", [128, T], BF16) for _ in range(2)]
    b_ot8 = [S.buf("ot8%d" % i) for i in range(2)]

    def epi8(i, info, pst, b_pst):
        kind, j = info
        if kind == "g":
            S.op("act", lambda e: e.activation(out=sgt[:], in_=pst[:], func=AF.Silu), reads=[b_pst], writes=[b_sgt])
        else:
            os_ = j % 2
            S.op("dve", lambda e: e.tensor_tensor(out=ot8[os_][:], in0=pst[:], in1=sgt[:], op=ALU.mult),
                 reads=[b_pst, b_sgt], writes=[b_ot8[os_]])
            S.dma(lambda e: e.dma_start(out=hidT_d[j * 128:(j + 1) * 128, :], in_=ot8[os_][:]),
                  chan=b_ot8[os_], reads=[b_ot8[os_]])

    tiles8 = []
    for j in range(HC):
        tiles8.append((wg_d, j * 128, False, ("g", j)))
        tiles8.append((wu_d, j * 128, False, ("u", j)))
    gemm_fm(act2, b_act2, 32, tiles8, epi8, PK=8, NSTG=3)
    S.barrier()

    A.cur = mark
    gemm_tm(hidT_d, [22, 22, 21, 21], wd_d, h1_d, h2_d)
    S.barrier()

    A.cur = mark
    gbc = A.alloc("gbc", [128, D], F32)
    b_gbc = S.buf("gbc")
    S.dma(lambda e: e.dma_start(out=gbc[:], in_=g3_d.partition_broadcast(128)), chan=b_gbc, writes=[b_gbc])
    xt = [A.alloc("xt", [128, D], F32) for _ in range(2)]
    yo = [A.alloc("yo", [128, D], F32) for _ in range(2)]
    junk = A.alloc("junk", [128, D], BF16)
    st = A.alloc("st", [128, NT, 2], F32)
    b_xt = [S.buf("xt%d" % i) for i in range(2)]
    b_yo = [S.buf("yo%d" % i) for i in range(2)]
    b_junk, b_st = S.buf("junk"), S.buf("st")
    S.op("pool", lambda e: e.memset(st[:], 0.0), writes=[b_st])
    for tt in range(NT):
        sl = tt % 2
        S.dma(lambda e, sl=sl, tt=tt: e.dma_start(out=xt[sl][:], in_=h2_d[tt * 128:(tt + 1) * 128, :]),
              chan=b_xt[sl], writes=[b_xt[sl]])
        S.op("act", lambda e, sl=sl, tt=tt: e.activation(out=junk[:], in_=xt[sl][:], func=AF.Square, accum_out=st[:, tt, 0:1]),
             reads=[b_xt[sl], b_st], writes=[b_junk, b_st])
        S.op("dve", lambda e, tt=tt: e.tensor_scalar(out=st[:, tt, 1:2], in0=st[:, tt, 0:1], scalar1=1.0 / D, scalar2=EPS,
                                                     op0=ALU.mult, op1=ALU.add), reads=[b_st], writes=[b_st])
        S.op("act", lambda e, tt=tt: e.activation(out=st[:, tt, 1:2], in_=st[:, tt, 1:2], func=AF.Ln), reads=[b_st], writes=[b_st])
        S.op("act", lambda e, tt=tt: e.activation(out=st[:, tt, 1:2], in_=st[:, tt, 1:2], func=AF.Exp, scale=-0.5),
             reads=[b_st], writes=[b_st])
        S.op("dve", lambda e, sl=sl, tt=tt: e.scalar_tensor_tensor(out=yo[sl][:], in0=xt[sl][:], scalar=st[:, tt, 1:2],
                                                                   in1=gbc[:], op0=ALU.mult, op1=ALU.mult),
             reads=[b_xt[sl], b_st, b_gbc], writes=[b_yo[sl]])
        S.dma(lambda e, sl=sl, tt=tt: e.dma_start(out=out_d[tt * 128:(tt + 1) * 128, :], in_=yo[sl][:]),
              chan=b_yo[sl], reads=[b_yo[sl]])
    S.barrier()
    return finish(nc, S)


def finish(nc, S):
    S.finalize()
    return nc


_CACHE = {}


def kernel(**inputs):
    f32 = np.float32
    x = np.asarray(inputs["x"], dtype=f32)
    pos = np.asarray(inputs["positions"], dtype=np.int32)
    if "consts" not in _CACHE:
        _CACHE["consts"] = _consts()
    nc = build()
    dftw, ccsc, ident, invf, alt = _CACHE["consts"]
    g = lambda k: np.ascontiguousarray(np.asarray(inputs[k], dtype=f32)[0])
    bg = np.ascontiguousarray(np.asarray(inputs["b_gate"], dtype=f32)[0].reshape(2, 32, 128).transpose(2, 0, 1).reshape(128, 64))
    shared = {"norm_mix_g": g("norm_mix_g"), "w_in": g("w_in"), "bgate": bg,
              "lambda_q1": g("lambda_q1"), "lambda_k1": g("lambda_k1"), "lambda_q2": g("lambda_q2"),
              "lambda_k2": g("lambda_k2"), "subln_g": g("subln_g"), "w_fourier_out": g("w_fourier_out"),
              "w_attn_out": g("w_attn_out"), "w_out": g("w_out"), "norm_ffn_g": g("norm_ffn_g"),
              "w_ffn_gate": g("w_ffn_gate"), "w_ffn_up": g("w_ffn_up"), "w_ffn_down": g("w_ffn_down"),
              "norm_final_g": np.ascontiguousarray(np.asarray(inputs["norm_final_g"], dtype=f32)),
              "dftw": dftw, "ccsc": ccsc, "ident": ident, "invf": invf, "alt": alt}
    in_maps = []
    for b in range(8):
        m = dict(shared)
        m["x"] = np.ascontiguousarray(x[b])
        m["positions"] = np.ascontiguousarray(pos[b])
        in_maps.append(m)
    res = run_bass_kernel_spmd(nc, in_maps, core_ids=list(range(8)))
    return np.stack([np.asarray(r["out"], dtype=f32) for r in res.results], axis=0)
```

```python
import ml_dtypes
import time
import numpy as np
import concourse.bass as bass
import concourse.mybir as mybir
from concourse.bass_utils import run_bass_kernel_spmd

F32 = mybir.dt.float32
BF16 = mybir.dt.bfloat16
I32 = mybir.dt.int32
AF = mybir.ActivationFunctionType
ALU = mybir.AluOpType
AX = mybir.AxisListType

COMPUTE = ("pe", "act", "dve", "pool")


class Buf:
    __slots__ = ("name", "lw", "rd", "sem")

    def __init__(self, name):
        self.name = name
        self.lw = None
        self.rd = {}
        self.sem = None


class Ins:
    __slots__ = ("stream", "src", "fn", "deps", "sig", "sigval", "dma")

    def __init__(self, stream, src, fn, dma):
        self.stream = stream
        self.src = src
        self.fn = fn
        self.deps = []
        self.sig = False
        self.sigval = 0
        self.dma = dma


class Sched:
    def __init__(self, nc):
        self.nc = nc
        self.streams = {"pe": [], "act": [], "dve": [], "pool": [], "sp": []}
        self.all = []
        self.sems = {e: nc.alloc_semaphore("sem_" + e) for e in COMPUTE}
        self.dma_sem_pool = []
        self.ndma = 0
        self.last = {}
        self.bufs_with_sem = []

    def buf(self, name):
        return Buf(name)

    def _dma_sem(self, b):
        if b.sem is None:
            if self.dma_sem_pool:
                b.sem = self.dma_sem_pool.pop()
            else:
                self.ndma += 1
                b.sem = self.nc.alloc_semaphore("dsem%d" % self.ndma)
            self.bufs_with_sem.append(b)
        return b.sem

    def _track(self, ins, reads, writes):
        deps = []
        for b in reads:
            if b.lw is not None:
                deps.append((b.lw, "raw"))
        for b in writes:
            if b.lw is not None:
                deps.append((b.lw, "waw"))
            for r in b.rd.values():
                deps.append((r, "war"))
        for d, kind in deps:
            if d is ins:
                continue
            if (not ins.dma) and (not d.dma) and d.src == ins.src:
                if ins.src == "pe" or kind != "raw":
                    continue
            d.sig = True
            ins.deps.append(d)
        for b in reads:
            b.rd[ins.src] = ins
        for b in writes:
            b.lw = ins
            b.rd = {}

    def op(self, eng, fn, reads=(), writes=()):
        ins = Ins(eng, eng, fn, False)
        self._track(ins, reads, writes)
        self.streams[eng].append(ins)
        self.all.append(ins)
        self.last[eng] = ins
        return ins

    def dma(self, fn, chan, reads=(), writes=(), q="sp"):
        sem = self._dma_sem(chan)
        ins = Ins(q, sem, fn, True)
        ins.sig = True
        self._track(ins, reads, writes)
        self.streams[q].append(ins)
        self.all.append(ins)
        self.last[sem] = ins
        return ins

    def barrier(self):
        lasts = list(self.last.values())
        for d in lasts:
            d.sig = True
        for s in self.streams:
            ins = Ins(s, None, None, False)
            ins.deps = list(lasts)
            self.streams[s].append(ins)
            self.all.append(ins)
        for b in self.bufs_with_sem:
            self.dma_sem_pool.append(b.sem)
            b.sem = None
        self.bufs_with_sem = []

    def finalize(self):
        nc = self.nc
        cnt = {}
        for ins in self.all:
            if ins.src is None:
                continue
            if ins.sig:
                cnt[ins.src] = cnt.get(ins.src, 0) + (16 if ins.dma else 1)
                ins.sigval = cnt[ins.src]
        sems = self.sems
        streams = self.streams

        def replay(sname):
            def run(e):
                waited = {}
                for ins in streams[sname]:
                    need = {}
                    for d in ins.deps:
                        k = d.src
                        if d.sigval > need.get(k, 0):
                            need[k] = d.sigval
                    for k, v in need.items():
                        if v > waited.get(k, 0):
                            waited[k] = v
                            e.wait_ge(sems[k] if isinstance(k, str) else k, v)
                    if ins.fn is None:
                        continue
                    bi = ins.fn(e)
                    if ins.sig:
                        if ins.dma:
                            bi.then_inc(ins.src, 16)
                        else:
                            bi.then_inc(sems[ins.src], 1)
            return run

        with nc.Block() as block:
            block.tensor(replay("pe"))
            block.scalar(replay("act"))
            block.vector(replay("dve"))
            block.gpsimd(replay("pool"))
            block.sync(replay("sp"))


class SbufAlloc:
    def __init__(self, nc, base=None, top=None):
        self.nc = nc
        self.base = ((nc.sbuf_base + 63) // 64) * 64 if base is None else base
        self.top = nc.sbuf_top if top is None else top
        self.cur = self.base
        self.n = 0

    def reset(self):
        self.cur = self.base

    def alloc(self, name, shape, dtype):
        esz = 4 if dtype in (F32, I32) else 2
        per = esz
        for s in shape[1:]:
            per *= s
        off = self.cur
        self.cur = ((off + per + 63) // 64) * 64
        assert self.cur <= self.top, ("SBUF overflow", name, self.cur, self.top)
        self.n += 1
        return self.nc.alloc_sbuf_tensor_at("%s_%d" % (name, self.n), list(shape), dtype, offset=off)


D = 4096
T = 2048
NT = T // 128
FW = 2048
AW = 2048
HID = 11008
HC = HID // 128
NH = 8
EPS = 1e-6
LAM_INIT = 0.2


def _consts():
    s = np.arange(2048, dtype=np.float64)
    sp = np.arange(1024, dtype=np.float64)
    ang = 2.0 * np.pi * np.outer(s, sp) / 2048.0
    dftw = np.concatenate([np.cos(ang), np.sin(ang)], 0) / np.sqrt(2048.0)
    alt = np.tile(((-1.0) ** np.arange(128)).reshape(128, 1), (1, 16)) / np.sqrt(2048.0)
    c = np.arange(256, dtype=np.float64)
    angc = 2.0 * np.pi * np.outer(c, c) / 256.0
    ccsc = np.concatenate([np.cos(angc), np.sin(angc)], 1) / 16.0
    ident = np.eye(128)
    j = np.arange(0, 128, 2, dtype=np.float32) / np.float32(128)
    inv = (np.float32(10000.0) ** (-j)).astype(np.float32)
    invf = np.concatenate([inv, inv]).reshape(128, 1).astype(np.float32)
    return (dftw.astype(ml_dtypes.bfloat16), ccsc.astype(ml_dtypes.bfloat16),
            ident.astype(ml_dtypes.bfloat16), invf, alt.astype(ml_dtypes.bfloat16))


class K:
    pass


def build(upto=99, dbg=False):
    nc = bass.Bass("TRN2", target_bir_lowering=False)
    S = Sched(nc)

    def din(name, shape, dt):
        return nc.dram_tensor(name, list(shape), dt, kind="ExternalInput").ap()

    def dscr(name, shape, dt):
        return nc.dram_tensor(name, list(shape), dt, kind=("ExternalOutput" if dbg else "Internal")).ap()

    x_d = din("x", [T, D], F32)
    pos_d = din("positions", [T], I32)
    g1_d = din("norm_mix_g", [D], F32)
    win_d = din("w_in", [D, 16384], F32)
    bg_d = din("bgate", [128, 64], F32)
    lq1_d = din("lambda_q1", [128], F32)
    lk1_d = din("lambda_k1", [128], F32)
    lq2_d = din("lambda_q2", [128], F32)
    lk2_d = din("lambda_k2", [128], F32)
    sg_d = din("subln_g", [256], F32)
    wf_d = din("w_fourier_out", [FW, D], F32)
    wa_d = din("w_attn_out", [AW, D], F32)
    wo_d = din("w_out", [D, D], F32)
    g2_d = din("norm_ffn_g", [D], F32)
    wg_d = din("w_ffn_gate", [D, HID], F32)
    wu_d = din("w_ffn_up", [D, HID], F32)
    wd_d = din("w_ffn_down", [HID, D], F32)
    g3_d = din("norm_final_g", [D], F32)
    dftw_d = din("dftw", [4096, 1024], BF16)
    alt_d = din("alt", [128, 16], BF16)
    ccsc_d = din("ccsc", [256, 512], BF16)
    ident_d = din("ident", [128, 128], BF16)
    invf_d = din("invf", [128, 1], F32)
    out_d = nc.dram_tensor("out", [T, D], F32, kind="ExternalOutput").ap()

    fT_d = dscr("fT", [FW, T], BF16)
    qT_d = dscr("qT", [AW, T], BF16)
    kT_d = dscr("kT", [AW, T], BF16)
    vT_d = dscr("vT", [AW, T], BF16)
    gfT_d = dscr("gfT", [D, T], BF16)
    gaT_d = dscr("gaT", [D, T], BF16)
    gw_d = dscr("gw", [4096, 2048], BF16)
    yT_d = dscr("yT", [FW, T], BF16)
    oT_d = dscr("oT", [AW, T], BF16)
    tmpT_d = dscr("tmpT", [D, T], F32)
    mT_d = dscr("mT", [D, T], BF16)
    h1_d = dscr("h1", [T, D], F32)
    hidT_d = dscr("hidT", [HID, T], BF16)
    h2_d = dscr("h2", [T, D], F32)

    A = SbufAlloc(nc)
    ps = [nc.alloc_psum_tensor("psA", [128, 2048], F32), nc.alloc_psum_tensor("psB", [128, 2048], F32)]
    psb = [p.bitcast(BF16) for p in ps]
    b_ps = [S.buf("psA"), S.buf("psB")]
    b_bank = [S.buf("bank%d" % i) for i in range(8)]

    def bank(i):
        return ps[i // 4][:, (i % 4) * 512:(i % 4 + 1) * 512]

    def bank_bf(i):
        return psb[i // 4][:, (i % 4) * 1024:(i % 4 + 1) * 1024]

    ident = A.alloc("ident", [128, 128], BF16)
    b_ident = S.buf("ident")
    S.dma(lambda e: e.dma_start(out=ident[:], in_=ident_d), chan=b_ident, writes=[b_ident])
    small = A.alloc("small", [128, 64], F32)
    b_small = S.buf("small")
    bgate = A.alloc("bgate", [128, 64], F32)
    b_bgate = S.buf("bgate")
    S.dma(lambda e: e.dma_start(out=bgate[:], in_=bg_d), chan=b_bgate, writes=[b_bgate])
    g08 = A.alloc("g08", [128, 256], F32)
    b_g08 = S.buf("g08")
    S.dma(lambda e: e.dma_start(out=g08[:], in_=sg_d.partition_broadcast(128)), chan=b_g08, writes=[b_g08])
    S.op("dve", lambda e: e.tensor_scalar(out=g08[:], in0=g08[:], scalar1=1.0 - LAM_INIT, scalar2=None, op0=ALU.mult),
         reads=[b_g08], writes=[b_g08])
    lam4 = A.alloc("lam4", [128, 4, 128], F32)
    b_lam4 = S.buf("lam4")
    for i, dd in enumerate([lq1_d, lk1_d, lq2_d, lk2_d]):
        S.dma(lambda e, i=i, dd=dd: e.dma_start(out=lam4[:, i, :], in_=dd.partition_broadcast(128)),
              chan=b_lam4, writes=[b_lam4])
    lamp = A.alloc("lamp", [128, 2, 128], F32)
    b_lamp = S.buf("lamp")
    S.op("dve", lambda e: e.tensor_tensor(out=lamp[:, 0, :], in0=lam4[:, 0, :], in1=lam4[:, 1, :], op=ALU.mult),
         reads=[b_lam4], writes=[b_lamp])
    S.op("dve", lambda e: e.tensor_tensor(out=lamp[:, 1, :], in0=lam4[:, 2, :], in1=lam4[:, 3, :], op=ALU.mult),
         reads=[b_lam4, b_lamp], writes=[b_lamp])
    S.op("dve", lambda e: e.tensor_reduce(out=small[:, 0:2], in_=lamp[:], axis=AX.X, op=ALU.add),
         reads=[b_lamp], writes=[b_small])
    S.op("act", lambda e: e.activation(out=small[:, 2:4], in_=small[:, 0:2], func=AF.Exp),
         reads=[b_small], writes=[b_small])
    S.op("dve", lambda e: e.tensor_tensor(out=small[:, 4:5], in0=small[:, 3:4], in1=small[:, 2:3], op=ALU.subtract),
         reads=[b_small], writes=[b_small])
    S.op("dve", lambda e: e.tensor_scalar(out=small[:, 5:6], in0=small[:, 4:5], scalar1=-LAM_INIT, scalar2=None, op0=ALU.add),
         reads=[b_small], writes=[b_small])
    NEGLAM = small[:, 5:6]
    A.base = A.cur

    K.nc, K.S, K.A = nc, S, A

    def norm_to_actT(src_d, g_d, actT, b_act):
        gbc = A.alloc("gbc", [128, D], F32)
        b_gbc = S.buf("gbc")
        S.dma(lambda e: e.dma_start(out=gbc[:], in_=g_d.partition_broadcast(128)), chan=b_gbc, writes=[b_gbc])
        xt = [A.alloc("xt", [128, D], F32) for _ in range(2)]
        b_xt = [S.buf("xt%d" % i) for i in range(2)]
        junk = A.alloc("junk", [128, D], BF16)
        b_junk = S.buf("junk")
        ub = [A.alloc("ub", [128, D], BF16) for _ in range(2)]
        b_ub = [S.buf("ub%d" % i) for i in range(2)]
        st = A.alloc("st", [128, NT, 2], F32)
        b_stl = [S.buf("st%d" % i) for i in range(NT)]
        S.op("pool", lambda e: e.memset(st[:], 0.0), writes=b_stl)
        for tt in range(NT):
            sl = tt % 2
            b_st = b_stl[tt]
            S.dma(lambda e, sl=sl, tt=tt: e.dma_start(out=xt[sl][:], in_=src_d[tt * 128:(tt + 1) * 128, :]),
                  chan=b_xt[sl], writes=[b_xt[sl]])
            S.op("act", lambda e, sl=sl, tt=tt: e.activation(out=junk[:], in_=xt[sl][:], func=AF.Square,
                                                             accum_out=st[:, tt, 0:1]),
                 reads=[b_xt[sl], b_st], writes=[b_junk, b_st])
            S.op("pool", lambda e, tt=tt: e.tensor_scalar(out=st[:, tt, 1:2], in0=st[:, tt, 0:1], scalar1=1.0 / D,
                                                          scalar2=EPS, op0=ALU.mult, op1=ALU.add),
                 reads=[b_st], writes=[b_st])
            S.op("act", lambda e, tt=tt: e.activation(out=st[:, tt, 1:2], in_=st[:, tt, 1:2], func=AF.Ln),
                 reads=[b_st], writes=[b_st])
            S.op("act", lambda e, tt=tt: e.activation(out=st[:, tt, 1:2], in_=st[:, tt, 1:2], func=AF.Exp, scale=-0.5),
                 reads=[b_st], writes=[b_st])
            S.op("dve", lambda e, sl=sl, tt=tt: e.scalar_tensor_tensor(out=ub[sl][:], in0=xt[sl][:], scalar=st[:, tt, 1:2],
                                                                       in1=gbc[:], op0=ALU.mult, op1=ALU.mult),
                 reads=[b_xt[sl], b_st, b_gbc], writes=[b_ub[sl]])
            for c8 in range(4):
                bi = (tt * 4 + c8) % 8
                for j in range(8):
                    kc = c8 * 8 + j
                    S.op("pe", lambda e, bi=bi, j=j, kc=kc, sl=sl: e.transpose(
                        bank_bf(bi)[:, j * 128:(j + 1) * 128], ub[sl][:, kc * 128:(kc + 1) * 128], ident[:]),
                        reads=[b_ub[sl], b_ident], writes=[b_bank[bi]])
                eng = "dve"
                if eng == "act":
                    S.op("act", lambda e, bi=bi, c8=c8, tt=tt: e.activation(
                        out=actT[:, c8 * 8:(c8 + 1) * 8, tt * 128:(tt + 1) * 128],
                        in_=bank_bf(bi).rearrange("p (k t) -> p k t", k=8), func=AF.Copy),
                        reads=[b_bank[bi]], writes=[b_act])
                else:
                    S.op("dve", lambda e, bi=bi, c8=c8, tt=tt: e.tensor_copy(
                        out=actT[:, c8 * 8:(c8 + 1) * 8, tt * 128:(tt + 1) * 128],
                        in_=bank_bf(bi).rearrange("p (k t) -> p k t", k=8)),
                        reads=[b_bank[bi]], writes=[b_act])

    def load_actT(actT, b_act, src_d, kc0, KC):
        v = src_d.rearrange("(k p) t -> p k t", p=128)
        for tb in range(4):
            bb = b_act[tb] if isinstance(b_act, list) else b_act
            S.dma(lambda e, tb=tb: e.dma_start(out=actT[:, 0:KC, tb * 512:(tb + 1) * 512],
                                               in_=v[:, kc0:kc0 + KC, tb * 512:(tb + 1) * 512]),
                  chan=bb, writes=[bb])

    def gemm_fm(actT, b_act, KC, tiles, epi, kc0=0, PK=16, NW=3, NSTG=2, DIST=2, act_loader=None):
        stg = [A.alloc("stg", [128, PK, 128], F32) for _ in range(NSTG)]
        b_stg = [S.buf("stg%d" % i) for i in range(NSTG)]
        wb = [A.alloc("wb", [128, KC, 128], BF16) for _ in range(NW)]
        b_wb = [S.buf("wb%d" % i) for i in range(NW)]
        state = {"pc": 0}

        def load(i):
            w_ap, c0, isbf, _ = tiles[i]
            ws = i % NW
            wv = w_ap.rearrange("(k p) n -> p k n", p=128)
            if isbf:
                S.dma(lambda e: e.dma_start(out=wb[ws][:], in_=wv[:, kc0:kc0 + KC, c0:c0 + 128]),
                      chan=b_wb[ws], writes=[b_wb[ws]])
                return
            for p0 in range(0, KC, PK):
                n = min(PK, KC - p0)
                sl = state["pc"] % NSTG
                state["pc"] += 1
                S.dma(lambda e, sl=sl, p0=p0, n=n: e.dma_start(
                    out=stg[sl][:, 0:n, :], in_=wv[:, kc0 + p0:kc0 + p0 + n, c0:c0 + 128]),
                    chan=b_stg[sl], writes=[b_stg[sl]])
                S.op("pool", lambda e, sl=sl, p0=p0, n=n: e.tensor_copy(out=wb[ws][:, p0:p0 + n, :], in_=stg[sl][:, 0:n, :]),
                     reads=[b_stg[sl]], writes=[b_wb[ws]])

        n = len(tiles)
        for i in range(min(DIST, n)):
            load(i)
        if act_loader is not None:
            act_loader()
        for i in range(n):
            if i + DIST < n:
                load(i + DIST)
            pp = i % 2
            ws = i % NW
            for tb in range(4):
                for kc in range(KC):
                    S.op("pe", lambda e, pp=pp, tb=tb, ws=ws, kc=kc: e.matmul(
                        ps[pp][:, tb * 512:(tb + 1) * 512], lhsT=wb[ws][:, kc, :],
                        rhs=actT[:, kc, tb * 512:(tb + 1) * 512], start=(kc == 0), stop=(kc == KC - 1)),
                        reads=[b_wb[ws], (b_act[tb] if isinstance(b_act, list) else b_act)], writes=[b_ps[pp]])
            epi(i, tiles[i][3], ps[pp], b_ps[pp])

    def gemm_tm(src_d, KCs, w_ap, resid0_d, out_d_, NW=2, NSTG=3, PK=4):
        KCm = max(KCs)
        NSL = 8
        aT = A.alloc("aT", [128, KCm, T], BF16)
        b_aT = [S.buf("aT%d" % i) for i in range(NSL)]
        stg = [A.alloc("stg", [128, PK, 512], F32) for _ in range(NSTG)]
        b_stg = [S.buf("stg%d" % i) for i in range(NSTG)]
        wb = [A.alloc("wb", [128, KCm, 512], BF16) for _ in range(NW)]
        b_wb = [S.buf("wb%d" % i) for i in range(NW)]
        NR = 8
        rt = [A.alloc("rt", [128, 512], F32) for _ in range(NR)]
        b_rt = [S.buf("rt%d" % i) for i in range(NR)]
        dt = {}
        for tt in range(NT):
            for cb in range(8):
                dt[(tt, cb)] = S.buf("dt")
        wv = w_ap.rearrange("(k p) n -> p k n", p=128)
        sv = src_d.rearrange("(k p) t -> p k t", p=128)
        state = {"pc": 0}
        kc0s = [sum(KCs[:i]) for i in range(len(KCs))]

        def load_slice(pi_, sl_):
            KC, kc0 = KCs[pi_], kc0s[pi_]
            S.dma(lambda e: e.dma_start(out=aT[:, 0:KC, sl_ * 256:(sl_ + 1) * 256],
                                        in_=sv[:, kc0:kc0 + KC, sl_ * 256:(sl_ + 1) * 256]),
                  chan=b_aT[sl_], writes=[b_aT[sl_]])

        def pieces_of(pi_):
            KC = KCs[pi_]
            return [(p0, min(PK, KC - p0)) for p0 in range(0, KC, PK)]

        def load_piece(gb, k):
            pi_, cb = gb // 8, gb % 8
            ws = gb % NW
            p0, n = pieces_of(pi_)[k]
            kc0 = kc0s[pi_]
            sl = state["pc"] % NSTG
            state["pc"] += 1
            S.dma(lambda e: e.dma_start(
                out=stg[sl][:, 0:n, :], in_=wv[:, kc0 + p0:kc0 + p0 + n, cb * 512:(cb + 1) * 512]),
                chan=b_stg[sl], writes=[b_stg[sl]])
            S.op("pool", lambda e: e.tensor_copy(out=wb[ws][:, p0:p0 + n, :], in_=stg[sl][:, 0:n, :]),
                 reads=[b_stg[sl]], writes=[b_wb[ws]])

        NP = len(KCs)
        for k in range(len(pieces_of(0))):
            load_piece(0, k)
        for sl_ in range(NSL):
            load_slice(0, sl_)
        LA = 3
        NIT = NP * 8 * NT

        def load_resid(j):
            gb_, tt_ = j // NT, j % NT
            pj, cbj = gb_ // 8, gb_ % 8
            rsj = j % NR
            src = resid0_d if pj == 0 else out_d_
            rd = [dt[(tt_, cbj)]] if pj > 0 else []
            S.dma(lambda e: e.dma_start(
                out=rt[rsj][:], in_=src[tt_ * 128:(tt_ + 1) * 128, cbj * 512:(cbj + 1) * 512]),
                chan=b_rt[rsj], reads=rd, writes=[b_rt[rsj]])

        for j in range(LA):
            load_resid(j)
        it = 0
        for gb in range(NP * 8):
            pi_, cb = gb // 8, gb % 8
            KC = KCs[pi_]
            ws = gb % NW
            for tt in range(NT):
                if gb + 1 < NP * 8 and tt % 2 == 0 and tt // 2 < len(pieces_of((gb + 1) // 8)):
                    load_piece(gb + 1, tt // 2)
                if it + LA < NIT:
                    load_resid(it + LA)
                bi = it % 8
                rs = it % NR
                it += 1
                for kc in range(KC):
                    S.op("pe", lambda e, bi=bi, ws=ws, kc=kc, tt=tt, KC=KC: e.matmul(
                        bank(bi), lhsT=aT[:, kc, tt * 128:(tt + 1) * 128], rhs=wb[ws][:, kc, :],
                        start=(kc == 0), stop=(kc == KC - 1)),
                        reads=[b_wb[ws], b_aT[tt // 2]], writes=[b_bank[bi]])
                S.op("dve", lambda e, bi=bi, rs=rs: e.tensor_tensor(out=rt[rs][:], in0=bank(bi), in1=rt[rs][:], op=ALU.add),
                     reads=[b_bank[bi], b_rt[rs]], writes=[b_rt[rs]])
                S.dma(lambda e, rs=rs, tt=tt, cb=cb: e.dma_start(
                    out=out_d_[tt * 128:(tt + 1) * 128, cb * 512:(cb + 1) * 512], in_=rt[rs][:]),
                    chan=b_rt[rs], reads=[b_rt[rs]], writes=[dt[(tt, cb)]])
                if cb == 7 and pi_ + 1 < NP and tt % 2 == 1:
                    load_slice(pi_ + 1, tt // 2)

    K.norm_to_actT, K.load_actT, K.gemm_fm, K.gemm_tm = norm_to_actT, load_actT, gemm_fm, gemm_tm
    K.bank, K.bank_bf, K.b_bank, K.ps, K.psb, K.b_ps = bank, bank_bf, b_bank, ps, psb, b_ps

    mark = A.cur
    actT = A.alloc("actT", [128, 32, T], BF16)
    b_act = S.buf("actT")
    mark1 = A.cur
    norm_to_actT(x_d, g1_d, actT, b_act)
    S.barrier()
    if upto <= 0:
        return finish(nc, S)

    A.cur = mark1
    cosT = A.alloc("cosT", [128, T], F32)
    sinT = A.alloc("sinT", [128, T], F32)
    b_rope = S.buf("rope")
    _pm = A.cur
    posi = A.alloc("posi", [128, T], I32)
    A.cur = _pm
    t1 = A.alloc("t1", [128, 1024], F32)
    t2 = A.alloc("t2", [128, 1024], F32)
    b_posi = S.buf("posi")
    invf = A.alloc("invf", [128, 1], F32)
    b_invf = S.buf("invf")
    S.dma(lambda e: e.dma_start(out=invf[:], in_=invf_d), chan=b_invf, writes=[b_invf])
    S.dma(lambda e: e.dma_start(out=posi[:], in_=pos_d.partition_broadcast(128)), chan=b_posi, writes=[b_posi])
    S.op("dve", lambda e: e.tensor_copy(out=cosT[:], in_=posi[:]), reads=[b_posi], writes=[b_rope])
    S.op("dve", lambda e: e.tensor_scalar(out=cosT[:], in0=cosT[:], scalar1=invf[:, 0:1], scalar2=None, op0=ALU.mult),
         reads=[b_rope, b_invf], writes=[b_rope])
    TWO_PI = 2.0 * np.pi
    PI_LO = 3.1415925
    _om = A.cur
    tmpk = A.alloc("tmpk", [128, T], F32)
    A.cur = _om
    ot = [A.alloc("ot", [128, T], BF16) for _ in range(2)]
    b_ot = [S.buf("ot%d" % i) for i in range(2)]

    def reduce_sin(dst):
        S.op("dve", lambda e: e.tensor_scalar(out=posi[:], in0=cosT[:], scalar1=float(1.0 / TWO_PI), scalar2=None, op0=ALU.mult),
             reads=[b_rope], writes=[b_posi])
        S.op("dve", lambda e: e.tensor_copy(out=tmpk[:], in_=posi[:]), reads=[b_posi], writes=b_ot)
        S.op("dve", lambda e: e.scalar_tensor_tensor(out=dst[:], in0=tmpk[:], scalar=-float(TWO_PI), in1=cosT[:],
                                                     op0=ALU.mult, op1=ALU.add), reads=b_ot + [b_rope], writes=[b_rope])
        S.op("dve", lambda e: e.tensor_scalar(out=dst[:], in0=dst[:], scalar1=PI_LO, scalar2=-PI_LO, op0=ALU.min, op1=ALU.max),
             reads=[b_rope], writes=[b_rope])
        S.op("act", lambda e: e.activation(out=dst[:], in_=dst[:], func=AF.Sin), reads=[b_rope], writes=[b_rope])

    reduce_sin(sinT)
    S.op("dve", lambda e: e.tensor_scalar(out=cosT[:], in0=cosT[:], scalar1=float(0.5 * np.pi), scalar2=None, op0=ALU.add),
         reads=[b_rope], writes=[b_rope])
    reduce_sin(cosT)

    qsw = A.alloc("qsw", [128, 1024], F32)
    b_qsw, b_t1, b_t2 = S.buf("qsw"), S.buf("t1"), S.buf("t2")

    def epi1(i, info, pst, b_pst):
        kind, dst, row0, bcol = info
        os_ = i % 2
        if kind == "copy":
            if i % 2 == 0:
                S.op("act", lambda e: e.activation(out=ot[os_][:], in_=pst[:], func=AF.Copy),
                     reads=[b_pst], writes=[b_ot[os_]])
            else:
                S.op("dve", lambda e: e.tensor_copy(out=ot[os_][:], in_=pst[:]), reads=[b_pst], writes=[b_ot[os_]])
        elif kind == "sig":
            S.op("act", lambda e: e.activation(out=ot[os_][:], in_=pst[:], func=AF.Sigmoid, bias=bgate[:, bcol:bcol + 1]),
                 reads=[b_pst, b_bgate], writes=[b_ot[os_]])
        else:
            for hf in range(2):
                sl = slice(hf * 1024, (hf + 1) * 1024)
                S.op("act", lambda e, sl=sl: e.activation(out=qsw[0:64, :], in_=pst[64:128, sl], func=AF.Copy, scale=-1.0),
                     reads=[b_pst], writes=[b_qsw])
                S.op("act", lambda e, sl=sl: e.activation(out=qsw[64:128, :], in_=pst[0:64, sl], func=AF.Copy),
                     reads=[b_pst], writes=[b_qsw])
                S.op("act", lambda e, sl=sl: e.activation(out=t1[:], in_=pst[:, sl], func=AF.Copy),
                     reads=[b_pst], writes=[b_t1])
                S.op("dve", lambda e, sl=sl: e.tensor_tensor(out=t1[:], in0=t1[:], in1=cosT[:, sl], op=ALU.mult),
                     reads=[b_t1, b_rope], writes=[b_t1])
                S.op("dve", lambda e, sl=sl: e.tensor_tensor(out=t2[:], in0=qsw[:], in1=sinT[:, sl], op=ALU.mult),
                     reads=[b_qsw, b_rope], writes=[b_t2])
                S.op("dve", lambda e, sl=sl: e.tensor_tensor(out=ot[os_][:, sl], in0=t1[:], in1=t2[:], op=ALU.add),
                     reads=[b_t1, b_t2], writes=[b_ot[os_]])
        S.dma(lambda e: e.dma_start(out=dst[row0:row0 + 128, :], in_=ot[os_][:]), chan=b_ot[os_], reads=[b_ot[os_]])

    tiles = []
    for ct in range(128):
        c0 = ct * 128
        if ct < 16:
            info = ("copy", fT_d, ct * 128, 0)
        elif ct < 32:
            info = ("rope", qT_d, (ct - 16) * 128, 0)
        elif ct < 48:
            info = ("rope", kT_d, (ct - 32) * 128, 0)
        elif ct < 64:
            info = ("copy", vT_d, (ct - 48) * 128, 0)
        elif ct < 96:
            info = ("sig", gfT_d, (ct - 64) * 128, ct - 64)
        else:
            info = ("sig", gaT_d, (ct - 96) * 128, 32 + ct - 96)
        tiles.append((win_d, c0, False, info))
    if dbg and upto == 1.5:
        tiles = tiles[0:2] + tiles[16:18] + tiles[32:34] + tiles[48:50] + tiles[64:66] + tiles[96:98]
    gemm_fm(actT, b_act, 32, tiles, epi1, PK=8, NSTG=3)
    S.barrier()
    if upto <= 2:
        return finish(nc, S)
    A.cur = mark
    dfc = A.alloc("dfc", [128, 16, 1024], BF16)
    dfs = A.alloc("dfs", [128, 16, 1024], BF16)
    altt = A.alloc("altt", [128, 16], BF16)
    b_dfc = [S.buf("dfc%d" % i) for i in range(2)]
    b_dfs = [S.buf("dfs%d" % i) for i in range(2)]
    b_alt = S.buf("alt")
    mark3 = A.cur
    fTs = A.alloc("fTs", [128, 16, T], BF16)
    b_fTs = [S.buf("fTs%d" % i) for i in range(4)]
    load_actT(fTs, b_fTs, fT_d, 0, 16)
    ccsc = A.alloc("ccsc", [128, 2, 512], BF16)
    b_ccsc = S.buf("ccsc")
    S.dma(lambda e: e.dma_start(out=ccsc[:], in_=ccsc_d.rearrange("(j p) n -> p j n", p=128)), chan=b_ccsc, writes=[b_ccsc])
    gsb = [A.alloc("gsb", [128, 2, T], BF16) for _ in range(2)]
    b_gsb = [S.buf("gsb%d" % i) for i in range(2)]
    S.dma(lambda e: e.dma_start(out=altt[:], in_=alt_d), chan=b_alt, writes=[b_alt])
    dv = dftw_d.rearrange("(k p) t -> p k t", p=128)
    for (dst, bd, k0) in ((dfc, b_dfc, 0), (dfs, b_dfs, 16)):
        for hf in range(2):
            S.dma(lambda e, dst=dst, hf=hf, k0=k0: e.dma_start(out=dst[:, :, hf * 512:(hf + 1) * 512],
                                                                in_=dv[:, k0:k0 + 16, hf * 512:(hf + 1) * 512]),
                  chan=bd[hf], writes=[bd[hf]])
    it = 0
    for tt in range(NT):
        gs = tt % 2
        for g in range(8):
            bi = it % 8
            it += 1
            for j in range(2):
                S.op("pe", lambda e, bi=bi, g=g, j=j, tt=tt: e.matmul(
                    bank(bi), lhsT=fTs[:, 2 * g + j, tt * 128:(tt + 1) * 128], rhs=ccsc[:, j, :],
                    start=(j == 0), stop=(j == 1)), reads=[b_fTs[tt // 4], b_ccsc], writes=[b_bank[bi]])
            if it % 2 == 0:
                S.op("act", lambda e, bi=bi, g=g, gs=gs: e.activation(
                    out=gsb[gs][:, :, g * 256:(g + 1) * 256], in_=bank(bi).rearrange("p (c n) -> p c n", c=2), func=AF.Copy),
                    reads=[b_bank[bi]], writes=[b_gsb[gs]])
            else:
                S.op("dve", lambda e, bi=bi, g=g, gs=gs: e.tensor_copy(
                    out=gsb[gs][:, :, g * 256:(g + 1) * 256], in_=bank(bi).rearrange("p (c n) -> p c n", c=2)),
                    reads=[b_bank[bi]], writes=[b_gsb[gs]])
        for c in range(2):
            S.dma(lambda e, gs=gs, tt=tt, c=c: e.dma_start(
                out=gw_d[c * 2048 + tt * 128:c * 2048 + (tt + 1) * 128, :], in_=gsb[gs][:, c, :]),
                chan=b_gsb[gs], reads=[b_gsb[gs]])
    S.barrier()

    A.cur = mark3
    ot3 = [A.alloc("ot3", [128, T], BF16) for _ in range(2)]
    b_ot3 = [S.buf("ot3%d" % i) for i in range(2)]
    Bs = [A.alloc("Bs", [128, 1024], F32) for _ in range(2)]
    b_Bs = [S.buf("Bs%d" % i) for i in range(2)]
    NW3 = 3
    wb3 = [A.alloc("wb3", [128, 32, 128], BF16) for _ in range(NW3)]
    b_wb3 = [S.buf("wb3%d" % i) for i in range(NW3)]
    gwv = gw_d.rearrange("(k p) n -> p k n", p=128)

    def load3(i):
        ws = i % NW3
        S.dma(lambda e: e.dma_start(out=wb3[ws][:], in_=gwv[:, :, i * 128:(i + 1) * 128]),
              chan=b_wb3[ws], writes=[b_wb3[ws]])

    load3(0)
    load3(1)
    for i in range(16):
        if i + 2 < 16:
            load3(i + 2)
        pp = i % 2
        ws = i % NW3
        pst = ps[pp]
        for (mat, bm, k0, c0) in ((dfc, b_dfc, 0, 0), (dfs, b_dfs, 16, 1024)):
            for hf in range(2):
                for kc in range(16):
                    S.op("pe", lambda e, pst=pst, ws=ws, kc=kc, hf=hf, mat=mat, k0=k0, c0=c0: e.matmul(
                        pst[:, c0 + hf * 512:c0 + (hf + 1) * 512], lhsT=wb3[ws][:, k0 + kc, :],
                        rhs=mat[:, kc, hf * 512:(hf + 1) * 512], start=(kc == 0), stop=(kc == 15)),
                        reads=[b_wb3[ws], bm[hf]], writes=[b_ps[pp]])
        for kc in range(16):
            S.op("pe", lambda e, pst=pst, ws=ws, kc=kc: e.matmul(
                pst[:, 1024:1025], lhsT=wb3[ws][:, kc, :], rhs=altt[:, kc:kc + 1], start=False, stop=(kc == 15),
                skip_group_check=True), reads=[b_wb3[ws], b_alt], writes=[b_ps[pp]])
        os_ = i % 2
        S.op("act", lambda e, pst=pst, os_=os_: e.activation(out=Bs[os_][:], in_=pst[:, 1024:2048], func=AF.Copy),
             reads=[b_ps[pp]], writes=[b_Bs[os_]])
        S.op("dve", lambda e, pst=pst, os_=os_: e.tensor_tensor(out=ot3[os_][:, 0:1024], in0=pst[:, 0:1024], in1=Bs[os_][:],
                                                                 op=ALU.subtract),
             reads=[b_ps[pp], b_Bs[os_]], writes=[b_ot3[os_]])
        pstride = ot3[os_][:].ap[0][0]
        rev = bass.AP(ot3[os_], 2047, [[pstride, 128], [-1, 1023]])
        S.op("dve", lambda e, pst=pst, os_=os_, rev=rev: e.tensor_tensor(out=rev, in0=pst[:, 1:1024], in1=Bs[os_][:, 1:1024],
                                                                          op=ALU.add),
             reads=[b_ps[pp], b_Bs[os_]], writes=[b_ot3[os_]])
        S.op("dve", lambda e, pst=pst, os_=os_: e.tensor_copy(out=ot3[os_][:, 0:1], in_=pst[:, 0:1]),
             reads=[b_ps[pp]], writes=[b_ot3[os_]])
        S.op("dve", lambda e, os_=os_: e.tensor_copy(out=ot3[os_][:, 1024:1025], in_=Bs[os_][:, 0:1]),
             reads=[b_Bs[os_]], writes=[b_ot3[os_]])
        S.dma(lambda e, os_=os_, i=i: e.dma_start(out=yT_d[i * 128:(i + 1) * 128, :], in_=ot3[os_][:]),
              chan=b_ot3[os_], reads=[b_ot3[os_]])
    S.barrier()
    if upto <= 3:
        return finish(nc, S)

    A.cur = mark
    qs = [A.alloc("qs", [128, 2, T], BF16) for _ in range(2)]
    ks = [A.alloc("ks", [128, 2, T], BF16) for _ in range(2)]
    vs = [A.alloc("vs", [128, 2, T], BF16) for _ in range(2)]
    V1 = [A.alloc("V1", [128, 16, 257], BF16) for _ in range(2)]
    oTs = [A.alloc("oTs", [128, 2, T], BF16) for _ in range(2)]
    b_qs = [S.buf("qs%d" % i) for i in range(2)]
    b_ks = [S.buf("ks%d" % i) for i in range(2)]
    b_vs = [S.buf("vs%d" % i) for i in range(2)]
    b_V1 = [S.buf("V1%d" % i) for i in range(2)]
    b_oTs = [S.buf("oTs%d" % i) for i in range(2)]
    NPT = 3
    pt = [A.alloc("pt", [128, 512], BF16) for _ in range(NPT)]
    b_pt = [S.buf("pt%d" % i) for i in range(NPT)]
    o1 = [A.alloc("o1", [128, 256], F32) for _ in range(2)]
    of = [A.alloc("of", [128, 256], F32) for _ in range(2)]
    sq = [A.alloc("sq", [128, 256], F32) for _ in range(2)]
    onb = [A.alloc("onb", [128, 256], BF16) for _ in range(2)]
    stat = [A.alloc("stat", [128, 8], F32) for _ in range(2)]
    b_o1 = [S.buf("o1%d" % i) for i in range(2)]
    b_of = [S.buf("of%d" % i) for i in range(2)]
    b_sq = [S.buf("sq%d" % i) for i in range(2)]
    b_onb = [S.buf("onb%d" % i) for i in range(2)]
    b_stat = [S.buf("stat%d" % i) for i in range(2)]
    for par in range(2):
        S.op("pool", lambda e, par=par: e.memset(V1[par][:, :, 256:257], 1.0), writes=[b_V1[par]])
    SC = 1.0 / float(np.sqrt(128.0))

    def load_head(h):
        p_ = h % 2
        r0 = h * 256
        for (dst, b_dst, src) in ((qs, b_qs, qT_d), (ks, b_ks, kT_d), (vs, b_vs, vT_d)):
            S.dma(lambda e, dst=dst, src=src: e.dma_start(
                out=dst[p_][:], in_=src[r0:r0 + 256, :].rearrange("(c p) t -> p c t", p=128)),
                chan=b_dst[p_], writes=[b_dst[p_]])

    def build_v(h, k4):
        p_ = h % 2
        for kk in range(4):
            kt = k4 * 4 + kk
            for vc in range(2):
                S.op("pe", lambda e, kk=kk, vc=vc, kt=kt: e.transpose(
                    bank_bf(7)[:, (kk * 2 + vc) * 128:(kk * 2 + vc + 1) * 128],
                    vs[p_][:, vc, kt * 128:(kt + 1) * 128], ident[:]),
                    reads=[b_vs[p_], b_ident], writes=[b_bank[7]])
        S.op("dve", lambda e: e.tensor_copy(
            out=V1[p_][:, k4 * 4:(k4 + 1) * 4, 0:256], in_=bank_bf(7).rearrange("p (k n) -> p k n", k=4)),
            reads=[b_bank[7]], writes=[b_V1[p_]])

    def out_transposes(p_, qsub, q0):
        for vc in range(2):
            S.op("pe", lambda e, vc=vc: e.transpose(
                bank_bf(7)[:, vc * 128:(vc + 1) * 128], onb[qsub][:, vc * 128:(vc + 1) * 128], ident[:]),
                reads=[b_onb[qsub], b_ident], writes=[b_bank[7]])
        S.op("dve", lambda e: e.tensor_copy(
            out=oTs[p_][:, :, q0:q0 + 128], in_=bank_bf(7)[:, 0:256].rearrange("p (c t) -> p c t", c=2)),
            reads=[b_bank[7]], writes=[b_oTs[p_]])

    def store_head(h):
        p_ = h % 2
        r0 = h * 256
        S.dma(lambda e: e.dma_start(
            out=oT_d[r0:r0 + 256, :].rearrange("(c p) t -> p c t", p=128), in_=oTs[p_][:]),
            chan=b_oTs[p_], reads=[b_oTs[p_]])

    steps = [(h, qb, c, k2) for h in range(NH) for qb in range(8) for c in range(2) for k2 in range(8)]
    NS = len(steps)

    def score(n):
        h, qb, c, k2 = steps[n]
        p_ = h % 2
        sb = 4 + n % 3
        pi = n % NPT
        for j in range(2):
            kt = 2 * k2 + j
            S.op("pe", lambda e, j=j, kt=kt: e.matmul(bank(sb)[:, j * 256:(j + 1) * 256], lhsT=ks[p_][:, c, kt * 128:(kt + 1) * 128],
                                                      rhs=qs[p_][:, c, qb * 256:(qb + 1) * 256], start=True, stop=True),
                 reads=[b_ks[p_], b_qs[p_]], writes=[b_bank[sb]])
        S.op("act", lambda e: e.activation(out=pt[pi][:], in_=bank(sb), func=AF.Exp, scale=SC),
             reads=[b_bank[sb]], writes=[b_pt[pi]])

    load_head(0)
    for k4 in range(4):
        build_v(0, k4)
    pending = []
    seqc = [0]

    def defer(due, fn):
        seqc[0] += 1
        pending.append((due, seqc[0], fn))
        pending.sort(key=lambda t: (t[0], t[1]))

    score(0)
    score(1)
    for n in range(NS):
        h, qb, c, k2 = steps[n]
        p_ = h % 2
        first = (qb == 0 and c == 0 and k2 == 0)
        if first and h + 1 < NH:
            load_head(h + 1)
        while pending and pending[0][0] <= n:
            pending.pop(0)[2]()
        if n + 2 < NS:
            score(n + 2)
        pi = n % NPT
        for j in range(2):
            kt = 2 * k2 + j
            for qsub in range(2):
                ab = c * 2 + qsub
                S.op("pe", lambda e, ab=ab, pi=pi, qsub=qsub, kt=kt, p_=p_, j=j: e.matmul(
                    bank(ab)[:, 0:257], lhsT=pt[pi][:, j * 256 + qsub * 128:j * 256 + (qsub + 1) * 128], rhs=V1[p_][:, kt, :],
                    start=(kt == 0), stop=(kt == 15)),
                    reads=[b_pt[pi], b_V1[p_]], writes=[b_bank[ab]])
        if h + 1 < NH and c == 0 and k2 == 4 and 2 <= qb < 6:
            build_v(h + 1, qb - 2)
        kt = 15 if k2 == 7 else -1
        if kt == 15:
            for qsub in range(2):
                ab = c * 2 + qsub
                st_ = stat[qsub]
                bs = b_stat[qsub]
                S.op("dve", lambda e, ab=ab, st_=st_: e.reciprocal(out=st_[:, 0:1], in_=bank(ab)[:, 256:257]),
                     reads=[b_bank[ab]], writes=[bs])
                if c == 0:
                    S.op("dve", lambda e, ab=ab, st_=st_, qsub=qsub: e.tensor_scalar(
                        out=o1[qsub][:], in0=bank(ab)[:, 0:256], scalar1=st_[:, 0:1], scalar2=None, op0=ALU.mult),
                        reads=[b_bank[ab], bs], writes=[b_o1[qsub]])
                else:
                    S.op("dve", lambda e, st_=st_: e.tensor_tensor(out=st_[:, 1:2], in0=st_[:, 0:1], in1=NEGLAM, op=ALU.mult),
                         reads=[bs, b_small], writes=[bs])
                    S.op("dve", lambda e, ab=ab, st_=st_, qsub=qsub: e.scalar_tensor_tensor(
                        out=of[qsub][:], in0=bank(ab)[:, 0:256], scalar=st_[:, 1:2], in1=o1[qsub][:],
                        op0=ALU.mult, op1=ALU.add), reads=[b_bank[ab], bs, b_o1[qsub]], writes=[b_of[qsub]])
                    S.op("pool", lambda e, qsub=qsub: e.tensor_tensor(out=sq[qsub][:], in0=of[qsub][:], in1=of[qsub][:], op=ALU.mult),
                         reads=[b_of[qsub]], writes=[b_sq[qsub]])
                    S.op("dve", lambda e, st_=st_, qsub=qsub: e.tensor_reduce(out=st_[:, 2:3], in_=sq[qsub][:], axis=AX.X, op=ALU.add),
                         reads=[b_sq[qsub]], writes=[bs])
                    S.op("dve", lambda e, st_=st_: e.tensor_scalar(out=st_[:, 3:4], in0=st_[:, 2:3], scalar1=1.0 / 256.0,
                                                                   scalar2=EPS, op0=ALU.mult, op1=ALU.add),
                         reads=[bs], writes=[bs])
                    q0 = qb * 256 + qsub * 128

                    def stage_act(st_=st_, bs=bs):
                        S.op("act", lambda e: e.activation(out=st_[:, 4:5], in_=st_[:, 3:4], func=AF.Ln),
                             reads=[bs], writes=[bs])
                        S.op("act", lambda e: e.activation(out=st_[:, 5:6], in_=st_[:, 4:5], func=AF.Exp, scale=-0.5),
                             reads=[bs], writes=[bs])

                    def stage_onb(st_=st_, bs=bs, qsub=qsub):
                        S.op("dve", lambda e: e.scalar_tensor_tensor(
                            out=onb[qsub][:], in0=of[qsub][:], scalar=st_[:, 5:6], in1=g08[:], op0=ALU.mult, op1=ALU.mult),
                            reads=[b_of[qsub], bs, b_g08], writes=[b_onb[qsub]])

                    defer(n + 4 + qsub, stage_act)
                    defer(n + 6 + qsub, stage_onb)
                    defer(n + 9 + qsub, (lambda p_=p_, qsub=qsub, q0=q0: out_transposes(p_, qsub, q0)))
                    if qb == 7 and qsub == 1:
                        defer(n + 12, (lambda h=h: store_head(h)))
    while pending:
        pending.pop(0)[2]()
    S.barrier()
    if upto <= 4:
        return finish(nc, S)

    A.cur = mark
    yTs = A.alloc("yTs", [128, 16, T], BF16)
    b_yTs = [S.buf("yTs%d" % i) for i in range(4)]
    gt = [A.alloc("gt", [128, T], BF16) for _ in range(2)]
    tm = [A.alloc("tm", [128, T], F32) for _ in range(2)]
    b_gt = [S.buf("gt%d" % i) for i in range(2)]
    b_tm = [S.buf("tm%d" % i) for i in range(2)]

    def pre5a(i):
        sl = i % 2
        S.dma(lambda e: e.dma_start(out=gt[sl][:], in_=gfT_d[i * 128:(i + 1) * 128, :]), chan=b_gt[sl], writes=[b_gt[sl]])

    pre5a(0)

    def epi5a(i, info, pst, b_pst):
        sl = i % 2
        if i + 1 < 32:
            pre5a(i + 1)
        S.op("dve", lambda e: e.tensor_tensor(out=tm[sl][:], in0=pst[:], in1=gt[sl][:], op=ALU.mult),
             reads=[b_pst, b_gt[sl]], writes=[b_tm[sl]])
        S.dma(lambda e: e.dma_start(out=tmpT_d[i * 128:(i + 1) * 128, :], in_=tm[sl][:]), chan=b_tm[sl], reads=[b_tm[sl]])

    gemm_fm(yTs, b_yTs, 16, [(wf_d, ct * 128, False, None) for ct in range(32)], epi5a,
            act_loader=lambda: load_actT(yTs, b_yTs, yT_d, 0, 16))
    S.barrier()

    A.cur = mark
    oTa = A.alloc("oTa", [128, 16, T], BF16)
    b_oTa = [S.buf("oTa%d" % i) for i in range(4)]
    gt = [A.alloc("gt", [128, T], BF16) for _ in range(2)]
    tm = [A.alloc("tm", [128, T], F32) for _ in range(2)]
    m1 = A.alloc("m1", [128, T], F32)
    mo = [A.alloc("mo", [128, T], BF16) for _ in range(2)]
    b_gt = [S.buf("gt%d" % i) for i in range(2)]
    b_tm = [S.buf("tm%d" % i) for i in range(2)]
    b_m1 = S.buf("m1")
    b_mo = [S.buf("mo%d" % i) for i in range(2)]

    def pre5b(i):
        sl = i % 2
        S.dma(lambda e: e.dma_start(out=gt[sl][:], in_=gaT_d[i * 128:(i + 1) * 128, :]), chan=b_gt[sl], writes=[b_gt[sl]])
        S.dma(lambda e: e.dma_start(out=tm[sl][:], in_=tmpT_d[i * 128:(i + 1) * 128, :]), chan=b_tm[sl], writes=[b_tm[sl]])

    pre5b(0)

    def epi5b(i, info, pst, b_pst):
        sl = i % 2
        if i + 1 < 32:
            pre5b(i + 1)
        S.op("dve", lambda e: e.tensor_tensor(out=m1[:], in0=pst[:], in1=gt[sl][:], op=ALU.mult),
             reads=[b_pst, b_gt[sl]], writes=[b_m1])
        S.op("dve", lambda e: e.tensor_tensor(out=mo[sl][:], in0=m1[:], in1=tm[sl][:], op=ALU.add),
             reads=[b_m1, b_tm[sl]], writes=[b_mo[sl]])
        S.dma(lambda e: e.dma_start(out=mT_d[i * 128:(i + 1) * 128, :], in_=mo[sl][:]), chan=b_mo[sl], reads=[b_mo[sl]])

    gemm_fm(oTa, b_oTa, 16, [(wa_d, ct * 128, False, None) for ct in range(32)], epi5b,
            act_loader=lambda: load_actT(oTa, b_oTa, oT_d, 0, 16))
    S.barrier()

    A.cur = mark
    gemm_tm(mT_d, [16, 16], wo_d, x_d, h1_d)
    S.barrier()
    if upto <= 6:
        return finish(nc, S)

    A.cur = mark
    act2 = A.alloc("act2", [128, 32, T], BF16)
    b_act2 = S.buf("act2")
    m7 = A.cur
    norm_to_actT(h1_d, g2_d, act2, b_act2)
    S.barrier()
    A.cur = m7
    sgt = A.alloc("sgt", [128, T], F32)
    b_sgt = S.buf("sgt")
    ot8 = [A.alloc("ot8", [128, T], BF16) for _ in range(2)]
    b_ot8 = [S.buf("ot8%d" % i) for i in range(2)]

    def epi8(i, info, pst, b_pst):
        kind, j = info
        if kind == "g":
            S.op("act", lambda e: e.activation(out=sgt[:], in_=pst[:], func=AF.Silu), reads=[b_pst], writes=[b_sgt])
        else:
            os_ = j % 2
            S.op("dve", lambda e: e.tensor_tensor(out=ot8[os_][:], in0=pst[:], in1=sgt[:], op=ALU.mult),
                 reads=[b_pst, b_sgt], writes=[b_ot8[os_]])
            S.dma(lambda e: e.dma_start(out=hidT_d[j * 128:(j + 1) * 128, :], in_=ot8[os_][:]),
                  chan=b_ot8[os_], reads=[b_ot8[os_]])

    tiles8 = []
    for j in range(HC):
        tiles8.append((wg_d, j * 128, False, ("g", j)))
        tiles8.append((wu_d, j * 128, False, ("u", j)))
    gemm_fm(act2, b_act2, 32, tiles8, epi8, PK=8, NSTG=3)
    S.barrier()

    A.cur = mark
    gemm_tm(hidT_d, [22, 22, 21, 21], wd_d, h1_d, h2_d)
    S.barrier()

    A.cur = mark
    gbc = A.alloc("gbc", [128, D], F32)
    b_gbc = S.buf("gbc")
    S.dma(lambda e: e.dma_start(out=gbc[:], in_=g3_d.partition_broadcast(128)), chan=b_gbc, writes=[b_gbc])
    xt = [A.alloc("xt", [128, D], F32) for _ in range(2)]
    yo = [A.alloc("yo", [128, D], F32) for _ in range(2)]
    junk = A.alloc("junk", [128, D], BF16)
    st = A.alloc("st", [128, NT, 2], F32)
    b_xt = [S.buf("xt%d" % i) for i in range(2)]
    b_yo = [S.buf("yo%d" % i) for i in range(2)]
    b_junk, b_st = S.buf("junk"), S.buf("st")
    S.op("pool", lambda e: e.memset(st[:], 0.0), writes=[b_st])
    for tt in range(NT):
        sl = tt % 2
        S.dma(lambda e, sl=sl, tt=tt: e.dma_start(out=xt[sl][:], in_=h2_d[tt * 128:(tt + 1) * 128, :]),
              chan=b_xt[sl], writes=[b_xt[sl]])
        S.op("act", lambda e, sl=sl, tt=tt: e.activation(out=junk[:], in_=xt[sl][:], func=AF.Square, accum_out=st[:, tt, 0:1]),
             reads=[b_xt[sl], b_st], writes=[b_junk, b_st])
        S.op("dve", lambda e, tt=tt: e.tensor_scalar(out=st[:, tt, 1:2], in0=st[:, tt, 0:1], scalar1=1.0 / D, scalar2=EPS,
                                                     op0=ALU.mult, op1=ALU.add), reads=[b_st], writes=[b_st])
        S.op("act", lambda e, tt=tt: e.activation(out=st[:, tt, 1:2], in_=st[:, tt, 1:2], func=AF.Ln), reads=[b_st], writes=[b_st])
        S.op("act", lambda e, tt=tt: e.activation(out=st[:, tt, 1:2], in_=st[:, tt, 1:2], func=AF.Exp, scale=-0.5),
             reads=[b_st], writes=[b_st])
        S.op("dve", lambda e, sl=sl, tt=tt: e.scalar_tensor_tensor(out=yo[sl][:], in0=xt[sl][:], scalar=st[:, tt, 1:2],
                                                                   in1=gbc[:], op0=ALU.mult, op1=ALU.mult),
             reads=[b_xt[sl], b_st, b_gbc], writes=[b_yo[sl]])
        S.dma(lambda e, sl=sl, tt=tt: e.dma_start(out=out_d[tt * 128:(tt + 1) * 128, :], in_=yo[sl][:]),
              chan=b_yo[sl], reads=[b_yo[sl]])
    S.barrier()
    return finish(nc, S)


def finish(nc, S):
    S.finalize()
    return nc


_CACHE = {}


def kernel(**inputs):
    f32 = np.float32
    x = np.asarray(inputs["x"], dtype=f32)
    pos = np.asarray(inputs["positions"], dtype=np.int32)
    if "consts" not in _CACHE:
        _CACHE["consts"] = _consts()
    nc = build()
    dftw, ccsc, ident, invf, alt = _CACHE["consts"]
    g = lambda k: np.ascontiguousarray(np.asarray(inputs[k], dtype=f32)[0])
    bg = np.ascontiguousarray(np.asarray(inputs["b_gate"], dtype=f32)[0].reshape(2, 32, 128).transpose(2, 0, 1).reshape(128, 64))
    shared = {"norm_mix_g": g("norm_mix_g"), "w_in": g("w_in"), "bgate": bg,
              "lambda_q1": g("lambda_q1"), "lambda_k1": g("lambda_k1"), "lambda_q2": g("lambda_q2"),
              "lambda_k2": g("lambda_k2"), "subln_g": g("subln_g"), "w_fourier_out": g("w_fourier_out"),
              "w_attn_out": g("w_attn_out"), "w_out": g("w_out"), "norm_ffn_g": g("norm_ffn_g"),
              "w_ffn_gate": g("w_ffn_gate"), "w_ffn_up": g("w_ffn_up"), "w_ffn_down": g("w_ffn_down"),
              "norm_final_g": np.ascontiguousarray(np.asarray(inputs["norm_final_g"], dtype=f32)),
              "dftw": dftw, "ccsc": ccsc, "ident": ident, "invf": invf, "alt": alt}
    in_maps = []
    for b in range(8):
        m = dict(shared)
        m["x"] = np.ascontiguousarray(x[b])
        m["positions"] = np.ascontiguousarray(pos[b])
        in_maps.append(m)
    res = run_bass_kernel_spmd(nc, in_maps, core_ids=list(range(8)))
    return np.stack([np.asarray(r["out"], dtype=f32) for r in res.results], axis=0)
```
